# Optimizing a Trainium2 kernel written in Bass

```python
import math
import jax, jax.numpy as jnp
from jax import lax
import numpy as np

D_MODEL = 1024
BATCH = 4
SEQ = 4096
DEPTH = 4

GRID_W = 64
CTX_LEN = 256
D_HY = D_MODEL
N_BANDS = 16
FILTER_EMB = 1 + 2 * N_BANDS
FILTER_ORDER = 64
SHORT_W = 3
GLA_HEADS = 4
GLA_DK = D_MODEL // 8
GLA_DV = D_MODEL // 4
GLA_DK_TOT = GLA_HEADS * GLA_DK
GLA_DV_TOT = GLA_HEADS * GLA_DV
GATE_RANK = 16
GATE_TAU = 16.0
GLA_CHUNK = 64
D_FF = 4 * D_MODEL
EPS = 1e-6
IN_SIZES = (3 * D_HY, GLA_DK_TOT, GLA_DK_TOT, GLA_DV_TOT, GLA_DV_TOT, GATE_RANK, GATE_RANK, D_MODEL, D_MODEL)
IN_WIDTH = sum(IN_SIZES)
IN_SPLITS = tuple(np.cumsum(IN_SIZES)[:-1].tolist())

kernel_name = "hyena_gla_parallel_gated_dit_trunk"


def rmsnorm(x, g):
    xf = x.astype(jnp.float32)
    y = xf * lax.rsqrt(jnp.mean(xf * xf, axis=-1, keepdims=True) + EPS)
    return (y * g.astype(jnp.float32)).astype(x.dtype)


def conv3(x, w, b):
    L = x.shape[-2]
    xp = jnp.pad(x, [(0, 0)] * (x.ndim - 2) + [(1, 1), (0, 0)])
    return xp[..., :L, :] * w[0] + xp[..., 1:L + 1, :] * w[1] + xp[..., 2:, :] * w[2] + b


def hyena_filter(L, w1, b1, w2, b2, w3, freq, decay):
    f32 = jnp.float32
    t = jnp.linspace(0.0, 1.0, L, dtype=f32)[:, None]
    ang = (2.0 * math.pi / L) * jnp.arange(L, dtype=f32)[:, None] * \
        jnp.linspace(1e-4, N_BANDS - 1, N_BANDS, dtype=f32)[None, :]
    z = jnp.concatenate([t, jnp.cos(ang), -jnp.sin(ang)], axis=-1)
    fr = freq.astype(f32)
    a = jnp.sin(fr * (z @ w1.astype(f32) + b1.astype(f32)))
    a = jnp.sin(fr * (a @ w2.astype(f32) + b2.astype(f32)))
    hk = a @ w3.astype(f32)
    window = jnp.exp(-t * jnp.abs(decay.astype(f32))[None, :])
    h_f = hk[:, :D_HY] * window
    h_b = hk[:, D_HY:] * window
    return jnp.concatenate([h_f, jnp.zeros((1, D_HY), f32), h_b[:0:-1]], axis=0)


def long_conv(u, hcirc):
    L = u.shape[1]
    U = jnp.fft.rfft(u.astype(jnp.float32), n=2 * L, axis=1)
    H = jnp.fft.rfft(hcirc, axis=0)
    return jnp.fft.irfft(U * H[None], n=2 * L, axis=1)[:, :L].astype(u.dtype)


def hyena_branch(hy_p, hcirc, conv_w, conv_b, skip, rows):
    B, L, C3 = hy_p.shape
    if rows is None:
        u = conv3(hy_p, conv_w, conv_b)
    else:
        u = conv3(hy_p.reshape(B, rows, GRID_W, C3), conv_w, conv_b).reshape(B, L, C3)
    x0, x1, v = jnp.split(u, 3, axis=-1)
    z = x1 * v
    return x0 * (long_conv(z, hcirc) + skip * z)


def gla_inputs(q, k, v, lr_f, lr_b, gate_w, gate_b):
    B, L, _ = q.shape

    def heads(a):
        return a.reshape(B, L, GLA_HEADS, -1).transpose(0, 2, 1, 3).astype(jnp.float32)

    gf = jax.nn.log_sigmoid((lr_f @ gate_w[0] + gate_b[0]).astype(jnp.float32)) / GATE_TAU
    gb = jax.nn.log_sigmoid((lr_b @ gate_w[1] + gate_b[1]).astype(jnp.float32)) / GATE_TAU
    return heads(q) * (GLA_DK ** -0.5), heads(k), heads(v), heads(gf), heads(gb)


def gla_chunk_scan(q, k, v, g, s0, with_out):
    B, H, L, DK = q.shape
    DV = v.shape[-1]
    n = L // GLA_CHUNK

    def to_chunks(a):
        return jnp.moveaxis(a.reshape(B, H, n, GLA_CHUNK, a.shape[-1]), 2, 0)

    lower = jnp.tril(jnp.ones((GLA_CHUNK, GLA_CHUNK), dtype=bool))

    def step(S, blk):
        qb, kb, vb, gb = blk
        b = jnp.cumsum(gb, axis=2)
        b_end = b[:, :, -1:, :]
        S_next = jnp.exp(b_end)[:, :, 0, :, None] * S + \
            jnp.einsum('bhcd,bhce->bhde', kb * jnp.exp(b_end - b), vb)
        if not with_out:
            return S_next, None
        q_t = qb * jnp.exp(b)
        k_t = kb * jnp.exp(-b)
        att = jnp.where(lower, jnp.einsum('bhtd,bhsd->bhts', q_t, k_t), 0.0)
        o = jnp.einsum('bhts,bhse->bhte', att, vb) + jnp.einsum('bhtd,bhde->bhte', q_t, S)
        return S_next, o

    S_fin, o = lax.scan(step, s0, (to_chunks(q), to_chunks(k), to_chunks(v), to_chunks(g)))
    if with_out:
        o = jnp.moveaxis(o, 0, 2).reshape(B, H, L, DV)
    return S_fin, o


def gla_bidir(q, k, v, gf, gb, s0_f, s0_b, with_out):
    flip = lambda a: jnp.flip(a, axis=2)
    s_f, o_f = gla_chunk_scan(q, k, v, gf, s0_f, with_out)
    s_b, o_b = gla_chunk_scan(flip(q), flip(k), flip(v), flip(gb), s0_b, with_out)
    o = o_f + flip(o_b) if with_out else None
    return o, s_f, s_b


def gla_output(o, r, g):
    B, H, L, DV = o.shape
    on = o * lax.rsqrt(jnp.mean(o * o, axis=-1, keepdims=True) + EPS) * g.astype(jnp.float32)
    on = on.transpose(0, 2, 1, 3).reshape(B, L, H * DV).astype(r.dtype)
    return on * jax.nn.silu(r)


def mixer_output(hy_p, r, gt_hy, gt_gla, o_gla, hcirc, rows, conv_w, conv_b, skip, norm_g, w_bhy, w_bgla, w_o):
    y_hy = hyena_branch(hy_p, hcirc, conv_w, conv_b, skip, rows)
    y_gla = gla_output(o_gla, r, norm_g)
    merged = jax.nn.sigmoid(gt_hy) * (y_hy @ w_bhy) + jax.nn.sigmoid(gt_gla) * (y_gla @ w_bgla)
    return merged @ w_o


def sq_relu_mlp(h, w1, w2):
    return jnp.square(jax.nn.relu(h @ w1)) @ w2


def setup_inputs(seed: int = 0) -> dict:
    key = jax.random.key(seed)
    ks = jax.random.split(key, 32)
    f32 = jnp.float32

    def nrm(k, shape, s):
        return jax.random.normal(k, shape, f32) * s

    decay_base = jnp.linspace(-math.log(1e-2) / 1.5, -math.log(1e-2) / 0.3, D_HY, dtype=f32)
    return {
        "x": nrm(ks[0], (BATCH, SEQ, D_MODEL), 1.0),
        "c": nrm(ks[1], (BATCH, D_MODEL), 1.0),
        "ctx": nrm(ks[2], (BATCH, CTX_LEN, D_MODEL), 1.0),
        "c_ctx": nrm(ks[3], (D_MODEL,), 1.0),
        "ada_w": nrm(ks[4], (DEPTH, D_MODEL, 6 * D_MODEL), 0.5 * D_MODEL ** -0.5),
        "ada_b": nrm(ks[5], (DEPTH, 6 * D_MODEL), 0.02),
        "norm1_g": 1.0 + nrm(ks[6], (DEPTH, D_MODEL), 0.02),
        "norm2_g": 1.0 + nrm(ks[7], (DEPTH, D_MODEL), 0.02),
        "w_in": nrm(ks[8], (DEPTH, D_MODEL, IN_WIDTH), D_MODEL ** -0.5),
        "hy_conv_w": nrm(ks[9], (DEPTH, SHORT_W, 3 * D_HY), SHORT_W ** -0.5),
        "hy_conv_b": nrm(ks[10], (DEPTH, 3 * D_HY), 0.02),
        "hy_filt_w1": nrm(ks[11], (DEPTH, FILTER_EMB, FILTER_ORDER), FILTER_EMB ** -0.5),
        "hy_filt_b1": nrm(ks[12], (DEPTH, FILTER_ORDER), 0.1),
        "hy_filt_w2": nrm(ks[13], (DEPTH, FILTER_ORDER, FILTER_ORDER), FILTER_ORDER ** -0.5),
        "hy_filt_b2": nrm(ks[14], (DEPTH, FILTER_ORDER), 0.1),
        "hy_filt_w3": nrm(ks[15], (DEPTH, FILTER_ORDER, 2 * D_HY), 0.05 * FILTER_ORDER ** -0.5),
        "hy_filt_freq": 1.0 + nrm(ks[16], (DEPTH, FILTER_ORDER), 0.05),
        "hy_decay": decay_base[None, :] * (1.0 + nrm(ks[17], (DEPTH, D_HY), 0.05)),
        "hy_skip": nrm(ks[18], (DEPTH, D_HY), 1.0),
        "gla_gate_w": nrm(ks[19], (DEPTH, 2, GATE_RANK, GLA_DK_TOT), GATE_RANK ** -0.5),
        "gla_gate_b": nrm(ks[20], (DEPTH, 2, GLA_DK_TOT), 0.1),
        "gla_norm_g": 1.0 + nrm(ks[21], (DEPTH, GLA_DV), 0.02),
        "w_branch_hy": nrm(ks[22], (DEPTH, D_HY, D_MODEL), D_HY ** -0.5),
        "w_branch_gla": nrm(ks[23], (DEPTH, GLA_DV_TOT, D_MODEL), GLA_DV_TOT ** -0.5),
        "w_out": nrm(ks[24], (DEPTH, D_MODEL, D_MODEL), D_MODEL ** -0.5),
        "mlp_w1": nrm(ks[25], (DEPTH, D_MODEL, D_FF), D_MODEL ** -0.5),
        "mlp_w2": nrm(ks[26], (DEPTH, D_FF, D_MODEL), D_FF ** -0.5),
        "final_g": 1.0 + nrm(ks[27], (D_MODEL,), 0.02),
    }


def reference(x, c, ctx, c_ctx, ada_w, ada_b, norm1_g, norm2_g, w_in, hy_conv_w, hy_conv_b,
              hy_filt_w1, hy_filt_b1, hy_filt_w2, hy_filt_b2, hy_filt_w3, hy_filt_freq, hy_decay, hy_skip,
              gla_gate_w, gla_gate_b, gla_norm_g, w_branch_hy, w_branch_gla, w_out, mlp_w1, mlp_w2, final_g):
    B, L, _ = x.shape
    L_ctx = ctx.shape[1]
    rows = L // GRID_W
    silu_c = jax.nn.silu(c)
    silu_cc = jax.nn.silu(c_ctx)
    for l in range(DEPTH):
        last = l == DEPTH - 1
        mod = (silu_c @ ada_w[l] + ada_b[l])[:, None, :]
        mod_c = (silu_cc @ ada_w[l] + ada_b[l])[None, None, :]
        sh1, sc1, g1, sh2, sc2, g2 = jnp.split(mod, 6, axis=-1)
        csh1, csc1, cg1, csh2, csc2, cg2 = jnp.split(mod_c, 6, axis=-1)

        h = rmsnorm(x, norm1_g[l]) * (1 + sc1) + sh1
        hc = rmsnorm(ctx, norm1_g[l]) * (1 + csc1) + csh1
        hy_p, q, k, v, r, lr_f, lr_b, gt_hy, gt_gla = jnp.split(h @ w_in[l], IN_SPLITS, axis=-1)
        hy_pc, qc, kc, vc, rc, lr_fc, lr_bc, gt_hyc, gt_glac = jnp.split(hc @ w_in[l], IN_SPLITS, axis=-1)

        zeros_state = jnp.zeros((B, GLA_HEADS, GLA_DK, GLA_DV), jnp.float32)
        q_c, k_c, v_c, gf_c, gb_c = gla_inputs(qc, kc, vc, lr_fc, lr_bc, gla_gate_w[l], gla_gate_b[l])
        o_c, s_f_ctx, s_b_ctx = gla_bidir(q_c, k_c, v_c, gf_c, gb_c, zeros_state, zeros_state, not last)
        q_l, k_l, v_l, gf_l, gb_l = gla_inputs(q, k, v, lr_f, lr_b, gla_gate_w[l], gla_gate_b[l])
        o_l, _, _ = gla_bidir(q_l, k_l, v_l, gf_l, gb_l, s_f_ctx, s_b_ctx, True)

        filt = (hy_filt_w1[l], hy_filt_b1[l], hy_filt_w2[l], hy_filt_b2[l], hy_filt_w3[l], hy_filt_freq[l], hy_decay[l])
        shared = (hy_conv_w[l], hy_conv_b[l], hy_skip[l], gla_norm_g[l], w_branch_hy[l], w_branch_gla[l], w_out[l])
        out = mixer_output(hy_p, r, gt_hy, gt_gla, o_l, hyena_filter(L, *filt), rows, *shared)
        x = x + g1 * out
        if not last:
            out_c = mixer_output(hy_pc, rc, gt_hyc, gt_glac, o_c, hyena_filter(L_ctx, *filt), None, *shared)
            ctx = ctx + cg1 * out_c

        h2 = rmsnorm(x, norm2_g[l]) * (1 + sc2) + sh2
        x = x + g2 * sq_relu_mlp(h2, mlp_w1[l], mlp_w2[l])
        if not last:
            h2c = rmsnorm(ctx, norm2_g[l]) * (1 + csc2) + csh2
            ctx = ctx + cg2 * sq_relu_mlp(h2c, mlp_w1[l], mlp_w2[l])
    return rmsnorm(x, final_g)
```

```python
import math
from contextlib import ExitStack
import numpy as np
import ml_dtypes
import concourse.bass as bass
import concourse.mybir as mybir
from concourse.bass_utils import run_bass_kernel_spmd

F32 = mybir.dt.float32
BF16 = mybir.dt.bfloat16
AF = mybir.ActivationFunctionType
ALU = mybir.AluOpType

D = 1024
L = 4096
LC = 256
NT = L + LC
DEPTH = 4
DFF = 4096
INW = 8224
EPS = 1e-6
PI = math.pi


SAME_ENGINE_ORDERED = ("pe",)


class Buf:
    __slots__ = ("w", "r")

    def __init__(self):
        self.w = None
        self.r = {}


class KB:
    def __init__(self, nc, stack, ndma=8):
        self.nc = nc
        self.engs = {"pe": nc.tensor, "act": nc.scalar, "dve": nc.vector, "pool": nc.gpsimd, "sp": nc.sync}
        self.ops = {e: [] for e in self.engs}
        self.cnt = {e: 0 for e in self.engs}
        self.seen = {e: {} for e in self.engs}
        self.sem = {}
        for e in self.engs:
            self.sem[e] = stack.enter_context(nc.semaphore("s_" + e))
        self.ndma = ndma
        self.dma_i = 0
        self.dma_val = [0] * ndma
        for i in range(ndma):
            self.sem["d%d" % i] = stack.enter_context(nc.semaphore("sd%d" % i))

    def emit(self, eng, fn, reads=(), writes=(), dma=False, inc=True):
        need = {}

        def add(ev):
            if ev is None:
                return
            k, v = ev
            if need.get(k, 0) < v:
                need[k] = v

        for b in reads:
            add(b.w)
        for b in writes:
            add(b.w)
            for k, v in b.r.items():
                add((k, v))
        if dma:
            i = self.dma_i
            self.dma_i = (i + 1) % self.ndma
            key = "d%d" % i
            add((key, self.dma_val[i]))
            self.dma_val[i] += 16
            ev = (key, self.dma_val[i])
            incv = 16
        else:
            if inc:
                self.cnt[eng] += 1
                ev = (eng, self.cnt[eng])
            else:
                ev = (eng, self.cnt[eng] + 1)
            incv = 1
        waits = []
        seen = self.seen[eng]
        for k, v in need.items():
            if v <= 0:
                continue
            if k == eng and eng in SAME_ENGINE_ORDERED:
                continue
            if seen.get(k, 0) < v:
                seen[k] = v
                waits.append((k, v))
        self.ops[eng].append((waits, fn, ev[0] if (dma or inc) else None, incv))
        for b in reads:
            if b.r.get(ev[0], 0) < ev[1]:
                b.r[ev[0]] = ev[1]
        for b in writes:
            b.w = ev
            b.r = {}

    def barrier(self):
        cur = dict(self.cnt)
        for i in range(self.ndma):
            cur["d%d" % i] = self.dma_val[i]
        for e in self.engs:
            waits = []
            seen = self.seen[e]
            for k, v in cur.items():
                if k == e or v <= 0:
                    continue
                if seen.get(k, 0) < v:
                    seen[k] = v
                    waits.append((k, v))
            if waits:
                self.ops[e].append((waits, None, None, 0))

    def replay(self):
        nc = self.nc
        with nc.Block() as block:
            decs = {"sp": block.sync, "act": block.scalar, "dve": block.vector, "pool": block.gpsimd, "pe": block.tensor}
            for name, dec in decs.items():
                def mk(name):
                    def body(e):
                        for waits, fn, semk, incv in self.ops[name]:
                            for k, v in waits:
                                e.wait_ge(self.sem[k], v)
                            if fn is None:
                                continue
                            ins = getattr(e, fn[0])(**fn[1])
                            if semk is not None:
                                ins.then_inc(self.sem[semk], incv)
                    return body
                dec(mk(name))


WKEYS = ("hy_filt_w1", "hy_filt_b1", "hy_filt_w2", "hy_filt_b2", "hy_filt_w3", "hy_filt_freq", "hy_decay", "ada_w", "ada_b", "norm1_g", "norm2_g", "mlp_w1", "mlp_w2", "final_g", "w_in", "hy_conv_w", "hy_conv_b", "hy_skip",
         "gla_gate_w", "gla_gate_b", "gla_norm_g", "w_branch_hy", "w_branch_gla", "w_out")


def bf(a):
    return np.asarray(a, np.float32).astype(ml_dtypes.bfloat16)


def host_consts():
    c = {}
    c["ident"] = np.eye(128, dtype=np.float32)
    c["ones"] = np.ones((128, 128), np.float32)
    s_ = np.arange(128)[:, None]; t_ = np.arange(128)[None, :]
    g = -1.0 / 16.0
    n2 = np.arange(64)[:, None]; k2 = np.arange(65)[None, :]
    th = 2 * np.pi * n2 * k2 / 128.0
    C2, S2 = np.cos(th), np.sin(th)
    c["G4"] = bf(np.stack([np.concatenate([C2, -S2, -S2, C2], 1), np.concatenate([S2, C2, C2, S2], 1),
                           np.concatenate([C2, C2, S2, -S2], 1), np.concatenate([S2, S2, -C2, C2], 1)], 1))
    bands = np.linspace(1e-4, 15, 16, dtype=np.float32)
    zf, tq = [], []
    for ci_, (R_, P1, PK, boff) in enumerate(((64, 128, 128, 64), (4, 8, 36, 32))):
        Lc = 64 * R_
        n1 = np.arange(R_)[:, None]; k1 = np.arange(P1)[None, :]
        a = 2 * np.pi * n1 * k1 / P1
        c["F1_%d" % ci_] = bf(np.concatenate([np.cos(a), -np.sin(a)], 1))
        l1 = np.zeros(PK); valid = np.zeros(PK)
        l1[0:R_] = np.arange(R_); valid[0:R_] = 1
        l1[boff:boff + R_] = np.arange(R_) - R_; valid[boff:boff + R_] = 1
        a = 2 * np.pi * l1[:, None] * k1 / P1
        c["F1h_%d" % ci_] = bf(valid[:, None] * np.concatenate([np.cos(a), -np.sin(a)], 1))
        kk = np.arange(P1)[:, None]
        na = np.arange(R_)[None, :]; nb = (np.arange(R_)[None, :] - 1) % P1
        Ca, Cb = np.cos(2 * np.pi * kk * na / P1), np.cos(2 * np.pi * kk * nb / P1)
        Sa, Sb = np.sin(2 * np.pi * kk * na / P1), np.sin(2 * np.pi * kk * nb / P1)
        c["Ea_%d" % ci_] = bf(np.concatenate([Ca, Cb, Sa, Sb], 1))
        c["Eb_%d" % ci_] = bf(np.concatenate([-Sa, -Sb, Ca, Cb], 1))
        if ci_ == 0:
            ne = (np.arange(R_ + 1)[None, :] - 1) % P1
            Ce, Se = np.cos(2 * np.pi * kk * ne / P1), np.sin(2 * np.pi * kk * ne / P1)
            c["Ec_a"] = bf(np.concatenate([Ce, Se], 1))
            c["Ec_b"] = bf(np.concatenate([-Se, Ce], 1))
        w = np.full((65, 1), 2.0); w[0] = 1.0; w[64] = 1.0
        w = w / (P1 * 128.0)
        kq = np.arange(65)[:, None]; nn = np.arange(64)[None, :]
        tlo = 2 * np.pi * kq * nn / 128.0; thi = 2 * np.pi * kq * (nn + 64) / 128.0
        c["R4_%d" % ci_] = bf(np.stack([w * np.cos(tlo), w * np.cos(thi), -w * np.sin(tlo), -w * np.sin(thi)], 1))
        q = np.arange(Lc)
        if ci_ == 0:
            pos_b = 64 * (R_ - q // 64) - (q % 64)
            pos_b = np.where(pos_b >= Lc, 0, pos_b)
        else:
            pos_b = np.where(q == 0, 0, Lc - q)
        pos = np.concatenate([q, pos_b])
        tl = np.linspace(0.0, 1.0, Lc, dtype=np.float32)
        ang = (np.float32(2.0 * math.pi / Lc) * np.arange(Lc, dtype=np.float32))[:, None] * bands[None, :]
        feat = np.concatenate([tl[:, None], np.cos(ang), -np.sin(ang)], 1).astype(np.float32)
        zf.append(feat[pos].T)
        tq.append(tl[pos][None, :])
    n_ = np.arange(512)[:, None].astype(np.float64); k_ = np.arange(256)[None, :].astype(np.float64)
    fre = np.cos(2 * np.pi * n_ * k_ / 512.0)
    fim = -np.sin(2 * np.pi * n_ * k_ / 512.0)
    fim[:, 0] = (-1.0) ** np.arange(512)
    Fc = np.concatenate([fre, fim], 1)
    c["Fc"] = bf(Fc.reshape(4, 128, 512).transpose(1, 0, 2))
    tt_ = np.arange(256)[None, :].astype(np.float64); kk_ = np.arange(256)[:, None].astype(np.float64)
    ire = 2.0 * np.cos(2 * np.pi * kk_ * tt_ / 512.0) / 512.0
    ire[0, :] = 1.0 / 512.0
    iim = -2.0 * np.sin(2 * np.pi * kk_ * tt_ / 512.0) / 512.0
    iim[0, :] = ((-1.0) ** np.arange(256)) / 512.0
    Fi = np.concatenate([ire, iim], 0)
    c["Finv"] = bf(Fi.reshape(4, 128, 256).transpose(1, 0, 2))
    c["zfeat"] = np.ascontiguousarray(np.concatenate(zf, 1), np.float32)
    tqf = np.concatenate(tq, 1).astype(np.float32)[0]
    thi = tqf.astype(ml_dtypes.bfloat16)
    tlo = (tqf - thi.astype(np.float32)).astype(ml_dtypes.bfloat16)
    c["tq"] = np.ascontiguousarray(np.stack([thi, thi, tlo, tlo]))
    c["umats"] = np.stack([g * (s_ <= t_), g * (s_ >= t_), g * (s_ > t_), g * (s_ < t_),
                           1.0 * (s_ <= t_), 1.0 * (s_ >= t_)]).astype(np.float32)
    return c


def build(nlayers=DEPTH, do_mix=True, do_mlp=True, do_hy=True, dbg=False):
    nc = bass.Bass("TRN2", target_bir_lowering=False)

    def din(name, shape, dt=F32):
        return nc.dram_tensor(name, list(shape), dt, kind="ExternalInput").ap()

    def dscr(name, shape, dt=F32):
        return nc.dram_tensor(name, list(shape), dt, kind="Internal").ap()

    x_in = din("x", [L, D])
    ctx_in = din("ctx", [LC, D])
    cvec = din("cvec", [2, D])
    ada_w = din("ada_w", [DEPTH, D, 6 * D])
    ada_b = din("ada_b", [DEPTH, 6 * D])
    norm1_g = din("norm1_g", [DEPTH, D])
    norm2_g = din("norm2_g", [DEPTH, D])
    mlp_w1 = din("mlp_w1", [DEPTH, D, DFF])
    mlp_w2 = din("mlp_w2", [DEPTH, DFF, D])
    final_g = din("final_g", [D])
    w_in = din("w_in", [DEPTH, D, INW])
    hy_conv_w = din("hy_conv_w", [DEPTH, 3, 3072])
    hy_conv_b = din("hy_conv_b", [DEPTH, 3072])
    hy_skip = din("hy_skip", [DEPTH, D])
    gla_gate_w = din("gla_gate_w", [DEPTH, 2, 16, 512])
    gla_gate_b = din("gla_gate_b", [DEPTH, 2, 512])
    gla_norm_g = din("gla_norm_g", [DEPTH, 256])
    w_branch_hy = din("w_branch_hy", [DEPTH, D, D])
    w_branch_gla = din("w_branch_gla", [DEPTH, D, D])
    w_out = din("w_out", [DEPTH, D, D])
    umats_d = din("umats", [6, 128, 128])
    hy_filt_w1 = din("hy_filt_w1", [DEPTH, 33, 64])
    hy_filt_b1 = din("hy_filt_b1", [DEPTH, 64])
    hy_filt_w2 = din("hy_filt_w2", [DEPTH, 64, 64])
    hy_filt_b2 = din("hy_filt_b2", [DEPTH, 64])
    hy_filt_w3 = din("hy_filt_w3", [DEPTH, 64, 2048])
    hy_filt_freq = din("hy_filt_freq", [DEPTH, 64])
    hy_decay = din("hy_decay", [DEPTH, D])
    zfeat_d = din("zfeat", [33, 8704])
    tq_d = din("tq", [4, 8704], BF16)
    dF1 = [din("F1_0", [64, 256], BF16), din("F1_1", [4, 16], BF16)]
    dF1h = [din("F1h_0", [128, 256], BF16), din("F1h_1", [36, 16], BF16)]
    dEa = [din("Ea_0", [128, 256], BF16), din("Ea_1", [8, 16], BF16)]
    dEb = [din("Eb_0", [128, 256], BF16), din("Eb_1", [8, 16], BF16)]
    dR4 = [din("R4_0", [65, 4, 64], BF16), din("R4_1", [65, 4, 64], BF16)]
    dG4 = din("G4", [64, 4, 260], BF16)
    dEc = [din("Ec_a", [128, 130], BF16), din("Ec_b", [128, 130], BF16)]
    dFc = din("Fc", [128, 4, 512], BF16)
    dFinv = din("Finv", [128, 4, 256], BF16)
    x0T_d = dscr("x0T_d", [8, 128, NT], BF16)
    zT_d = dscr("zT_d", [8, 128, NT], BF16)
    ycT_d = dscr("ycT_d", [8, 128, NT], BF16)
    qT_d = dscr("qT_d", [4, 128, NT], BF16)
    kT_d = dscr("kT_d", [4, 128, NT], BF16)
    ktok_d = dscr("ktok_d", [NT, 512], BF16)
    vtok_d = dscr("vtok_d", [NT, 1024], BF16)
    srT_d = dscr("srT_d", [8, 128, NT], BF16)
    sghT_d = dscr("sghT_d", [8, 128, NT], BF16)
    sggT_d = dscr("sggT_d", [8, 128, NT], BF16)
    ygT_d = dscr("ygT_d", [8, 128, NT], BF16)
    oT_d = dscr("oT_d", [8, 128, NT])
    B_scr = Buf(); B_scr2 = Buf(); B_oT = Buf(); B_yc = Buf()
    if dbg:
        dbg_t = {n: nc.dram_tensor("dbg_" + n, [8, 128, NT], BF16, kind="ExternalOutput").ap() for n in ("z", "yc", "x0", "yg", "sgh", "sgg", "sr")}
        dbg_x = nc.dram_tensor("dbg_x", [8, 128, NT], F32, kind="ExternalOutput").ap()
    ident_d = din("ident", [128, 128])
    ones_d = din("ones", [128, 128])
    y_out = nc.dram_tensor("y", [L, D], F32, kind="ExternalOutput").ap()

    xT_d = dscr("xT_d", [8, 128, NT])
    B_xT = Buf()

    with ExitStack() as top:
        kb = KB(nc, top)
        E = kb.emit

        def I(eng, _op, reads=(), writes=(), dma=False, inc=True, **kw):
            kb.emit(eng, (_op, kw), reads=reads, writes=writes, dma=dma, inc=inc)

        uid = [0]

        def sb(stack, name, shape, dt=F32):
            uid[0] += 1
            return stack.enter_context(nc.sbuf_tensor("sb%d_%s" % (uid[0], name), list(shape), dt))

        ps_t = [top.enter_context(nc.psum_tensor("ps%d" % i, [128, 512], F32)) for i in range(8)]
        ps_b = [Buf() for _ in range(8)]
        ps_i = [0]

        def psum():
            i = ps_i[0]
            ps_i[0] = (i + 1) % 7
            return ps_t[i], ps_b[i]

        ident = sb(top, "ident", [128, 128]); B_ident = Buf()
        ones = sb(top, "ones", [128, 128]); B_ones = Buf()
        E("sp", ("dma_start", dict(out=ident[:, :], in_=ident_d[:, :])), writes=[B_ident], dma=True)
        E("sp", ("dma_start", dict(out=ones[:, :], in_=ones_d[:, :])), writes=[B_ones], dma=True)
        B_U = Buf()
        B_lr = [Buf(), Buf()]
        vstg = [sb(top, "vstg%d" % i, [128, 128]) for i in range(2)]; B_vst = [Buf(), Buf()]
        vsi = [0]

        def load_T(dst, src2d, J, B_dst, view=None):
            i = vsi[0]; vsi[0] = 1 - i
            I("sp", "dma_start", out=vstg[i][0:J, :], in_=src2d, writes=[B_vst[i]], dma=True)
            pt, pb = psum()
            I("pe", "transpose", out=pt[:, 0:J], in_=vstg[i][0:J, :], identity=ident[0:J, 0:J], reads=[B_vst[i], B_ident], writes=[pb])
            src = pt[:, 0:J] if view is None else view(pt[:, 0:J])
            I("dve", "tensor_copy", out=dst, in_=src, reads=[pb], writes=[B_dst])

        scv = sb(top, "scv", [128, 8, 2]); B_scv = Buf()
        craw = sb(top, "craw", [128, 2, 8]); B_craw = Buf()
        load_T(craw[:, :, :], cvec.rearrange("j (k p) -> (j k) p", p=128), 16, B_craw, view=lambda a: a.rearrange("p (j k) -> p j k", j=2))
        for j in range(2):
            E("act", ("activation", dict(out=scv[:, :, j], in_=craw[:, j, :], func=AF.Silu)),
              reads=[B_craw], writes=[B_scv])
        modTs = [sb(top, "modT%d" % i, [128, 48, 2]) for i in range(2)]; Bm = [Buf(), Buf()]
        sc1s = [sb(top, "sc1_%d" % i, [128, 8, 2]) for i in range(2)]; sc2s = [sb(top, "sc2_%d" % i, [128, 8, 2]) for i in range(2)]
        Bs = [Buf(), Buf()]
        gvl = [sb(top, "gvl%d" % i, [128, 2, 8]) for i in range(2)]; Bg = [Buf(), Buf()]
        gfin = sb(top, "gfin", [128, 8]); B_gfin = Buf()
        load_T(gfin[:, :], final_g.rearrange("(k p) -> k p", p=128), 8, B_gfin)
        modT, sc1, sc2, B_mod, B_sc, B_gvec = modTs[0], sc1s[0], sc2s[0], Bm[0], Bs[0], Bg[0]
        prefetched = set()

        def P0_steps(l_, stack, tgt, W=512):
            wa = [sb(stack, "wa%d" % i, [128, 8, W]) for i in range(2)]; B_wa = [Buf(), Buf()]
            abv = sb(stack, "abv", [128, 48]); B_abv = Buf()
            pt, pb = ps_t[7], ps_b[7]

            def pre():
                load_T(abv[:, :], ada_b[l_, :].rearrange("(j p) -> j p", p=128), 48, B_abv)
                load_T(gvl[tgt][:, 0, :], norm1_g[l_, :].rearrange("(k p) -> k p", p=128), 8, Bg[tgt])
                load_T(gvl[tgt][:, 1, :], norm2_g[l_, :].rearrange("(k p) -> k p", p=128), 8, Bg[tgt])

            def dma(g):
                I("sp", "dma_start", out=wa[g % 2][:, :, :], in_=ada_w[l_, :, g * W:(g + 1) * W].rearrange("(k p) n -> p k n", p=128),
                  writes=[B_wa[g % 2]], dma=True)

            def mm(g):
                for jj in range(W // 128):
                    j = g * (W // 128) + jj
                    for k in range(8):
                        I("pe", "matmul", out=pt[:, 2 * j:2 * j + 2], lhsT=wa[g % 2][:, k, jj * 128:(jj + 1) * 128], rhs=scv[:, k, :],
                          start=(k == 0), stop=(k == 7), reads=[B_wa[g % 2], B_scv], writes=[pb], inc=(k == 7))

            def fin():
                I("dve", "tensor_tensor", out=modTs[tgt][:, :, :], in0=pt[:, 0:96].rearrange("p (j c) -> p j c", c=2),
                  in1=abv[:, :].unsqueeze(2).broadcast_to([128, 48, 2]), op=ALU.add, reads=[pb, B_abv], writes=[Bm[tgt]])
                for (dst, gi, mb) in ((sc1s[tgt], 0, 8), (sc2s[tgt], 1, 32)):
                    I("dve", "scalar_tensor_tensor", out=dst[:, :, :], in0=modTs[tgt][:, mb:mb + 8, :], scalar=1.0,
                      in1=gvl[tgt][:, gi, :].unsqueeze(2).broadcast_to([128, 8, 2]), op0=ALU.add, op1=ALU.mult,
                      reads=[Bm[tgt], Bg[tgt]], writes=[Bs[tgt]])
            return pre, dma, mm, fin

        def norm_tile(ph, tag, xt, B_x, T, scale_fn, shift_fn, out, B_out, scr):
            sq, B_sq, rs, B_rs = scr["sq"], scr["B_sq"], scr["rs"], scr["B_rs"]
            E("act", ("activation", dict(out=sq[:, :, :T], in_=xt[:, :, :T], func=AF.Square)),
              reads=[B_x], writes=[B_sq])
            pt, pb = psum()
            for k in range(8):
                E("pe", ("matmul", dict(out=pt[:, :T], lhsT=ones[:, :], rhs=sq[:, k, :T], start=(k == 0), stop=(k == 7))),
                  reads=[B_sq, B_ones], writes=[pb], inc=(k == 7))
            E("dve", ("tensor_scalar", dict(out=rs[:, :T], in0=pt[:, :T], scalar1=1.0 / D, scalar2=EPS,
                                               op0=ALU.mult, op1=ALU.add)), reads=[pb], writes=[B_rs])
            E("act", ("activation", dict(out=rs[:, :T], in_=rs[:, :T], func=AF.Sqrt)), reads=[B_rs], writes=[B_rs])
            E("dve", ("reciprocal", dict(out=rs[:, :T], in_=rs[:, :T])), reads=[B_rs], writes=[B_rs])
            E("dve", ("tensor_tensor", dict(out=sq[:, :, :T], in0=xt[:, :, :T],
                                               in1=rs[:, :T].unsqueeze(1).broadcast_to([128, 8, T]), op=ALU.mult)),
              reads=[B_x, B_rs], writes=[B_sq])
            for k in range(8):
                eng = "act" if k % 2 == 0 else "dve"
                sc = scale_fn(k)
                sh = shift_fn(k) if shift_fn is not None else None
                if eng == "act":
                    if sh is None:
                        E("act", ("activation", dict(out=out[:, k, :T], in_=sq[:, k, :T], func=AF.Identity,
                                                                      scale=sc)), reads=[B_sq, B_sc, B_gvec], writes=[B_out])
                    else:
                        E("act", ("activation", dict(out=out[:, k, :T], in_=sq[:, k, :T], func=AF.Identity,
                                                                             scale=sc, bias=sh)), reads=[B_sq, B_sc, B_gvec, B_mod], writes=[B_out])
                else:
                    if sh is None:
                        E("dve", ("tensor_scalar", dict(out=out[:, k, :T], in0=sq[:, k, :T], scalar1=sc, scalar2=None,
                                                                          op0=ALU.mult)), reads=[B_sq, B_sc, B_gvec], writes=[B_out])
                    else:
                        E("dve", ("tensor_scalar", dict(out=out[:, k, :T], in0=sq[:, k, :T], scalar1=sc, scalar2=sh,
                                                                                 op0=ALU.mult, op1=ALU.add)),
                          reads=[B_sq, B_sc, B_gvec, B_mod], writes=[B_out])

        with ExitStack() as ph:
            xin = [sb(ph, "xin%d" % i, [128, D]) for i in range(2)]; B_xin = [Buf(), Buf()]
            xo = [sb(ph, "xo%d" % i, [128, 8, 128]) for i in range(2)]; B_xo = [Buf(), Buf()]
            for ti in range(NT // 128):
                s = ti % 2
                src = x_in[ti * 128:(ti + 1) * 128, :] if ti < L // 128 else ctx_in[(ti - L // 128) * 128:(ti - L // 128 + 1) * 128, :]
                E("sp", ("dma_start", dict(out=xin[s][:, :], in_=src)), writes=[B_xin[s]], dma=True)
                for half in range(2):
                    pt, pb = psum()
                    for j in range(4):
                        k = half * 4 + j
                        E("pe", ("transpose", dict(out=pt[:, j * 128:(j + 1) * 128], in_=xin[s][:, k * 128:(k + 1) * 128],
                                                                            identity=ident[:, :])),
                          reads=[B_xin[s], B_ident], writes=[pb], inc=(j == 3))
                    eng = "act" if half == 0 else "dve"
                    if eng == "act":
                        E("act", ("copy", dict(out=xo[s][:, half * 4:(half + 1) * 4, :],
                                                                         in_=pt[:, :].rearrange("p (j t) -> p j t", j=4))),
                          reads=[pb], writes=[B_xo[s]])
                    else:
                        E("dve", ("tensor_copy", dict(out=xo[s][:, half * 4:(half + 1) * 4, :],
                                                                                in_=pt[:, :].rearrange("p (j t) -> p j t", j=4))),
                          reads=[pb], writes=[B_xo[s]])
                E("sp", ("dma_start", dict(out=xT_d[:, :, ti * 128:(ti + 1) * 128].rearrange("k p t -> p k t"),
                                                         in_=xo[s][:, :, :])), reads=[B_xo[s]], writes=[B_xT], dma=True)
        kb.barrier()

        for l in range(nlayers):
            last = (l == DEPTH - 1)
            ntok = L if last else NT
            cur = l % 2
            modT, sc1, sc2, B_mod, B_sc, B_gvec = modTs[cur], sc1s[cur], sc2s[cur], Bm[cur], Bs[cur], Bg[cur]
            if l not in prefetched:
                with ExitStack() as ph:
                    pre_, dma_, mm_, fin_ = P0_steps(l, ph, cur)
                    pre_()
                    for g in range(12):
                        dma_(g)
                        mm_(g)
                    fin_()
                kb.barrier()

            if do_mix:
                mixst = ExitStack()
                lrT = [sb(mixst, "lrT%d" % i, [32, NT]) for i in range(2)]
                with ExitStack() as ph:
                    hT = sb(ph, "hT", [128, 8, NT], BF16); B_hT = Buf()
                    xt = [sb(ph, "pxt%d" % i, [128, 8, 512]) for i in range(2)]; B_xt = [Buf(), Buf()]
                    scr = {"sq": sb(ph, "psq", [128, 8, 512]), "B_sq": Buf(), "rs": sb(ph, "prs", [128, 512]), "B_rs": Buf()}
                    tiles = [(i * 512, 512, 0) for i in range(8)] + [(L, 256, 1)]
                    for ti, (t0, T, col) in enumerate(tiles):
                        s = ti % 2
                        I("sp", "dma_start", out=xt[s][:, :, :T], in_=xT_d[:, :, t0:t0 + T].rearrange("k p t -> p k t"),
                          reads=[B_xT], writes=[B_xt[s]], dma=True)
                        norm_tile(ph, "p", xt[s], B_xt[s], T, lambda k, col=col: sc1[:, k, col:col + 1],
                                  lambda k, col=col: modT[:, k, col:col + 1], hT[:, :, t0:t0 + T], B_hT, scr)
                    cw = sb(ph, "cw", [128, 3, 24]); cbias = sb(ph, "cbias", [128, 24]); B_cw = Buf()
                    load_T(cw[:, :, :], hy_conv_w[l, :, :].rearrange("t (j p) -> (t j) p", p=128), 72, B_cw,
                           view=lambda a: a.rearrange("p (t j) -> p t j", t=3))
                    load_T(cbias[:, :], hy_conv_b[l, :].rearrange("(j p) -> j p", p=128), 24, B_cw)
                    I("dve", "memset", ap=lrT[0][:, :], constant=1.0, writes=[B_lr[0]])
                    I("dve", "memset", ap=lrT[1][:, :], constant=1.0, writes=[B_lr[1]])
                    wg = [sb(ph, "wg%d" % i, [128, 8, 512], BF16) for i in range(2)]; B_wg = [Buf(), Buf()]
                    wgi = [0]
                    st = [sb(ph, "st%d" % i, [128, 512], BF16) for i in range(4)]; B_st = [Buf() for _ in range(4)]
                    sti = [0]
                    uu = [sb(ph, "uu%d" % i, [128, 512]) for i in range(3)]; B_uu = [Buf() for _ in range(3)]

                    def load_w(cols):
                        s = wgi[0]; wgi[0] = 1 - s
                        o = 0
                        for (c0, n) in cols:
                            I("pool", "dma_start", out=wg[s][:, :, o:o + n], in_=w_in[l, :, c0:c0 + n].rearrange("(k p) n -> p k n", p=128),
                              writes=[B_wg[s]], dma=True)
                            o += n
                        return s

                    def fm_mm(s, j, t0, T, M=128):
                        pt, pb = psum()
                        for k in range(8):
                            I("pe", "matmul", out=pt[0:M, :T], lhsT=wg[s][:, k, j * 128:j * 128 + M], rhs=hT[:, k, t0:t0 + T],
                              start=(k == 0), stop=(k == 7), reads=[B_wg[s], B_hT], writes=[pb], inc=(k == 7))
                        return pt, pb

                    def stage_out(dst_ap, T):
                        i = sti[0]; sti[0] = (i + 1) % 4
                        return i

                    def fm_family(c0, nblk, dst, act, scale=1.0):
                        for g0 in range(0, nblk, 4):
                            nb = min(4, nblk - g0)
                            s = load_w([(c0 + g0 * 128, nb * 128)])
                            for (t0, T, col) in tiles:
                                for j in range(nb):
                                    pt, pb = fm_mm(s, j, t0, T)
                                    i = sti[0]; sti[0] = (i + 1) % 4
                                    I("act", "activation", out=st[i][:, :T], in_=pt[:, :T], func=act, scale=scale, reads=[pb], writes=[B_st[i]])
                                    I("sp", "dma_start", out=dst[g0 + j, :, t0:t0 + T], in_=st[i][:, :T], reads=[B_st[i]], writes=[B_scr], dma=True)

                    for cb in range(8):
                        s = load_w([(cb * 128, 128), (1024 + cb * 128, 128), (2048 + cb * 128, 128)])
                        for (t0, T, col) in tiles:
                            Wd = 64 if col == 0 else 256
                            R_ = T // Wd
                            pts = [fm_mm(s, j, t0, T) for j in range(3)]
                            for j in range(3):
                                pt, pb = pts[j]
                                blk = j * 8 + cb
                                u3 = uu[j][:, :T].rearrange("p (r w) -> p r w", w=Wd)
                                p3 = pt[:, :T].rearrange("p (r w) -> p r w", w=Wd)
                                I("act", "activation", out=uu[j][:, :T], in_=pt[:, :T], func=AF.Identity, scale=cw[:, 1, blk:blk + 1],
                                  bias=cbias[:, blk:blk + 1], reads=[pb, B_cw], writes=[B_uu[j]])
                                I("dve", "scalar_tensor_tensor", out=u3[:, :, 1:Wd], in0=p3[:, :, 0:Wd - 1], scalar=cw[:, 0, blk:blk + 1],
                                  in1=u3[:, :, 1:Wd], op0=ALU.mult, op1=ALU.add, reads=[pb, B_cw, B_uu[j]], writes=[B_uu[j]])
                                I("dve", "scalar_tensor_tensor", out=u3[:, :, 0:Wd - 1], in0=p3[:, :, 1:Wd], scalar=cw[:, 2, blk:blk + 1],
                                  in1=u3[:, :, 0:Wd - 1], op0=ALU.mult, op1=ALU.add, reads=[pb, B_cw, B_uu[j]], writes=[B_uu[j]])
                            i = sti[0]; sti[0] = (i + 1) % 4
                            I("act", "copy", out=st[i][:, :T], in_=uu[0][:, :T], reads=[B_uu[0]], writes=[B_st[i]])
                            I("sp", "dma_start", out=x0T_d[cb, :, t0:t0 + T], in_=st[i][:, :T], reads=[B_st[i]], writes=[B_scr], dma=True)
                            i = sti[0]; sti[0] = (i + 1) % 4
                            I("dve", "tensor_tensor", out=st[i][:, :T], in0=uu[1][:, :T], in1=uu[2][:, :T], op=ALU.mult,
                              reads=[B_uu[1], B_uu[2]], writes=[B_st[i]])
                            I("sp", "dma_start", out=zT_d[cb, :, t0:t0 + T], in_=st[i][:, :T], reads=[B_st[i]], writes=[B_scr], dma=True)
                    fm_family(3072, 4, qT_d, AF.Copy, scale=128.0 ** -0.5)
                    fm_family(3584, 4, kT_d, AF.Copy)
                    fm_family(5120, 8, srT_d, AF.Silu)
                    fm_family(6176, 8, sghT_d, AF.Sigmoid)
                    fm_family(7200, 8, sggT_d, AF.Sigmoid)
                    s = load_w([(6144, 32)])
                    for (t0, T, col) in tiles:
                        for d_ in range(2):
                            pt, pb = psum()
                            for k in range(8):
                                I("pe", "matmul", out=pt[0:16, :T], lhsT=wg[s][:, k, d_ * 16:d_ * 16 + 16], rhs=hT[:, k, t0:t0 + T],
                                  start=(k == 0), stop=(k == 7), reads=[B_wg[s], B_hT], writes=[pb], inc=(k == 7))
                            I("act", "copy", out=lrT[d_][0:16, t0:t0 + T], in_=pt[0:16, :T], reads=[pb], writes=[B_lr[d_]])
                    for (c0, ncol, dst) in ((3584, 512, ktok_d), (4096, 512, vtok_d), (4608, 512, vtok_d)):
                        s = load_w([(c0, 512)])
                        o0 = 512 if c0 == 4608 else 0
                        for tb in range(NT // 128):
                            pt, pb = psum()
                            for k in range(8):
                                I("pe", "matmul", out=pt[:, :], lhsT=hT[:, k, tb * 128:(tb + 1) * 128], rhs=wg[s][:, k, :],
                                  start=(k == 0), stop=(k == 7), reads=[B_wg[s], B_hT], writes=[pb], inc=(k == 7))
                            i = sti[0]; sti[0] = (i + 1) % 4
                            I("act" if tb % 2 == 0 else "dve", "copy" if tb % 2 == 0 else "tensor_copy", out=st[i][:, :], in_=pt[:, :], reads=[pb], writes=[B_st[i]])
                            I("sp", "dma_start", out=dst[tb * 128:(tb + 1) * 128, o0:o0 + 512], in_=st[i][:, :], reads=[B_st[i]], writes=[B_scr], dma=True)
                kb.barrier()

                with ExitStack() as ph:
                    gwa = sb(ph, "gwa", [32, 2, 512]); B_gwa = Buf()
                    I("sp", "dma_start", out=gwa[0:16, :, :], in_=gla_gate_w[l, :, :, :].rearrange("d r n -> r d n"), writes=[B_gwa], dma=True)
                    I("sp", "dma_start", out=gwa[16:17, :, :], in_=gla_gate_b[l:l + 1, :, :], writes=[B_gwa], dma=True)
                    um = sb(ph, "um", [128, 6, 128])
                    I("sp", "dma_start", out=um[:, :, :], in_=umats_d.rearrange("m p t -> p m t"), writes=[B_U], dma=True)
                    U_f, U_b, Us_f, Us_b, M_f, M_b = [um[:, i, :] for i in range(6)]
                    gng = sb(ph, "gng", [128, 2]); B_gng = Buf()
                    load_T(gng[:, :], gla_norm_g[l, :].rearrange("(j p) -> j p", p=128), 2, B_gng)
                    S = [sb(ph, "S%d" % h, [128, 256]) for h in range(4)]; B_S = [Buf() for _ in range(4)]
                    Sb = [sb(ph, "Sb%d" % h, [128, 256], BF16) for h in range(4)]; B_Sb = [Buf() for _ in range(4)]
                    qTl = [sb(ph, "qTl%d" % i, [128, 4, 512], BF16) for i in range(2)]
                    kTl = [sb(ph, "kTl%d" % i, [128, 4, 512], BF16) for i in range(2)]
                    ktl = [sb(ph, "ktl%d" % i, [128, 4, 512], BF16) for i in range(2)]
                    vtl = [sb(ph, "vtl%d" % i, [128, 4, 1024], BF16) for i in range(2)]
                    B_ld = [Buf(), Buf()]
                    srl = sb(ph, "srl", [128, 8, 512], BF16); B_srl = Buf()
                    ofl = sb(ph, "ofl", [128, 8, 512]); B_ofl = Buf()
                    oTs = sb(ph, "oTs", [128, 8, 512]); B_oTs = Buf()
                    osq = sb(ph, "osq", [128, 8, 512]); B_osq = Buf()
                    ygs = sb(ph, "ygs", [128, 8, 512], BF16); B_ygs = Buf()
                    rsn = sb(ph, "rsn", [128, 512]); B_rsn = Buf()
                    Gt = sb(ph, "Gt", [128, 512]); B_Gt = Buf()
                    e2 = [sb(ph, "e2_%d" % i, [128, 256]) for i in range(8)]; B_e2 = [Buf() for _ in range(8)]
                    en = [sb(ph, "en_%d" % i, [128, 128]) for i in range(8)]; B_en = [Buf() for _ in range(8)]
                    qtl_ = [sb(ph, "qtil%d" % i, [128, 128], BF16) for i in range(8)]; B_qt = [Buf() for _ in range(8)]
                    ktl_ = [sb(ph, "ktil%d" % i, [128, 128], BF16) for i in range(8)]; B_kt = [Buf() for _ in range(8)]
                    kht_ = [sb(ph, "khat%d" % i, [128, 128], BF16) for i in range(8)]; B_kh = [Buf() for _ in range(8)]
                    atm = [sb(ph, "atm%d" % i, [128, 128], BF16) for i in range(8)]; B_atm = [Buf() for _ in range(8)]
                    hs = [0]
                    scs = [(L, 2)] + [(i * 512, 4) for i in range(8)]
                    for dr in range(2):
                        for h in range(4):
                            I("pool", "memset", ap=S[h][:, :], constant=0.0, writes=[B_S[h]])
                            I("pool", "memset", ap=Sb[h][:, :], constant=0.0, writes=[B_Sb[h]])
                        order = scs if dr == 0 else [scs[0]] + scs[:0:-1]
                        Um, Usm, Mm = (U_f, Us_f, M_f) if dr == 0 else (U_b, Us_b, M_b)
                        endcol = 127 if dr == 0 else 0
                        for sci, (t0, nch) in enumerate(order):
                            T = nch * 128
                            b_ = sci % 2
                            I("sp", "dma_start", out=qTl[b_][:, :, :T], in_=qT_d[:, :, t0:t0 + T].rearrange("h p t -> p h t"),
                              reads=[B_scr], writes=[B_ld[b_]], dma=True)
                            I("sp", "dma_start", out=kTl[b_][:, :, :T], in_=kT_d[:, :, t0:t0 + T].rearrange("h p t -> p h t"),
                              reads=[B_scr], writes=[B_ld[b_]], dma=True)
                            I("sp", "dma_start", out=ktl[b_][:, :nch, :], in_=ktok_d[t0:t0 + T, :].rearrange("(c p) n -> p c n", p=128),
                              reads=[B_scr], writes=[B_ld[b_]], dma=True)
                            I("sp", "dma_start", out=vtl[b_][:, :nch, :], in_=vtok_d[t0:t0 + T, :].rearrange("(c p) n -> p c n", p=128),
                              reads=[B_scr], writes=[B_ld[b_]], dma=True)
                            if dr == 1:
                                I("sp", "dma_start", out=srl[:, :, :T], in_=srT_d[:, :, t0:t0 + T].rearrange("k p t -> p k t"),
                                  reads=[B_scr], writes=[B_srl], dma=True)
                                I("sp", "dma_start", out=ofl[:, :, :T], in_=oT_d[:, :, t0:t0 + T].rearrange("k p t -> p k t"),
                                  reads=[B_oT], writes=[B_ofl], dma=True)
                            chunks = list(range(nch)) if dr == 0 else list(range(nch - 1, -1, -1))
                            def stA(cc, gen):
                                tc0 = cc * 128
                                pg, pgb = psum()
                                I("pe", "matmul", out=pg[:, :], lhsT=lrT[dr][0:17, t0 + tc0:t0 + tc0 + 128], rhs=gwa[0:17, dr, :],
                                  start=True, stop=True, reads=[B_lr[dr], B_gwa], writes=[pgb])
                                I("act", "activation", out=Gt[:, :], in_=pg[:, :], func=AF.Exp, scale=-1.0, reads=[pgb], writes=[B_Gt])
                                I("act", "activation", out=Gt[:, :], in_=Gt[:, :], func=AF.Ln, bias=1.0, reads=[B_Gt], writes=[B_Gt])
                                for h in range(4):
                                    w_ = gen * 4 + h
                                    Gh = Gt[:, h * 128:(h + 1) * 128]
                                    p1, p1b = psum()
                                    I("pe", "matmul", out=p1[:, 0:128], lhsT=Gh, rhs=Um[:, :], start=True, stop=True,
                                      reads=[B_Gt, B_U], writes=[p1b], inc=False)
                                    I("pe", "matmul", out=p1[:, 128:256], lhsT=Usm[:, :], rhs=Gh, start=True, stop=True,
                                      reads=[B_Gt, B_U], writes=[p1b])
                                    I("act", "activation", out=e2[w_][:, :], in_=p1[:, 0:256], func=AF.Exp, reads=[p1b], writes=[B_e2[w_]])
                                    I("act", "activation", out=en[w_][:, :], in_=p1[:, 0:128], func=AF.Exp, scale=-1.0, reads=[p1b], writes=[B_en[w_]])
                                    I("dve", "tensor_tensor", out=qtl_[w_][:, :], in0=qTl[b_][:, h, tc0:tc0 + 128], in1=e2[w_][:, 0:128], op=ALU.mult,
                                      reads=[B_ld[b_], B_e2[w_]], writes=[B_qt[w_]])
                                    I("pool", "tensor_tensor", out=ktl_[w_][:, :], in0=kTl[b_][:, h, tc0:tc0 + 128], in1=en[w_][:, :], op=ALU.mult,
                                      reads=[B_ld[b_], B_en[w_]], writes=[B_kt[w_]])
                                    I("pool", "tensor_tensor", out=kht_[w_][:, :], in0=ktl[b_][:, cc, h * 128:(h + 1) * 128], in1=e2[w_][:, 128:256], op=ALU.mult,
                                      reads=[B_ld[b_], B_e2[w_]], writes=[B_kh[w_]])

                            def stB(cc, gen):
                                tc0 = cc * 128
                                for h in range(4):
                                    w_ = gen * 4 + h
                                    p2, p2b = psum()
                                    I("pe", "matmul", out=p2[:, 0:128], lhsT=ktl_[w_][:, :], rhs=qtl_[w_][:, :], start=True, stop=True,
                                      reads=[B_kt[w_], B_qt[w_]], writes=[p2b])
                                    I("dve", "tensor_tensor", out=atm[w_][:, :], in0=p2[:, 0:128], in1=Mm[:, :], op=ALU.mult,
                                      reads=[p2b, B_U], writes=[B_atm[w_]])

                            def stC(cc, gen):
                                tc0 = cc * 128
                                for h in range(4):
                                    w_ = gen * 4 + h
                                    p3, p3b = psum()
                                    for eb in range(2):
                                        I("pe", "matmul", out=p3[:, eb * 128:(eb + 1) * 128], lhsT=vtl[b_][:, cc, h * 256 + eb * 128:h * 256 + (eb + 1) * 128],
                                          rhs=atm[w_][:, :], start=True, stop=False, reads=[B_ld[b_], B_atm[w_]], writes=[p3b], inc=False)
                                        I("pe", "matmul", out=p3[:, eb * 128:(eb + 1) * 128], lhsT=Sb[h][:, eb * 128:(eb + 1) * 128],
                                          rhs=qtl_[w_][:, :], start=False, stop=True, reads=[B_Sb[h], B_qt[w_]], writes=[p3b], inc=(eb == 1))
                                    o3 = p3[:, 0:256].rearrange("p (j t) -> p j t", j=2)
                                    if dr == 0:
                                        I("act", "copy", out=oTs[:, 2 * h:2 * h + 2, tc0:tc0 + 128], in_=o3, reads=[p3b], writes=[B_oTs])
                                    else:
                                        I("dve", "tensor_tensor", out=oTs[:, 2 * h:2 * h + 2, tc0:tc0 + 128], in0=o3, in1=ofl[:, 2 * h:2 * h + 2, tc0:tc0 + 128],
                                          op=ALU.add, reads=[p3b, B_ofl], writes=[B_oTs])
                                    p4, p4b = psum()
                                    I("pe", "matmul", out=p4[:, 0:256], lhsT=kht_[w_][:, :], rhs=vtl[b_][:, cc, h * 256:(h + 1) * 256], start=True, stop=True,
                                      reads=[B_kh[w_], B_ld[b_]], writes=[p4b])
                                    I("dve", "scalar_tensor_tensor", out=S[h][:, :], in0=S[h][:, :], scalar=e2[w_][:, endcol:endcol + 1], in1=p4[:, 0:256],
                                      op0=ALU.mult, op1=ALU.add, reads=[B_S[h], B_e2[w_], p4b], writes=[B_S[h]])
                                    I("act", "copy", out=Sb[h][:, :], in_=S[h][:, :], reads=[B_S[h]], writes=[B_Sb[h]])

                            gens = []
                            for cc in chunks:
                                gens.append(hs[0]); hs[0] = 1 - hs[0]
                            for i_, cc in enumerate(chunks):
                                stA(cc, gens[i_])
                                if i_ > 0:
                                    stC(chunks[i_ - 1], gens[i_ - 1])
                                stB(cc, gens[i_])
                            stC(chunks[-1], gens[-1])
                            if dr == 0:
                                I("sp", "dma_start", out=oT_d[:, :, t0:t0 + T].rearrange("k p t -> p k t"), in_=oTs[:, :, :T],
                                  reads=[B_oTs], writes=[B_oT], dma=True)
                            else:
                                I("act", "activation", out=osq[:, :, :T], in_=oTs[:, :, :T], func=AF.Square, reads=[B_oTs], writes=[B_osq])
                                for h in range(4):
                                    pn, pnb = psum()
                                    for eb in range(2):
                                        I("pe", "matmul", out=pn[:, :T], lhsT=ones[:, :], rhs=osq[:, 2 * h + eb, :T], start=(eb == 0), stop=(eb == 1),
                                          reads=[B_ones, B_osq], writes=[pnb], inc=(eb == 1))
                                    I("dve", "tensor_scalar", out=rsn[:, :T], in0=pn[:, :T], scalar1=1.0 / 256, scalar2=EPS, op0=ALU.mult, op1=ALU.add,
                                      reads=[pnb], writes=[B_rsn])
                                    I("act", "activation", out=rsn[:, :T], in_=rsn[:, :T], func=AF.Sqrt, reads=[B_rsn], writes=[B_rsn])
                                    I("dve", "reciprocal", out=rsn[:, :T], in_=rsn[:, :T], reads=[B_rsn], writes=[B_rsn])
                                    for eb in range(2):
                                        I("dve", "tensor_tensor", out=osq[:, 2 * h + eb, :T], in0=oTs[:, 2 * h + eb, :T], in1=rsn[:, :T], op=ALU.mult,
                                          reads=[B_oTs, B_rsn, B_osq], writes=[B_osq])
                                        I("dve", "scalar_tensor_tensor", out=ygs[:, 2 * h + eb, :T], in0=osq[:, 2 * h + eb, :T], scalar=gng[:, eb:eb + 1],
                                          in1=srl[:, 2 * h + eb, :T], op0=ALU.mult, op1=ALU.mult, reads=[B_osq, B_gng, B_srl], writes=[B_ygs])
                                I("sp", "dma_start", out=ygT_d[:, :, t0:t0 + T].rearrange("k p t -> p k t"), in_=ygs[:, :, :T],
                                  reads=[B_ygs], writes=[B_scr2], dma=True)
                kb.barrier()
                mixst.close()

                if do_hy:
                    with ExitStack() as ph:
                        cfgs = [dict(R=64, P1=128, PK=128, boff=64, Lc=4096, tb=0, ci=0)]
                        if not last:
                            cfgs.append(dict(R=4, P1=8, PK=36, boff=32, Lc=256, tb=L, ci=1))
                        fw1 = sb(ph, "fw1", [33, 64]); fw2 = sb(ph, "fw2", [64, 64]); fpar = sb(ph, "fpar", [64, 5]); B_fp = Buf()
                        I("sp", "dma_start", out=fw1[:, :], in_=hy_filt_w1[l, :, :], writes=[B_fp], dma=True)
                        I("sp", "dma_start", out=fw2[:, :], in_=hy_filt_w2[l, :, :], writes=[B_fp], dma=True)
                        for i_, src in enumerate((hy_filt_freq, hy_filt_b1, hy_filt_b2)):
                            I("sp", "dma_start", out=fpar[:, i_:i_ + 1], in_=src[l, :].rearrange("(p o) -> p o", o=1), writes=[B_fp], dma=True)
                        I("dve", "tensor_tensor", out=fpar[:, 3:4], in0=fpar[:, 0:1], in1=fpar[:, 1:2], op=ALU.mult, reads=[B_fp], writes=[B_fp])
                        I("dve", "tensor_tensor", out=fpar[:, 4:5], in0=fpar[:, 0:1], in1=fpar[:, 2:3], op=ALU.mult, reads=[B_fp], writes=[B_fp])
                        w3s = sb(ph, "w3s", [64, 2048], BF16); B_w3 = Buf()
                        I("pool", "dma_start", out=w3s[:, :], in_=hy_filt_w3[l, :, :], writes=[B_w3], dma=True)
                        nad = sb(ph, "nad", [1, 1024]); B_nad = Buf()
                        I("sp", "dma_start", out=nad[:, :], in_=hy_decay[l:l + 1, :], writes=[B_nad], dma=True)
                        nad2 = sb(ph, "nad2", [1, 1024])
                        I("dve", "tensor_scalar", out=nad2[:, :], in0=nad[:, :], scalar1=-1.0, scalar2=None, op0=ALU.mult,
                          reads=[B_nad], writes=[B_nad])
                        I("dve", "tensor_tensor", out=nad[:, :], in0=nad[:, :], in1=nad2[:, :], op=ALU.min,
                          reads=[B_nad], writes=[B_nad])
                        a2tab = [sb(ph, "a2t0", [64, 8192], BF16), sb(ph, "a2t1", [64, 512], BF16)]; B_a2 = [Buf(), Buf()]
                        tqs = [sb(ph, "tq0", [4, 8192], BF16), sb(ph, "tq1", [4, 512], BF16)]; B_tq = Buf()
                        I("sp", "dma_start", out=tqs[0][:, :], in_=tq_d[0:4, 0:8192], writes=[B_tq], dma=True)
                        I("sp", "dma_start", out=tqs[1][:, :], in_=tq_d[0:4, 8192:8704], writes=[B_tq], dma=True)
                        nhi = sb(ph, "nhi", [1, 1024], BF16); nlo = sb(ph, "nlo", [1, 1024], BF16); nad4 = sb(ph, "nad4", [4, 1024], BF16)
                        B_nad4 = Buf()
                        I("dve", "tensor_copy", out=nhi[:, :], in_=nad[:, :], reads=[B_nad], writes=[B_nad4])
                        I("dve", "tensor_tensor", out=nad2[:, :], in0=nad[:, :], in1=nhi[:, :], op=ALU.subtract, reads=[B_nad, B_nad4], writes=[B_nad4])
                        I("dve", "tensor_copy", out=nlo[:, :], in_=nad2[:, :], reads=[B_nad4], writes=[B_nad4])
                        for r_, src_ in ((0, nhi), (1, nlo), (2, nhi), (3, nlo)):
                            I("sp", "dma_start", out=nad4[r_:r_ + 1, :], in_=src_[:, :], reads=[B_nad4], writes=[B_nad4], dma=True)
                        cst = {}
                        B_cst = Buf()
                        for ci_, (R_, P1_, PK_) in enumerate(((64, 128, 128), (4, 8, 36))):
                            for nm, shp, src in (("F1", [R_, 2 * P1_], dF1[ci_]), ("F1h", [PK_, 2 * P1_], dF1h[ci_]),
                                                 ("Ea", [P1_, 4 * R_], dEa[ci_]), ("Eb", [P1_, 4 * R_], dEb[ci_])):
                                t_ = sb(ph, "%s%d" % (nm, ci_), shp, BF16)
                                I("sp", "dma_start", out=t_[:, :], in_=src[:, :], writes=[B_cst], dma=True)
                                cst[(nm, ci_)] = t_
                            t_ = sb(ph, "R4_%d" % ci_, [65, 4, 64], BF16)
                            I("sp", "dma_start", out=t_[:, :, :], in_=dR4[ci_][:, :, :], writes=[B_cst], dma=True)
                            cst[("R4", ci_)] = t_
                        Eca = sb(ph, "Eca", [128, 130], BF16); Ecb = sb(ph, "Ecb", [128, 130], BF16)
                        I("sp", "dma_start", out=Eca[:, :], in_=dEc[0][:, :], writes=[B_cst], dma=True)
                        I("sp", "dma_start", out=Ecb[:, :], in_=dEc[1][:, :], writes=[B_cst], dma=True)
                        G4 = sb(ph, "G4", [64, 4, 260], BF16)
                        I("sp", "dma_start", out=G4[:, :, :], in_=dG4[:, :, :], writes=[B_cst], dma=True)

                        tgst = ExitStack()
                        zfc = [sb(tgst, "zfc%d" % i, [33, 512]) for i in range(2)]; B_zfc = [Buf(), Buf()]
                        arg = sb(tgst, "arg", [64, 512]); B_arg = Buf()
                        tw = sb(tgst, "tw", [64, 512]); B_tw = Buf()
                        a1 = sb(tgst, "a1", [64, 512]); B_a1 = Buf()

                        def sin_layer(pt, pb, n, fcol, dst, B_dst):
                            I("dve", "tensor_scalar", out=arg[:, :n], in0=pt[0:64, :n], scalar1=fpar[:, 0:1], scalar2=fpar[:, fcol:fcol + 1],
                              op0=ALU.mult, op1=ALU.add, reads=[pb, B_fp], writes=[B_arg])
                            I("dve", "tensor_scalar", out=tw[:, :n], in0=arg[:, :n], scalar1=PI, scalar2=-2 * PI, op0=ALU.is_gt, op1=ALU.mult,
                              reads=[B_arg], writes=[B_tw])
                            I("dve", "tensor_tensor", out=arg[:, :n], in0=arg[:, :n], in1=tw[:, :n], op=ALU.add, reads=[B_arg, B_tw], writes=[B_arg])
                            I("dve", "tensor_scalar", out=tw[:, :n], in0=arg[:, :n], scalar1=-PI, scalar2=2 * PI, op0=ALU.is_lt, op1=ALU.mult,
                              reads=[B_arg], writes=[B_tw])
                            I("dve", "tensor_tensor", out=arg[:, :n], in0=arg[:, :n], in1=tw[:, :n], op=ALU.add, reads=[B_arg, B_tw], writes=[B_arg])
                            I("dve", "tensor_scalar", out=arg[:, :n], in0=arg[:, :n], scalar1=-PI, scalar2=PI, op0=ALU.max, op1=ALU.min,
                              reads=[B_arg], writes=[B_arg])
                            I("act", "activation", out=dst, in_=arg[:, :n], func=AF.Sin, reads=[B_arg], writes=[B_dst])

                        for cf in cfgs:
                            ci_, Lc = cf["ci"], cf["Lc"]
                            zoff = 0 if ci_ == 0 else 8192
                            for c0 in range(0, 2 * Lc, 512):
                                n = min(512, 2 * Lc - c0)
                                s = (c0 // 512) % 2
                                I("sp", "dma_start", out=zfc[s][:, :n], in_=zfeat_d[:, zoff + c0:zoff + c0 + n], writes=[B_zfc[s]], dma=True)
                                pt, pb = psum()
                                I("pe", "matmul", out=pt[0:64, :n], lhsT=fw1[:, :], rhs=zfc[s][:, :n], start=True, stop=True,
                                  reads=[B_fp, B_zfc[s]], writes=[pb])
                                sin_layer(pt, pb, n, 3, a1[:, :n], B_a1)
                                pt, pb = psum()
                                I("pe", "matmul", out=pt[0:64, :n], lhsT=fw2[:, :], rhs=a1[:, :n], start=True, stop=True,
                                  reads=[B_fp, B_a1], writes=[pb])
                                sin_layer(pt, pb, n, 4, a2tab[ci_][:, c0:c0 + n], B_a2[ci_])
                            I("dve", "memset", ap=a2tab[ci_][:, Lc:Lc + 1], constant=0.0, writes=[B_a2[ci_]])
                        kb.barrier()
                        tgst.close()

                        if not last:
                            with ExitStack() as cst_:
                                Fc = sb(cst_, "Fc", [128, 4, 512], BF16); Finv = sb(cst_, "Finv", [128, 4, 256], BF16); B_Fc = Buf()
                                I("sp", "dma_start", out=Fc[:, :, :], in_=dFc[:, :, :], writes=[B_Fc], dma=True)
                                I("sp", "dma_start", out=Finv[:, :, :], in_=dFinv[:, :, :], writes=[B_Fc], dma=True)
                                hct = sb(cst_, "hct", [128, 4, 1024], BF16); B_hct = Buf()
                                Hs = sb(cst_, "Hs", [128, 4, 1024]); B_Hs = Buf()
                                wexc = [sb(cst_, "wexc%d" % i, [128, 512]) for i in range(2)]; B_wexc = [Buf(), Buf()]
                                zcf = sb(cst_, "zcf", [128, 8, 256], BF16); B_zcf = Buf()
                                zcf32 = sb(cst_, "zcf32", [128, 8, 256]); B_zcf32 = Buf()
                                zct = sb(cst_, "zct", [128, 2, 1024], BF16); B_zct = Buf()
                                Yc = sb(cst_, "Yc", [128, 4, 1024], BF16); B_Yc = Buf()
                                yct = sb(cst_, "yct", [128, 2, 1024]); B_yct = Buf()
                                ycf = sb(cst_, "ycf", [128, 8, 256], BF16); B_ycf = Buf()
                                tA = [sb(cst_, "tA%d" % i, [128, 512]) for i in range(2)]; B_tA = [Buf(), Buf()]
                                tB = [sb(cst_, "tB%d" % i, [128, 512]) for i in range(2)]; B_tB = [Buf(), Buf()]
                                I("sp", "dma_start", out=zcf[:, :, :], in_=zT_d[:, :, L:NT].rearrange("k p t -> p k t"), reads=[B_scr], writes=[B_zcf], dma=True)
                                I("act", "copy", out=zcf32[:, :, :], in_=zcf[:, :, :], reads=[B_zcf], writes=[B_zcf32])
                                wi = 0
                                for j in range(4):
                                    half = 0 if j < 2 else 1
                                    for cN in range(2):
                                        csl = slice(cN * 512, (cN + 1) * 512)
                                        psh, pshb = psum()
                                        I("pe", "matmul", out=psh[:, :], lhsT=a2tab[1][:, j * 128:(j + 1) * 128],
                                          rhs=w3s[:, half * 1024 + cN * 512:half * 1024 + (cN + 1) * 512], start=True, stop=True,
                                          reads=[B_a2[1], B_w3], writes=[pshb])
                                        psw, pswb = psum()
                                        I("pe", "matmul", out=psw[:, :], lhsT=tqs[1][0:4, j * 128:(j + 1) * 128], rhs=nad4[0:4, csl], start=True, stop=True,
                                          reads=[B_tq, B_nad4], writes=[pswb])
                                        w_ = wi % 2; wi += 1
                                        I("act", "activation", out=wexc[w_][:, :], in_=psw[:, :], func=AF.Exp, reads=[pswb], writes=[B_wexc[w_]])
                                        I("dve", "tensor_tensor", out=hct[:, j, csl], in0=psh[:, :], in1=wexc[w_][:, :], op=ALU.mult,
                                          reads=[pshb, B_wexc[w_]], writes=[B_hct])
                                for m in range(4):
                                    for cN in range(2):
                                        csl = slice(cN * 512, (cN + 1) * 512)
                                        pt, pb = psum()
                                        for j in range(4):
                                            I("pe", "matmul", out=pt[:, :], lhsT=Fc[:, j, m * 128:(m + 1) * 128], rhs=hct[:, j, csl], start=(j == 0), stop=(j == 3),
                                              reads=[B_Fc, B_hct], writes=[pb], inc=(j == 3))
                                        if (m + cN) % 2 == 0:
                                            I("act", "copy", out=Hs[:, m, csl], in_=pt[:, :], reads=[pb], writes=[B_Hs])
                                        else:
                                            I("dve", "tensor_copy", out=Hs[:, m, csl], in_=pt[:, :], reads=[pb], writes=[B_Hs])
                                for sc in range(2):
                                    for cbh in range(2):
                                        pt, pb = psum()
                                        for jj in range(4):
                                            cb = cbh * 4 + jj
                                            I("pe", "transpose", out=pt[:, jj * 128:(jj + 1) * 128], in_=zcf32[:, cb, sc * 128:(sc + 1) * 128], identity=ident[:, :],
                                              reads=[B_zcf32, B_ident], writes=[pb], inc=(jj == 3))
                                        I("act", "copy", out=zct[:, sc, cbh * 512:(cbh + 1) * 512], in_=pt[:, :], reads=[pb], writes=[B_zct])
                                for cN in range(2):
                                    csl = slice(cN * 512, (cN + 1) * 512)
                                    pz = []
                                    for m in range(4):
                                        pt, pb = psum()
                                        for j in range(2):
                                            I("pe", "matmul", out=pt[:, :], lhsT=Fc[:, j, m * 128:(m + 1) * 128], rhs=zct[:, j, csl], start=(j == 0), stop=(j == 1),
                                              reads=[B_Fc, B_zct], writes=[pb], inc=(j == 1))
                                        pz.append((pt, pb))
                                    for b_ in range(2):
                                        (rp, rb), (ip, ib) = pz[b_], pz[2 + b_]
                                        I("dve", "tensor_tensor", out=tA[0][:, :], in0=rp[:, :], in1=Hs[:, b_, csl], op=ALU.mult, reads=[rb, B_Hs], writes=[B_tA[0]])
                                        I("dve", "tensor_tensor", out=tB[0][:, :], in0=ip[:, :], in1=Hs[:, 2 + b_, csl], op=ALU.mult, reads=[ib, B_Hs], writes=[B_tB[0]])
                                        I("pool", "tensor_tensor", out=Yc[:, b_, csl], in0=tA[0][:, :], in1=tB[0][:, :], op=ALU.subtract,
                                          reads=[B_tA[0], B_tB[0]], writes=[B_Yc])
                                        I("dve", "tensor_tensor", out=tA[1][:, :], in0=rp[:, :], in1=Hs[:, 2 + b_, csl], op=ALU.mult, reads=[rb, B_Hs], writes=[B_tA[1]])
                                        I("dve", "tensor_tensor", out=tB[1][:, :], in0=ip[:, :], in1=Hs[:, b_, csl], op=ALU.mult, reads=[ib, B_Hs], writes=[B_tB[1]])
                                        I("pool", "tensor_tensor", out=Yc[:, 2 + b_, csl], in0=tA[1][:, :], in1=tB[1][:, :], op=ALU.add,
                                          reads=[B_tA[1], B_tB[1]], writes=[B_Yc])
                                    for blk in (0, 2):
                                        pp, ppb = pz[blk]
                                        I("dve", "tensor_tensor", out=Yc[0:1, blk, csl], in0=pp[0:1, :], in1=Hs[0:1, blk, csl], op=ALU.mult,
                                          reads=[ppb, B_Hs, B_Yc], writes=[B_Yc])
                                for tbk in range(2):
                                    for cN in range(2):
                                        csl = slice(cN * 512, (cN + 1) * 512)
                                        pt, pb = psum()
                                        for j in range(4):
                                            I("pe", "matmul", out=pt[:, :], lhsT=Finv[:, j, tbk * 128:(tbk + 1) * 128], rhs=Yc[:, j, csl], start=(j == 0), stop=(j == 3),
                                              reads=[B_Fc, B_Yc], writes=[pb], inc=(j == 3))
                                        I("act", "copy", out=yct[:, tbk, csl], in_=pt[:, :], reads=[pb], writes=[B_yct])
                                for cbh in range(2):
                                    for tbk in range(2):
                                        pt, pb = psum()
                                        for jj in range(4):
                                            cb = cbh * 4 + jj
                                            I("pe", "transpose", out=pt[:, jj * 128:(jj + 1) * 128], in_=yct[:, tbk, cb * 128:(cb + 1) * 128], identity=ident[:, :],
                                              reads=[B_yct, B_ident], writes=[pb], inc=(jj == 3))
                                        I("dve", "tensor_copy", out=ycf[:, cbh * 4:(cbh + 1) * 4, tbk * 128:(tbk + 1) * 128],
                                          in_=pt[:, :].rearrange("p (j t) -> p j t", j=4), reads=[pb], writes=[B_ycf])
                                I("sp", "dma_start", out=ycT_d[:, :, L:NT].rearrange("k p t -> p k t"), in_=ycf[:, :, :], reads=[B_ycf], writes=[B_yc], dma=True)
                            kb.barrier()

                        hq2 = sb(ph, "hq2", [128, 256, 64], BF16); B_hq = Buf()
                        wexp = [sb(ph, "wexp%d" % i, [128, 256]) for i in range(2)]; B_we = [Buf(), Buf()]
                        HAB = sb(ph, "HAB", [128, 32, 260], BF16); B_HAB = [Buf() for _ in range(32)]
                        zin = [sb(ph, "zin%d" % i, [64, 32, 64], BF16) for i in range(2)]; B_zin = [Buf(), Buf()]
                        yout = [sb(ph, "yout%d" % i, [64, 32, 64], BF16) for i in range(2)]; B_yo = [Buf(), Buf()]
                        AhG = sb(ph, "AhG", [64, 32, 256], BF16); B_AhG = [Buf() for _ in range(16)]
                        AsG = sb(ph, "AsG", [64, 32, 256], BF16); B_AsG = [Buf() for _ in range(16)]
                        YsG = sb(ph, "YsG", [128, 32, 130], BF16); B_YsG = [Buf() for _ in range(32)]
                        BsG = sb(ph, "BsG", [65, 32, 130], BF16); B_BsG = [Buf() for _ in range(16)]
                        tt = [sb(ph, "tt%d" % i, [128, 260]) for i in range(4)]; B_tt = [Buf() for _ in range(4)]
                        rot = {"tt": 0, "we": 0, "g": 0, "ev": 0}

                        def nxt(key, n=2):
                            v = rot[key]; rot[key] = (v + 1) % n
                            return v

                        def evac(out, in_, pb, B_out):
                            if nxt("ev") == 0:
                                I("act", "copy", out=out, in_=in_, reads=[pb], writes=[B_out])
                            else:
                                I("dve", "tensor_copy", out=out, in_=in_, reads=[pb], writes=[B_out])

                        cf = cfgs[0]
                        R_, P1, PK, boff, Lc, tb, ci_ = cf["R"], cf["P1"], cf["PK"], cf["boff"], cf["Lc"], cf["tb"], cf["ci"]
                        F1, F1h, Ea, Eb, R4 = [cst[(nm, ci_)] for nm in ("F1", "F1h", "Ea", "Eb", "R4")]
                        pf = None
                        if l + 1 < nlayers:
                            pf = P0_steps(l + 1, ph, (l + 1) % 2, W=256)
                            pf[0]()
                            prefetched.add(l + 1)
                        gidx = 0
                        for cq in range(4):
                            for l2 in range(64):
                                psh, pshb = psum()
                                psw, pswb = psum()
                                for half in range(2):
                                    r0 = 0 if half == 0 else boff
                                    base = half * Lc + l2
                                    cols = slice(base, base + 64 * (R_ - 1) + 1, 64)
                                    I("pe", "matmul", out=psh[r0:r0 + R_, 0:256], lhsT=a2tab[ci_][:, cols],
                                      rhs=w3s[:, half * 1024 + cq * 256:half * 1024 + (cq + 1) * 256], start=True, stop=True,
                                      reads=[B_a2[ci_], B_w3], writes=[pshb])
                                    I("pe", "matmul", out=psw[r0:r0 + R_, 0:256], lhsT=tqs[ci_][0:4, cols], rhs=nad4[0:4, cq * 256:(cq + 1) * 256],
                                      start=True, stop=True, reads=[B_tq, B_nad4], writes=[pswb])
                                w_ = nxt("we")
                                I("act", "activation", out=wexp[w_][:, :], in_=psw[:, 0:256], func=AF.Exp, reads=[pswb], writes=[B_we[w_]])
                                I("dve", "tensor_tensor", out=hq2[:, :, l2], in0=psh[:, 0:256], in1=wexp[w_][:, :],
                                  op=ALU.mult, reads=[pshb, B_we[w_]], writes=[B_hq])
                            for g in range(8):
                                if pf is not None:
                                    if 1 <= gidx <= 24:
                                        pf[2](gidx - 1)
                                    if gidx < 24:
                                        pf[1](gidx)
                                gidx += 1
                                c_lo = cq * 256 + g * 32
                                cb, p0 = c_lo // 128, c_lo % 128
                                gs = nxt("g")
                                for hh in range(2):
                                    I("sp", "dma_start", out=zin[gs][0:R_, hh * 16:(hh + 1) * 16, :],
                                      in_=zT_d[cb, p0 + hh * 16:p0 + (hh + 1) * 16, tb:tb + Lc].rearrange("c (a b) -> a c b", b=64),
                                      reads=[B_scr], writes=[B_zin[gs]], dma=True)
                                for pr in range(16):
                                    pt, pb = psum()
                                    for j in range(2):
                                        I("pe", "matmul", out=pt[0:64, j * 256:(j + 1) * 256], lhsT=hq2[:, g * 32 + pr * 2 + j, :], rhs=F1h[:, :],
                                          start=True, stop=True, reads=[B_hq, B_cst], writes=[pb], inc=(j == 1))
                                    evac(AhG[:, pr * 2:pr * 2 + 2, :], pt[0:64, :].rearrange("p (j w) -> p j w", j=2), pb, B_AhG[pr])
                                for pr in range(16):
                                    pt, pb = psum()
                                    for j in range(2):
                                        I("pe", "matmul", out=pt[0:64, j * 256:(j + 1) * 256], lhsT=zin[gs][:, pr * 2 + j, :], rhs=F1[:, :],
                                          start=True, stop=True, reads=[B_zin[gs], B_cst], writes=[pb], inc=(j == 1))
                                    evac(AsG[:, pr * 2:pr * 2 + 2, :], pt[0:64, :].rearrange("p (j w) -> p j w", j=2), pb, B_AsG[pr])
                                for ch in range(32):
                                    pt2, pb2 = psum()
                                    I("pe", "matmul", out=pt2[:, 0:260], lhsT=AhG[:, ch, 0:128], rhs=G4[:, 2, :], start=True, stop=False,
                                      reads=[B_AhG[ch // 2], B_cst], writes=[pb2], inc=False)
                                    I("pe", "matmul", out=pt2[:, 0:260], lhsT=AhG[:, ch, 128:256], rhs=G4[:, 3, :], start=False, stop=True,
                                      reads=[B_AhG[ch // 2], B_cst], writes=[pb2])
                                    evac(HAB[:, ch, :], pt2[:, 0:260], pb2, B_HAB[ch])
                                for ch in range(32):
                                    pt2, pb2 = psum()
                                    I("pe", "matmul", out=pt2[:, 0:260], lhsT=AsG[:, ch, 0:128], rhs=G4[:, 0, :], start=True, stop=False,
                                      reads=[B_AsG[ch // 2], B_cst], writes=[pb2], inc=False)
                                    I("pe", "matmul", out=pt2[:, 0:260], lhsT=AsG[:, ch, 128:256], rhs=G4[:, 1, :], start=False, stop=True,
                                      reads=[B_AsG[ch // 2], B_cst], writes=[pb2])
                                    t_ = nxt("tt", 4)
                                    I("dve", "tensor_tensor", out=tt[t_][:, :], in0=pt2[:, 0:260], in1=HAB[:, ch, :], op=ALU.mult,
                                      reads=[pb2, B_HAB[ch]], writes=[B_tt[t_]])
                                    I("pool", "tensor_tensor", out=YsG[:, ch, :], in0=tt[t_][:, 0:130], in1=tt[t_][:, 130:260], op=ALU.add,
                                      reads=[B_tt[t_]], writes=[B_YsG[ch]])
                                for pr in range(16):
                                    pt3, pb3 = psum()
                                    for j in range(2):
                                        ch = pr * 2 + j
                                        I("pe", "matmul", out=pt3[0:65, j * 256:j * 256 + 130], lhsT=YsG[:, ch, 0:65], rhs=Eca[:, :], start=True, stop=False,
                                          reads=[B_YsG[ch], B_cst], writes=[pb3], inc=False)
                                        I("pe", "matmul", out=pt3[0:65, j * 256:j * 256 + 130], lhsT=YsG[:, ch, 65:130], rhs=Ecb[:, :], start=False, stop=True,
                                          reads=[B_YsG[ch], B_cst], writes=[pb3], inc=(j == 1))
                                    evac(BsG[:, pr * 2:pr * 2 + 2, :], pt3[0:65, :].rearrange("p (j w) -> p j w", j=2)[:, :, 0:130], pb3, B_BsG[pr])
                                for oc in range(4):
                                    pt4, pb4 = psum()
                                    for jj in range(8):
                                        ch = oc * 8 + jj
                                        for i4, o_ in enumerate((1, 0, 66, 65)):
                                            I("pe", "matmul", out=pt4[0:R_, jj * 64:(jj + 1) * 64], lhsT=BsG[:, ch, o_:o_ + 64], rhs=R4[:, i4, :],
                                              start=(i4 == 0), stop=(i4 == 3), reads=[B_BsG[ch // 2], B_cst], writes=[pb4], inc=(i4 == 3 and jj == 7))
                                    evac(yout[gs][0:R_, oc * 8:(oc + 1) * 8, :], pt4[0:R_, :].rearrange("p (c n) -> p c n", c=8), pb4, B_yo[gs])
                                for hh in range(2):
                                    I("sp", "dma_start", out=ycT_d[cb, p0 + hh * 16:p0 + (hh + 1) * 16, tb:tb + Lc].rearrange("c (a b) -> a c b", b=64),
                                      in_=yout[gs][0:R_, hh * 16:(hh + 1) * 16, :], reads=[B_yo[gs]], writes=[B_yc], dma=True)
                        if pf is not None:
                            pf[3]()
                    kb.barrier()

                if dbg and l == 0:
                    for n, src in (("z", zT_d), ("yc", ycT_d), ("x0", x0T_d), ("yg", ygT_d), ("sgh", sghT_d), ("sgg", sggT_d), ("sr", srT_d)):
                        for k in range(8):
                            I("sp", "dma_start", out=dbg_t[n][k, :, :], in_=src[k, :, :], reads=[B_scr, B_scr2, B_yc], writes=[Buf()], dma=True)
                    kb.barrier()
                with ExitStack() as ph:
                    TT = 512
                    wb = [sb(ph, "wb%d" % i, [128, 8, D], BF16) for i in range(3)]; B_wb = [Buf(), Buf(), Buf()]
                    for i, wsrc in enumerate((w_branch_hy, w_branch_gla, w_out)):
                        I("pool", "dma_start", out=wb[i][:, :, :], in_=wsrc[l, :, :].rearrange("(k p) n -> p k n", p=128), writes=[B_wb[i]], dma=True)
                    skp = sb(ph, "skp", [128, 8]); B_skp = Buf()
                    load_T(skp[:, :], hy_skip[l, :].rearrange("(k p) -> k p", p=128), 8, B_skp)
                    names = ("x0", "yc", "z", "sgh", "sgg", "yg")
                    srcs = (x0T_d, ycT_d, zT_d, sghT_d, sggT_d, ygT_d)
                    lt = [sb(ph, "l%s" % n, [128, 8, TT], BF16) for n in names]
                    B_ltn = [Buf() for _ in names]
                    xt = [sb(ph, "oxt%d" % i, [128, 8, TT]) for i in range(2)]; B_xt = [Buf(), Buf()]
                    yh = sb(ph, "yh", [128, 8, TT], BF16); B_yh = Buf()
                    tmpf = sb(ph, "tmpf", [128, 8, TT]); B_tmpf = Buf()
                    mg = sb(ph, "mg", [128, 8, TT], BF16); B_mg = Buf()
                    mgf = [sb(ph, "mgf%d" % i, [128, TT]) for i in range(2)]; B_mgf = [Buf(), Buf()]
                    mgg = [sb(ph, "mgg%d" % i, [128, TT]) for i in range(2)]; B_mgg = [Buf(), Buf()]
                    tiles5 = [(i * 512, 512, 0) for i in range(8)] + ([] if last else [(L, 256, 1)])
                    for ti, (t0, T, col) in enumerate(tiles5):
                        s = ti % 2
                        for n_, src in enumerate(srcs):
                            if n_ == 1 and not do_hy:
                                continue
                            I("sp", "dma_start", out=lt[n_][:, :, :T], in_=src[:, :, t0:t0 + T].rearrange("k p t -> p k t"),
                              reads=[B_scr, B_scr2, B_yc], writes=[B_ltn[n_]], dma=True)
                        I("sp", "dma_start", out=xt[s][:, :, :T], in_=xT_d[:, :, t0:t0 + T].rearrange("k p t -> p k t"),
                          reads=[B_xT], writes=[B_xt[s]], dma=True)
                        x0l, ycl, zl, sghl, sggl, ygl = lt
                        if do_hy:
                            for k in range(8):
                                I("dve", "scalar_tensor_tensor", out=tmpf[:, k, :T], in0=zl[:, k, :T], scalar=skp[:, k:k + 1], in1=ycl[:, k, :T],
                                  op0=ALU.mult, op1=ALU.add, reads=[B_ltn[2], B_ltn[1], B_skp], writes=[B_tmpf])
                            I("dve", "tensor_tensor", out=yh[:, :, :T], in0=tmpf[:, :, :T], in1=x0l[:, :, :T], op=ALU.mult,
                              reads=[B_tmpf, B_ltn[0]], writes=[B_yh])
                        for db in range(8):
                            w_ = db % 2
                            pg_, pgb_ = psum()
                            for k in range(8):
                                I("pe", "matmul", out=pg_[:, :T], lhsT=wb[1][:, k, db * 128:(db + 1) * 128], rhs=ygl[:, k, :T],
                                  start=(k == 0), stop=(k == 7), reads=[B_wb[1], B_ltn[5]], writes=[pgb_], inc=(k == 7))
                            if do_hy:
                                ph_, phb_ = psum()
                                for k in range(8):
                                    I("pe", "matmul", out=ph_[:, :T], lhsT=wb[0][:, k, db * 128:(db + 1) * 128], rhs=yh[:, k, :T],
                                      start=(k == 0), stop=(k == 7), reads=[B_wb[0], B_yh], writes=[phb_], inc=(k == 7))
                                I("dve", "tensor_tensor", out=mgf[w_][:, :T], in0=ph_[:, :T], in1=sghl[:, db, :T], op=ALU.mult,
                                  reads=[phb_, B_ltn[3]], writes=[B_mgf[w_]])
                                I("dve", "tensor_tensor", out=mgg[w_][:, :T], in0=pg_[:, :T], in1=sggl[:, db, :T], op=ALU.mult,
                                  reads=[pgb_, B_ltn[4]], writes=[B_mgg[w_]])
                                I("dve", "tensor_tensor", out=mg[:, db, :T], in0=mgf[w_][:, :T], in1=mgg[w_][:, :T], op=ALU.add,
                                  reads=[B_mgf[w_], B_mgg[w_]], writes=[B_mg])
                            else:
                                I("dve", "tensor_tensor", out=mg[:, db, :T], in0=pg_[:, :T], in1=sggl[:, db, :T], op=ALU.mult,
                                  reads=[pgb_, B_ltn[4]], writes=[B_mg])
                        for db in range(8):
                            pt, pb = psum()
                            for k in range(8):
                                I("pe", "matmul", out=pt[:, :T], lhsT=wb[2][:, k, db * 128:(db + 1) * 128], rhs=mg[:, k, :T],
                                  start=(k == 0), stop=(k == 7), reads=[B_wb[2], B_mg], writes=[pb], inc=(k == 7))
                            I("dve", "scalar_tensor_tensor", out=xt[s][:, db, :T], in0=pt[:, :T], scalar=modT[:, 16 + db, col:col + 1],
                              in1=xt[s][:, db, :T], op0=ALU.mult, op1=ALU.add, reads=[pb, B_mod, B_xt[s]], writes=[B_xt[s]])
                        I("sp", "dma_start", out=xT_d[:, :, t0:t0 + T].rearrange("k p t -> p k t"), in_=xt[s][:, :, :T],
                          reads=[B_xt[s]], writes=[B_xT], dma=True)
                kb.barrier()

            if dbg and l == 0 and do_mix:
                for k in range(8):
                    I("sp", "dma_start", out=dbg_x[k, :, :], in_=xT_d[k, :, :], reads=[B_xT], writes=[Buf()], dma=True)
                kb.barrier()
            if do_mlp:
                with ExitStack() as ph:
                    TT = 512
                    w1s = sb(ph, "w1s", [128, 8, DFF], BF16); B_w1 = [Buf() for _ in range(4)]
                    w2s = sb(ph, "w2s", [128, 32, D], BF16); B_w2 = [Buf() for _ in range(4)]
                    for q in range(4):
                        I("pool", "dma_start", out=w1s[:, :, q * 1024:(q + 1) * 1024],
                          in_=mlp_w1[l, :, q * 1024:(q + 1) * 1024].rearrange("(k p) f -> p k f", p=128), writes=[B_w1[q]], dma=True)
                    for q in range(4):
                        I("pool", "dma_start", out=w2s[:, q * 8:(q + 1) * 8, :],
                          in_=mlp_w2[l, q * 1024:(q + 1) * 1024, :].rearrange("(k p) d -> p k d", p=128), writes=[B_w2[q]], dma=True)
                    xt1 = sb(ph, "mxt", [128, 8, TT]); xt = [xt1, xt1]; B_x1 = Buf(); B_xt = [B_x1, B_x1]
                    h2 = sb(ph, "mh2", [128, 8, TT], BF16); B_h2 = Buf()
                    fT = sb(ph, "mfT", [128, 32, TT], BF16); B_fT = [Buf() for _ in range(32)]
                    rl1 = sb(ph, "mrl", [128, TT]); rl = [rl1, rl1]; B_r1 = Buf(); B_rl = [B_r1, B_r1]
                    scr = {"sq": sb(ph, "msq", [128, 8, TT]), "B_sq": Buf(), "rs": sb(ph, "mrs", [128, TT]), "B_rs": Buf()}
                    tiles6 = [(i * 512, 512, 0) for i in range(8)] + ([] if last else [(L, 256, 1)])
                    for ti, (t0, T, col) in enumerate(tiles6):
                        s = ti % 2
                        I("sp", "dma_start", out=xt[s][:, :, :T], in_=xT_d[:, :, t0:t0 + T].rearrange("k p t -> p k t"),
                          reads=[B_xT], writes=[B_xt[s]], dma=True)
                        norm_tile(ph, "m", xt[s], B_xt[s], T, lambda k, col=col: sc2[:, k, col:col + 1],
                                  lambda k, col=col: modT[:, 24 + k, col:col + 1], h2, B_h2, scr)
                        for fb in range(32):
                            pt, pb = psum()
                            for k in range(8):
                                I("pe", "matmul", out=pt[:, :T], lhsT=w1s[:, k, fb * 128:(fb + 1) * 128], rhs=h2[:, k, :T],
                                  start=(k == 0), stop=(k == 7), reads=[B_w1[fb // 8], B_h2], writes=[pb], inc=(k == 7))
                            r = fb % 2
                            I("act", "activation", out=rl[r][:, :T], in_=pt[:, :T], func=AF.Relu, reads=[pb], writes=[B_rl[r]])
                            I("dve", "tensor_tensor", out=fT[:, fb, :T], in0=rl[r][:, :T], in1=rl[r][:, :T], op=ALU.mult,
                              reads=[B_rl[r]], writes=[B_fT[fb]])
                        for db in range(8):
                            pt, pb = psum()
                            for fk in range(32):
                                I("pe", "matmul", out=pt[:, :T], lhsT=w2s[:, fk, db * 128:(db + 1) * 128], rhs=fT[:, fk, :T],
                                  start=(fk == 0), stop=(fk == 31), reads=[B_w2[fk // 8], B_fT[fk]], writes=[pb], inc=(fk == 31))
                            I("dve", "scalar_tensor_tensor", out=xt[s][:, db, :T], in0=pt[:, :T], scalar=modT[:, 40 + db, col:col + 1],
                              in1=xt[s][:, db, :T], op0=ALU.mult, op1=ALU.add, reads=[pb, B_mod, B_xt[s]], writes=[B_xt[s]])
                        I("sp", "dma_start", out=xT_d[:, :, t0:t0 + T].rearrange("k p t -> p k t"), in_=xt[s][:, :, :T],
                          reads=[B_xt[s]], writes=[B_xT], dma=True)
                kb.barrier()

        with ExitStack() as ph:
            TT = 128
            xt = [sb(ph, "fxt%d" % i, [128, 8, TT]) for i in range(2)]; B_xt = [Buf(), Buf()]
            yn = sb(ph, "fyn", [128, 8, TT]); B_yn = Buf()
            yo = [sb(ph, "fyo%d" % i, [128, D]) for i in range(2)]; B_yo = [Buf(), Buf()]
            scr = {"sq": sb(ph, "fsq", [128, 8, TT]), "B_sq": Buf(), "rs": sb(ph, "frs", [128, TT]), "B_rs": Buf()}
            for ti in range(L // TT):
                s = ti % 2
                t0 = ti * TT
                E("sp", ("dma_start", dict(out=xt[s][:, :, :], in_=xT_d[:, :, t0:t0 + TT].rearrange("k p t -> p k t"))),
                  reads=[B_xT], writes=[B_xt[s]], dma=True)
                B_gvec = B_gfin
                norm_tile(ph, "f", xt[s], B_xt[s], TT, lambda k: gfin[:, k:k + 1], None, yn, B_yn, scr)
                for half in range(2):
                    pt, pb = psum()
                    for j in range(4):
                        k = half * 4 + j
                        E("pe", ("transpose", dict(out=pt[:, j * 128:(j + 1) * 128], in_=yn[:, k, :], identity=ident[:, :])),
                          reads=[B_yn, B_ident], writes=[pb], inc=(j == 3))
                    if half == 0:
                        E("act", ("copy", dict(out=yo[s][:, 0:512], in_=pt[:, :])), reads=[pb], writes=[B_yo[s]])
                    else:
                        E("dve", ("tensor_copy", dict(out=yo[s][:, 512:1024], in_=pt[:, :])), reads=[pb], writes=[B_yo[s]])
                E("sp", ("dma_start", dict(out=y_out[t0:t0 + TT, :], in_=yo[s][:, :])), reads=[B_yo[s]], dma=True)
        kb.barrier()
        kb.replay()
    return nc


def kernel(**inputs):
    consts = host_consts()
    f32 = lambda a: np.ascontiguousarray(np.asarray(a, np.float32))
    shared = {k: f32(inputs[k]) for k in WKEYS}
    nc = build()
    in_maps = []
    for core in range(8):
        b = core % 4
        m = dict(shared)
        m["x"] = f32(inputs["x"][b])
        m["ctx"] = f32(inputs["ctx"][b])
        m["cvec"] = f32(np.stack([np.asarray(inputs["c"])[b], np.asarray(inputs["c_ctx"])]))
        m.update(consts)
        in_maps.append(m)
    res = run_bass_kernel_spmd(nc, in_maps, core_ids=list(range(8)))
    out = np.stack([np.asarray(res.results[b]["y"], np.float32) for b in range(4)])
    return out
```

```python
import math
from contextlib import ExitStack
import numpy as np
import ml_dtypes
import concourse.bass as bass
import concourse.mybir as mybir
from concourse.bass_utils import run_bass_kernel_spmd

F32 = mybir.dt.float32
BF16 = mybir.dt.bfloat16
AF = mybir.ActivationFunctionType
ALU = mybir.AluOpType

D = 1024
L = 4096
LC = 256
NT = L + LC
DEPTH = 4
DFF = 4096
INW = 8224
EPS = 1e-6
PI = math.pi


SAME_ENGINE_ORDERED = ("pe",)


class Buf:
    __slots__ = ("w", "r")

    def __init__(self):
        self.w = None
        self.r = {}


class KB:
    def __init__(self, nc, stack, ndma=8):
        self.nc = nc
        self.engs = {"pe": nc.tensor, "act": nc.scalar, "dve": nc.vector, "pool": nc.gpsimd, "sp": nc.sync}
        self.ops = {e: [] for e in self.engs}
        self.cnt = {e: 0 for e in self.engs}
        self.seen = {e: {} for e in self.engs}
        self.sem = {}
        for e in self.engs:
            self.sem[e] = stack.enter_context(nc.semaphore("s_" + e))
        self.ndma = ndma
        self.dma_i = 0
        self.dma_val = [0] * ndma
        for i in range(ndma):
            self.sem["d%d" % i] = stack.enter_context(nc.semaphore("sd%d" % i))

    def emit(self, eng, fn, reads=(), writes=(), dma=False, inc=True):
        need = {}

        def add(ev):
            if ev is None:
                return
            k, v = ev
            if need.get(k, 0) < v:
                need[k] = v

        for b in reads:
            add(b.w)
        for b in writes:
            add(b.w)
            for k, v in b.r.items():
                add((k, v))
        if dma:
            i = self.dma_i
            self.dma_i = (i + 1) % self.ndma
            key = "d%d" % i
            add((key, self.dma_val[i]))
            self.dma_val[i] += 16
            ev = (key, self.dma_val[i])
            incv = 16
        else:
            if inc:
                self.cnt[eng] += 1
                ev = (eng, self.cnt[eng])
            else:
                ev = (eng, self.cnt[eng] + 1)
            incv = 1
        waits = []
        seen = self.seen[eng]
        for k, v in need.items():
            if v <= 0:
                continue
            if k == eng and eng in SAME_ENGINE_ORDERED:
                continue
            if seen.get(k, 0) < v:
                seen[k] = v
                waits.append((k, v))
        self.ops[eng].append((waits, fn, ev[0] if (dma or inc) else None, incv))
        for b in reads:
            if b.r.get(ev[0], 0) < ev[1]:
                b.r[ev[0]] = ev[1]
        for b in writes:
            b.w = ev
            b.r = {}

    def barrier(self):
        cur = dict(self.cnt)
        for i in range(self.ndma):
            cur["d%d" % i] = self.dma_val[i]
        for e in self.engs:
            waits = []
            seen = self.seen[e]
            for k, v in cur.items():
                if k == e or v <= 0:
                    continue
                if seen.get(k, 0) < v:
                    seen[k] = v
                    waits.append((k, v))
            if waits:
                self.ops[e].append((waits, None, None, 0))

    def replay(self):
        nc = self.nc
        with nc.Block() as block:
            decs = {"sp": block.sync, "act": block.scalar, "dve": block.vector, "pool": block.gpsimd, "pe": block.tensor}
            for name, dec in decs.items():
                def mk(name):
                    def body(e):
                        for waits, fn, semk, incv in self.ops[name]:
                            for k, v in waits:
                                e.wait_ge(self.sem[k], v)
                            if fn is None:
                                continue
                            ins = getattr(e, fn[0])(**fn[1])
                            if semk is not None:
                                ins.then_inc(self.sem[semk], incv)
                    return body
                dec(mk(name))


WKEYS = ("hy_filt_w1", "hy_filt_b1", "hy_filt_w2", "hy_filt_b2", "hy_filt_w3", "hy_filt_freq", "hy_decay", "ada_w", "ada_b", "norm1_g", "norm2_g", "mlp_w1", "mlp_w2", "final_g", "w_in", "hy_conv_w", "hy_conv_b", "hy_skip",
         "gla_gate_w", "gla_gate_b", "gla_norm_g", "w_branch_hy", "w_branch_gla", "w_out")


def bf(a):
    return np.asarray(a, np.float32).astype(ml_dtypes.bfloat16)


def host_consts():
    c = {}
    c["ident"] = np.eye(128, dtype=np.float32)
    c["ones"] = np.ones((128, 128), np.float32)
    s_ = np.arange(128)[:, None]; t_ = np.arange(128)[None, :]
    g = -1.0 / 16.0
    n2 = np.arange(64)[:, None]; k2 = np.arange(65)[None, :]
    th = 2 * np.pi * n2 * k2 / 128.0
    C2, S2 = np.cos(th), np.sin(th)
    c["G4"] = bf(np.stack([np.concatenate([C2, -S2, -S2, C2], 1), np.concatenate([S2, C2, C2, S2], 1),
                           np.concatenate([C2, C2, S2, -S2], 1), np.concatenate([S2, S2, -C2, C2], 1)], 1))
    bands = np.linspace(1e-4, 15, 16, dtype=np.float32)
    zf, tq = [], []
    for ci_, (R_, P1, PK, boff) in enumerate(((64, 128, 128, 64), (4, 8, 36, 32))):
        Lc = 64 * R_
        n1 = np.arange(R_)[:, None]; k1 = np.arange(P1)[None, :]
        a = 2 * np.pi * n1 * k1 / P1
        c["F1_%d" % ci_] = bf(np.concatenate([np.cos(a), -np.sin(a)], 1))
        l1 = np.zeros(PK); valid = np.zeros(PK)
        l1[0:R_] = np.arange(R_); valid[0:R_] = 1
        l1[boff:boff + R_] = np.arange(R_) - R_; valid[boff:boff + R_] = 1
        a = 2 * np.pi * l1[:, None] * k1 / P1
        c["F1h_%d" % ci_] = bf(valid[:, None] * np.concatenate([np.cos(a), -np.sin(a)], 1))
        kk = np.arange(P1)[:, None]
        na = np.arange(R_)[None, :]; nb = (np.arange(R_)[None, :] - 1) % P1
        Ca, Cb = np.cos(2 * np.pi * kk * na / P1), np.cos(2 * np.pi * kk * nb / P1)
        Sa, Sb = np.sin(2 * np.pi * kk * na / P1), np.sin(2 * np.pi * kk * nb / P1)
        c["Ea_%d" % ci_] = bf(np.concatenate([Ca, Cb, Sa, Sb], 1))
        c["Eb_%d" % ci_] = bf(np.concatenate([-Sa, -Sb, Ca, Cb], 1))
        if ci_ == 0:
            ne = (np.arange(R_ + 1)[None, :] - 1) % P1
            Ce, Se = np.cos(2 * np.pi * kk * ne / P1), np.sin(2 * np.pi * kk * ne / P1)
            c["Ec_a"] = bf(np.concatenate([Ce, Se], 1))
            c["Ec_b"] = bf(np.concatenate([-Se, Ce], 1))
        w = np.full((65, 1), 2.0); w[0] = 1.0; w[64] = 1.0
        w = w / (P1 * 128.0)
        kq = np.arange(65)[:, None]; nn = np.arange(64)[None, :]
        tlo = 2 * np.pi * kq * nn / 128.0; thi = 2 * np.pi * kq * (nn + 64) / 128.0
        c["R4_%d" % ci_] = bf(np.stack([w * np.cos(tlo), w * np.cos(thi), -w * np.sin(tlo), -w * np.sin(thi)], 1))
        q = np.arange(Lc)
        if ci_ == 0:
            pos_b = 64 * (R_ - q // 64) - (q % 64)
            pos_b = np.where(pos_b >= Lc, 0, pos_b)
        else:
            pos_b = np.where(q == 0, 0, Lc - q)
        pos = np.concatenate([q, pos_b])
        tl = np.linspace(0.0, 1.0, Lc, dtype=np.float32)
        ang = (np.float32(2.0 * math.pi / Lc) * np.arange(Lc, dtype=np.float32))[:, None] * bands[None, :]
        feat = np.concatenate([tl[:, None], np.cos(ang), -np.sin(ang)], 1).astype(np.float32)
        zf.append(feat[pos].T)
        tq.append(tl[pos][None, :])
    n_ = np.arange(512)[:, None].astype(np.float64); k_ = np.arange(256)[None, :].astype(np.float64)
    fre = np.cos(2 * np.pi * n_ * k_ / 512.0)
    fim = -np.sin(2 * np.pi * n_ * k_ / 512.0)
    fim[:, 0] = (-1.0) ** np.arange(512)
    Fc = np.concatenate([fre, fim], 1)
    c["Fc"] = bf(Fc.reshape(4, 128, 512).transpose(1, 0, 2))
    tt_ = np.arange(256)[None, :].astype(np.float64); kk_ = np.arange(256)[:, None].astype(np.float64)
    ire = 2.0 * np.cos(2 * np.pi * kk_ * tt_ / 512.0) / 512.0
    ire[0, :] = 1.0 / 512.0
    iim = -2.0 * np.sin(2 * np.pi * kk_ * tt_ / 512.0) / 512.0
    iim[0, :] = ((-1.0) ** np.arange(256)) / 512.0
    Fi = np.concatenate([ire, iim], 0)
    c["Finv"] = bf(Fi.reshape(4, 128, 256).transpose(1, 0, 2))
    c["zfeat"] = np.ascontiguousarray(np.concatenate(zf, 1), np.float32)
    tqf = np.concatenate(tq, 1).astype(np.float32)[0]
    thi = tqf.astype(ml_dtypes.bfloat16)
    tlo = (tqf - thi.astype(np.float32)).astype(ml_dtypes.bfloat16)
    c["tq"] = np.ascontiguousarray(np.stack([thi, thi, tlo, tlo]))
    c["umats"] = np.stack([g * (s_ <= t_), g * (s_ >= t_), g * (s_ > t_), g * (s_ < t_),
                           1.0 * (s_ <= t_), 1.0 * (s_ >= t_)]).astype(np.float32)
    return c


def build(nlayers=DEPTH, do_mix=True, do_mlp=True, do_hy=True, dbg=False):
    nc = bass.Bass("TRN2", target_bir_lowering=False)

    def din(name, shape, dt=F32):
        return nc.dram_tensor(name, list(shape), dt, kind="ExternalInput").ap()

    def dscr(name, shape, dt=F32):
        return nc.dram_tensor(name, list(shape), dt, kind="Internal").ap()

    x_in = din("x", [L, D])
    ctx_in = din("ctx", [LC, D])
    cvec = din("cvec", [2, D])
    ada_w = din("ada_w", [DEPTH, D, 6 * D])
    ada_b = din("ada_b", [DEPTH, 6 * D])
    norm1_g = din("norm1_g", [DEPTH, D])
    norm2_g = din("norm2_g", [DEPTH, D])
    mlp_w1 = din("mlp_w1", [DEPTH, D, DFF])
    mlp_w2 = din("mlp_w2", [DEPTH, DFF, D])
    final_g = din("final_g", [D])
    w_in = din("w_in", [DEPTH, D, INW])
    hy_conv_w = din("hy_conv_w", [DEPTH, 3, 3072])
    hy_conv_b = din("hy_conv_b", [DEPTH, 3072])
    hy_skip = din("hy_skip", [DEPTH, D])
    gla_gate_w = din("gla_gate_w", [DEPTH, 2, 16, 512])
    gla_gate_b = din("gla_gate_b", [DEPTH, 2, 512])
    gla_norm_g = din("gla_norm_g", [DEPTH, 256])
    w_branch_hy = din("w_branch_hy", [DEPTH, D, D])
    w_branch_gla = din("w_branch_gla", [DEPTH, D, D])
    w_out = din("w_out", [DEPTH, D, D])
    umats_d = din("umats", [6, 128, 128])
    hy_filt_w1 = din("hy_filt_w1", [DEPTH, 33, 64])
    hy_filt_b1 = din("hy_filt_b1", [DEPTH, 64])
    hy_filt_w2 = din("hy_filt_w2", [DEPTH, 64, 64])
    hy_filt_b2 = din("hy_filt_b2", [DEPTH, 64])
    hy_filt_w3 = din("hy_filt_w3", [DEPTH, 64, 2048])
    hy_filt_freq = din("hy_filt_freq", [DEPTH, 64])
    hy_decay = din("hy_decay", [DEPTH, D])
    zfeat_d = din("zfeat", [33, 8704])
    tq_d = din("tq", [4, 8704], BF16)
    dF1 = [din("F1_0", [64, 256], BF16), din("F1_1", [4, 16], BF16)]
    dF1h = [din("F1h_0", [128, 256], BF16), din("F1h_1", [36, 16], BF16)]
    dEa = [din("Ea_0", [128, 256], BF16), din("Ea_1", [8, 16], BF16)]
    dEb = [din("Eb_0", [128, 256], BF16), din("Eb_1", [8, 16], BF16)]
    dR4 = [din("R4_0", [65, 4, 64], BF16), din("R4_1", [65, 4, 64], BF16)]
    dG4 = din("G4", [64, 4, 260], BF16)
    dEc = [din("Ec_a", [128, 130], BF16), din("Ec_b", [128, 130], BF16)]
    dFc = din("Fc", [128, 4, 512], BF16)
    dFinv = din("Finv", [128, 4, 256], BF16)
    x0T_d = dscr("x0T_d", [8, 128, NT], BF16)
    zT_d = dscr("zT_d", [8, 128, NT], BF16)
    ycT_d = dscr("ycT_d", [8, 128, NT], BF16)
    qT_d = dscr("qT_d", [4, 128, NT], BF16)
    kT_d = dscr("kT_d", [4, 128, NT], BF16)
    ktok_d = dscr("ktok_d", [NT, 512], BF16)
    vtok_d = dscr("vtok_d", [NT, 1024], BF16)
    srT_d = dscr("srT_d", [8, 128, NT], BF16)
    sghT_d = dscr("sghT_d", [8, 128, NT], BF16)
    sggT_d = dscr("sggT_d", [8, 128, NT], BF16)
    ygT_d = dscr("ygT_d", [8, 128, NT], BF16)
    oT_d = dscr("oT_d", [8, 128, NT])
    B_scr = Buf(); B_scr2 = Buf(); B_oT = Buf(); B_yc = Buf()
    if dbg:
        dbg_t = {n: nc.dram_tensor("dbg_" + n, [8, 128, NT], BF16, kind="ExternalOutput").ap() for n in ("z", "yc", "x0", "yg", "sgh", "sgg", "sr")}
        dbg_x = nc.dram_tensor("dbg_x", [8, 128, NT], F32, kind="ExternalOutput").ap()
    ident_d = din("ident", [128, 128])
    ones_d = din("ones", [128, 128])
    y_out = nc.dram_tensor("y", [L, D], F32, kind="ExternalOutput").ap()

    xT_d = dscr("xT_d", [8, 128, NT])
    B_xT = Buf()

    with ExitStack() as top:
        kb = KB(nc, top)
        E = kb.emit

        def I(eng, _op, reads=(), writes=(), dma=False, inc=True, **kw):
            kb.emit(eng, (_op, kw), reads=reads, writes=writes, dma=dma, inc=inc)

        uid = [0]

        def sb(stack, name, shape, dt=F32):
            uid[0] += 1
            return stack.enter_context(nc.sbuf_tensor("sb%d_%s" % (uid[0], name), list(shape), dt))

        ps_t = [top.enter_context(nc.psum_tensor("ps%d" % i, [128, 512], F32)) for i in range(8)]
        ps_b = [Buf() for _ in range(8)]
        ps_i = [0]

        def psum():
            i = ps_i[0]
            ps_i[0] = (i + 1) % 7
            return ps_t[i], ps_b[i]

        ident = sb(top, "ident", [128, 128]); B_ident = Buf()
        ones = sb(top, "ones", [128, 128]); B_ones = Buf()
        E("sp", ("dma_start", dict(out=ident[:, :], in_=ident_d[:, :])), writes=[B_ident], dma=True)
        E("sp", ("dma_start", dict(out=ones[:, :], in_=ones_d[:, :])), writes=[B_ones], dma=True)
        B_U = Buf()
        B_lr = [Buf(), Buf()]
        vstg = [sb(top, "vstg%d" % i, [128, 128]) for i in range(2)]; B_vst = [Buf(), Buf()]
        vsi = [0]

        def load_T(dst, src2d, J, B_dst, view=None):
            i = vsi[0]; vsi[0] = 1 - i
            I("sp", "dma_start", out=vstg[i][0:J, :], in_=src2d, writes=[B_vst[i]], dma=True)
            pt, pb = psum()
            I("pe", "transpose", out=pt[:, 0:J], in_=vstg[i][0:J, :], identity=ident[0:J, 0:J], reads=[B_vst[i], B_ident], writes=[pb])
            src = pt[:, 0:J] if view is None else view(pt[:, 0:J])
            I("dve", "tensor_copy", out=dst, in_=src, reads=[pb], writes=[B_dst])

        scv = sb(top, "scv", [128, 8, 2]); B_scv = Buf()
        craw = sb(top, "craw", [128, 2, 8]); B_craw = Buf()
        load_T(craw[:, :, :], cvec.rearrange("j (k p) -> (j k) p", p=128), 16, B_craw, view=lambda a: a.rearrange("p (j k) -> p j k", j=2))
        for j in range(2):
            E("act", ("activation", dict(out=scv[:, :, j], in_=craw[:, j, :], func=AF.Silu)),
              reads=[B_craw], writes=[B_scv])
        modTs = [sb(top, "modT%d" % i, [128, 48, 2]) for i in range(2)]; Bm = [Buf(), Buf()]
        sc1s = [sb(top, "sc1_%d" % i, [128, 8, 2]) for i in range(2)]; sc2s = [sb(top, "sc2_%d" % i, [128, 8, 2]) for i in range(2)]
        Bs = [Buf(), Buf()]
        gvl = [sb(top, "gvl%d" % i, [128, 2, 8]) for i in range(2)]; Bg = [Buf(), Buf()]
        gfin = sb(top, "gfin", [128, 8]); B_gfin = Buf()
        load_T(gfin[:, :], final_g.rearrange("(k p) -> k p", p=128), 8, B_gfin)
        modT, sc1, sc2, B_mod, B_sc, B_gvec = modTs[0], sc1s[0], sc2s[0], Bm[0], Bs[0], Bg[0]
        prefetched = set()

        def P0_steps(l_, stack, tgt, W=512):
            wa = [sb(stack, "wa%d" % i, [128, 8, W]) for i in range(2)]; B_wa = [Buf(), Buf()]
            abv = sb(stack, "abv", [128, 48]); B_abv = Buf()
            pt, pb = ps_t[7], ps_b[7]

            def pre():
                load_T(abv[:, :], ada_b[l_, :].rearrange("(j p) -> j p", p=128), 48, B_abv)
                load_T(gvl[tgt][:, 0, :], norm1_g[l_, :].rearrange("(k p) -> k p", p=128), 8, Bg[tgt])
                load_T(gvl[tgt][:, 1, :], norm2_g[l_, :].rearrange("(k p) -> k p", p=128), 8, Bg[tgt])

            def dma(g):
                I("sp", "dma_start", out=wa[g % 2][:, :, :], in_=ada_w[l_, :, g * W:(g + 1) * W].rearrange("(k p) n -> p k n", p=128),
                  writes=[B_wa[g % 2]], dma=True)

            def mm(g):
                for jj in range(W // 128):
                    j = g * (W // 128) + jj
                    for k in range(8):
                        I("pe", "matmul", out=pt[:, 2 * j:2 * j + 2], lhsT=wa[g % 2][:, k, jj * 128:(jj + 1) * 128], rhs=scv[:, k, :],
                          start=(k == 0), stop=(k == 7), reads=[B_wa[g % 2], B_scv], writes=[pb], inc=(k == 7))

            def fin():
                I("dve", "tensor_tensor", out=modTs[tgt][:, :, :], in0=pt[:, 0:96].rearrange("p (j c) -> p j c", c=2),
                  in1=abv[:, :].unsqueeze(2).broadcast_to([128, 48, 2]), op=ALU.add, reads=[pb, B_abv], writes=[Bm[tgt]])
                for (dst, gi, mb) in ((sc1s[tgt], 0, 8), (sc2s[tgt], 1, 32)):
                    I("dve", "scalar_tensor_tensor", out=dst[:, :, :], in0=modTs[tgt][:, mb:mb + 8, :], scalar=1.0,
                      in1=gvl[tgt][:, gi, :].unsqueeze(2).broadcast_to([128, 8, 2]), op0=ALU.add, op1=ALU.mult,
                      reads=[Bm[tgt], Bg[tgt]], writes=[Bs[tgt]])
            return pre, dma, mm, fin

        def norm_tile(ph, tag, xt, B_x, T, scale_fn, shift_fn, out, B_out, scr):
            sq, B_sq, rs, B_rs = scr["sq"], scr["B_sq"], scr["rs"], scr["B_rs"]
            E("act", ("activation", dict(out=sq[:, :, :T], in_=xt[:, :, :T], func=AF.Square)),
              reads=[B_x], writes=[B_sq])
            pt, pb = psum()
            for k in range(8):
                E("pe", ("matmul", dict(out=pt[:, :T], lhsT=ones[:, :], rhs=sq[:, k, :T], start=(k == 0), stop=(k == 7))),
                  reads=[B_sq, B_ones], writes=[pb], inc=(k == 7))
            E("dve", ("tensor_scalar", dict(out=rs[:, :T], in0=pt[:, :T], scalar1=1.0 / D, scalar2=EPS,
                                               op0=ALU.mult, op1=ALU.add)), reads=[pb], writes=[B_rs])
            E("act", ("activation", dict(out=rs[:, :T], in_=rs[:, :T], func=AF.Sqrt)), reads=[B_rs], writes=[B_rs])
            E("dve", ("reciprocal", dict(out=rs[:, :T], in_=rs[:, :T])), reads=[B_rs], writes=[B_rs])
            E("dve", ("tensor_tensor", dict(out=sq[:, :, :T], in0=xt[:, :, :T],
                                               in1=rs[:, :T].unsqueeze(1).broadcast_to([128, 8, T]), op=ALU.mult)),
              reads=[B_x, B_rs], writes=[B_sq])
            for k in range(8):
                eng = "act" if k % 2 == 0 else "dve"
                sc = scale_fn(k)
                sh = shift_fn(k) if shift_fn is not None else None
                if eng == "act":
                    if sh is None:
                        E("act", ("activation", dict(out=out[:, k, :T], in_=sq[:, k, :T], func=AF.Identity,
                                                                      scale=sc)), reads=[B_sq, B_sc, B_gvec], writes=[B_out])
                    else:
                        E("act", ("activation", dict(out=out[:, k, :T], in_=sq[:, k, :T], func=AF.Identity,
                                                                             scale=sc, bias=sh)), reads=[B_sq, B_sc, B_gvec, B_mod], writes=[B_out])
                else:
                    if sh is None:
                        E("dve", ("tensor_scalar", dict(out=out[:, k, :T], in0=sq[:, k, :T], scalar1=sc, scalar2=None,
                                                                          op0=ALU.mult)), reads=[B_sq, B_sc, B_gvec], writes=[B_out])
                    else:
                        E("dve", ("tensor_scalar", dict(out=out[:, k, :T], in0=sq[:, k, :T], scalar1=sc, scalar2=sh,
                                                                                 op0=ALU.mult, op1=ALU.add)),
                          reads=[B_sq, B_sc, B_gvec, B_mod], writes=[B_out])

        with ExitStack() as ph:
            xin = [sb(ph, "xin%d" % i, [128, D]) for i in range(2)]; B_xin = [Buf(), Buf()]
            xo = [sb(ph, "xo%d" % i, [128, 8, 128]) for i in range(2)]; B_xo = [Buf(), Buf()]
            for ti in range(NT // 128):
                s = ti % 2
                src = x_in[ti * 128:(ti + 1) * 128, :] if ti < L // 128 else ctx_in[(ti - L // 128) * 128:(ti - L // 128 + 1) * 128, :]
                E("sp", ("dma_start", dict(out=xin[s][:, :], in_=src)), writes=[B_xin[s]], dma=True)
                for half in range(2):
                    pt, pb = psum()
                    for j in range(4):
                        k = half * 4 + j
                        E("pe", ("transpose", dict(out=pt[:, j * 128:(j + 1) * 128], in_=xin[s][:, k * 128:(k + 1) * 128],
                                                                            identity=ident[:, :])),
                          reads=[B_xin[s], B_ident], writes=[pb], inc=(j == 3))
                    eng = "act" if half == 0 else "dve"
                    if eng == "act":
                        E("act", ("copy", dict(out=xo[s][:, half * 4:(half + 1) * 4, :],
                                                                         in_=pt[:, :].rearrange("p (j t) -> p j t", j=4))),
                          reads=[pb], writes=[B_xo[s]])
                    else:
                        E("dve", ("tensor_copy", dict(out=xo[s][:, half * 4:(half + 1) * 4, :],
                                                                                in_=pt[:, :].rearrange("p (j t) -> p j t", j=4))),
                          reads=[pb], writes=[B_xo[s]])
                E("sp", ("dma_start", dict(out=xT_d[:, :, ti * 128:(ti + 1) * 128].rearrange("k p t -> p k t"),
                                                         in_=xo[s][:, :, :])), reads=[B_xo[s]], writes=[B_xT], dma=True)
        kb.barrier()

        for l in range(nlayers):
            last = (l == DEPTH - 1)
            ntok = L if last else NT
            cur = l % 2
            modT, sc1, sc2, B_mod, B_sc, B_gvec = modTs[cur], sc1s[cur], sc2s[cur], Bm[cur], Bs[cur], Bg[cur]
            if l not in prefetched:
                with ExitStack() as ph:
                    pre_, dma_, mm_, fin_ = P0_steps(l, ph, cur)
                    pre_()
                    for g in range(12):
                        dma_(g)
                        mm_(g)
                    fin_()
                kb.barrier()

            if do_mix:
                mixst = ExitStack()
                lrT = [sb(mixst, "lrT%d" % i, [32, NT]) for i in range(2)]
                with ExitStack() as ph:
                    hT = sb(ph, "hT", [128, 8, NT], BF16); B_hT = Buf()
                    xt = [sb(ph, "pxt%d" % i, [128, 8, 512]) for i in range(2)]; B_xt = [Buf(), Buf()]
                    scr = {"sq": sb(ph, "psq", [128, 8, 512]), "B_sq": Buf(), "rs": sb(ph, "prs", [128, 512]), "B_rs": Buf()}
                    tiles = [(i * 512, 512, 0) for i in range(8)] + [(L, 256, 1)]
                    for ti, (t0, T, col) in enumerate(tiles):
                        s = ti % 2
                        I("sp", "dma_start", out=xt[s][:, :, :T], in_=xT_d[:, :, t0:t0 + T].rearrange("k p t -> p k t"),
                          reads=[B_xT], writes=[B_xt[s]], dma=True)
                        norm_tile(ph, "p", xt[s], B_xt[s], T, lambda k, col=col: sc1[:, k, col:col + 1],
                                  lambda k, col=col: modT[:, k, col:col + 1], hT[:, :, t0:t0 + T], B_hT, scr)
                    cw = sb(ph, "cw", [128, 3, 24]); cbias = sb(ph, "cbias", [128, 24]); B_cw = Buf()
                    load_T(cw[:, :, :], hy_conv_w[l, :, :].rearrange("t (j p) -> (t j) p", p=128), 72, B_cw,
                           view=lambda a: a.rearrange("p (t j) -> p t j", t=3))
                    load_T(cbias[:, :], hy_conv_b[l, :].rearrange("(j p) -> j p", p=128), 24, B_cw)
                    I("dve", "memset", ap=lrT[0][:, :], constant=1.0, writes=[B_lr[0]])
                    I("dve", "memset", ap=lrT[1][:, :], constant=1.0, writes=[B_lr[1]])
                    wg = [sb(ph, "wg%d" % i, [128, 8, 512], BF16) for i in range(2)]; B_wg = [Buf(), Buf()]
                    wgi = [0]
                    st = [sb(ph, "st%d" % i, [128, 512], BF16) for i in range(4)]; B_st = [Buf() for _ in range(4)]
                    sti = [0]
                    uu = [sb(ph, "uu%d" % i, [128, 512]) for i in range(3)]; B_uu = [Buf() for _ in range(3)]

                    def load_w(cols):
                        s = wgi[0]; wgi[0] = 1 - s
                        o = 0
                        for (c0, n) in cols:
                            I("pool", "dma_start", out=wg[s][:, :, o:o + n], in_=w_in[l, :, c0:c0 + n].rearrange("(k p) n -> p k n", p=128),
                              writes=[B_wg[s]], dma=True)
                            o += n
                        return s

                    def fm_mm(s, j, t0, T, M=128):
                        pt, pb = psum()
                        for k in range(8):
                            I("pe", "matmul", out=pt[0:M, :T], lhsT=wg[s][:, k, j * 128:j * 128 + M], rhs=hT[:, k, t0:t0 + T],
                              start=(k == 0), stop=(k == 7), reads=[B_wg[s], B_hT], writes=[pb], inc=(k == 7))
                        return pt, pb

                    def stage_out(dst_ap, T):
                        i = sti[0]; sti[0] = (i + 1) % 4
                        return i

                    def fm_family(c0, nblk, dst, act, scale=1.0):
                        for g0 in range(0, nblk, 4):
                            nb = min(4, nblk - g0)
                            s = load_w([(c0 + g0 * 128, nb * 128)])
                            for (t0, T, col) in tiles:
                                for j in range(nb):
                                    pt, pb = fm_mm(s, j, t0, T)
                                    i = sti[0]; sti[0] = (i + 1) % 4
                                    I("act", "activation", out=st[i][:, :T], in_=pt[:, :T], func=act, scale=scale, reads=[pb], writes=[B_st[i]])
                                    I("sp", "dma_start", out=dst[g0 + j, :, t0:t0 + T], in_=st[i][:, :T], reads=[B_st[i]], writes=[B_scr], dma=True)

                    for cb in range(8):
                        s = load_w([(cb * 128, 128), (1024 + cb * 128, 128), (2048 + cb * 128, 128)])
                        for (t0, T, col) in tiles:
                            Wd = 64 if col == 0 else 256
                            R_ = T // Wd
                            pts = [fm_mm(s, j, t0, T) for j in range(3)]
                            for j in range(3):
                                pt, pb = pts[j]
                                blk = j * 8 + cb
                                u3 = uu[j][:, :T].rearrange("p (r w) -> p r w", w=Wd)
                                p3 = pt[:, :T].rearrange("p (r w) -> p r w", w=Wd)
                                I("act", "activation", out=uu[j][:, :T], in_=pt[:, :T], func=AF.Identity, scale=cw[:, 1, blk:blk + 1],
                                  bias=cbias[:, blk:blk + 1], reads=[pb, B_cw], writes=[B_uu[j]])
                                I("dve", "scalar_tensor_tensor", out=u3[:, :, 1:Wd], in0=p3[:, :, 0:Wd - 1], scalar=cw[:, 0, blk:blk + 1],
                                  in1=u3[:, :, 1:Wd], op0=ALU.mult, op1=ALU.add, reads=[pb, B_cw, B_uu[j]], writes=[B_uu[j]])
                                I("dve", "scalar_tensor_tensor", out=u3[:, :, 0:Wd - 1], in0=p3[:, :, 1:Wd], scalar=cw[:, 2, blk:blk + 1],
                                  in1=u3[:, :, 0:Wd - 1], op0=ALU.mult, op1=ALU.add, reads=[pb, B_cw, B_uu[j]], writes=[B_uu[j]])
                            i = sti[0]; sti[0] = (i + 1) % 4
                            I("act", "copy", out=st[i][:, :T], in_=uu[0][:, :T], reads=[B_uu[0]], writes=[B_st[i]])
                            I("sp", "dma_start", out=x0T_d[cb, :, t0:t0 + T], in_=st[i][:, :T], reads=[B_st[i]], writes=[B_scr], dma=True)
                            i = sti[0]; sti[0] = (i + 1) % 4
                            I("dve", "tensor_tensor", out=st[i][:, :T], in0=uu[1][:, :T], in1=uu[2][:, :T], op=ALU.mult,
                              reads=[B_uu[1], B_uu[2]], writes=[B_st[i]])
                            I("sp", "dma_start", out=zT_d[cb, :, t0:t0 + T], in_=st[i][:, :T], reads=[B_st[i]], writes=[B_scr], dma=True)
                    fm_family(3072, 4, qT_d, AF.Copy, scale=128.0 ** -0.5)
                    fm_family(3584, 4, kT_d, AF.Copy)
                    fm_family(5120, 8, srT_d, AF.Silu)
                    fm_family(6176, 8, sghT_d, AF.Sigmoid)
                    fm_family(7200, 8, sggT_d, AF.Sigmoid)
                    s = load_w([(6144, 32)])
                    for (t0, T, col) in tiles:
                        for d_ in range(2):
                            pt, pb = psum()
                            for k in range(8):
                                I("pe", "matmul", out=pt[0:16, :T], lhsT=wg[s][:, k, d_ * 16:d_ * 16 + 16], rhs=hT[:, k, t0:t0 + T],
                                  start=(k == 0), stop=(k == 7), reads=[B_wg[s], B_hT], writes=[pb], inc=(k == 7))
                            I("act", "copy", out=lrT[d_][0:16, t0:t0 + T], in_=pt[0:16, :T], reads=[pb], writes=[B_lr[d_]])
                    for (c0, ncol, dst) in ((3584, 512, ktok_d), (4096, 512, vtok_d), (4608, 512, vtok_d)):
                        s = load_w([(c0, 512)])
                        o0 = 512 if c0 == 4608 else 0
                        for tb in range(NT // 128):
                            pt, pb = psum()
                            for k in range(8):
                                I("pe", "matmul", out=pt[:, :], lhsT=hT[:, k, tb * 128:(tb + 1) * 128], rhs=wg[s][:, k, :],
                                  start=(k == 0), stop=(k == 7), reads=[B_wg[s], B_hT], writes=[pb], inc=(k == 7))
                            i = sti[0]; sti[0] = (i + 1) % 4
                            I("act" if tb % 2 == 0 else "dve", "copy" if tb % 2 == 0 else "tensor_copy", out=st[i][:, :], in_=pt[:, :], reads=[pb], writes=[B_st[i]])
                            I("sp", "dma_start", out=dst[tb * 128:(tb + 1) * 128, o0:o0 + 512], in_=st[i][:, :], reads=[B_st[i]], writes=[B_scr], dma=True)
                kb.barrier()

                with ExitStack() as ph:
                    gwa = sb(ph, "gwa", [32, 2, 512]); B_gwa = Buf()
                    I("sp", "dma_start", out=gwa[0:16, :, :], in_=gla_gate_w[l, :, :, :].rearrange("d r n -> r d n"), writes=[B_gwa], dma=True)
                    I("sp", "dma_start", out=gwa[16:17, :, :], in_=gla_gate_b[l:l + 1, :, :], writes=[B_gwa], dma=True)
                    um = sb(ph, "um", [128, 6, 128])
                    I("sp", "dma_start", out=um[:, :, :], in_=umats_d.rearrange("m p t -> p m t"), writes=[B_U], dma=True)
                    U_f, U_b, Us_f, Us_b, M_f, M_b = [um[:, i, :] for i in range(6)]
                    gng = sb(ph, "gng", [128, 2]); B_gng = Buf()
                    load_T(gng[:, :], gla_norm_g[l, :].rearrange("(j p) -> j p", p=128), 2, B_gng)
                    S = [sb(ph, "S%d" % h, [128, 256]) for h in range(4)]; B_S = [Buf() for _ in range(4)]
                    Sb = [sb(ph, "Sb%d" % h, [128, 256], BF16) for h in range(4)]; B_Sb = [Buf() for _ in range(4)]
                    qTl = [sb(ph, "qTl%d" % i, [128, 4, 512], BF16) for i in range(2)]
                    kTl = [sb(ph, "kTl%d" % i, [128, 4, 512], BF16) for i in range(2)]
                    ktl = [sb(ph, "ktl%d" % i, [128, 4, 512], BF16) for i in range(2)]
                    vtl = [sb(ph, "vtl%d" % i, [128, 4, 1024], BF16) for i in range(2)]
                    B_ld = [Buf(), Buf()]
                    srl = sb(ph, "srl", [128, 8, 512], BF16); B_srl = Buf()
                    ofl = sb(ph, "ofl", [128, 8, 512]); B_ofl = Buf()
                    oTs = sb(ph, "oTs", [128, 8, 512]); B_oTs = Buf()
                    osq = sb(ph, "osq", [128, 8, 512]); B_osq = Buf()
                    ygs = sb(ph, "ygs", [128, 8, 512], BF16); B_ygs = Buf()
                    rsn = sb(ph, "rsn", [128, 512]); B_rsn = Buf()
                    Gt = sb(ph, "Gt", [128, 512]); B_Gt = Buf()
                    e2 = [sb(ph, "e2_%d" % i, [128, 256]) for i in range(8)]; B_e2 = [Buf() for _ in range(8)]
                    en = [sb(ph, "en_%d" % i, [128, 128]) for i in range(8)]; B_en = [Buf() for _ in range(8)]
                    qtl_ = [sb(ph, "qtil%d" % i, [128, 128], BF16) for i in range(8)]; B_qt = [Buf() for _ in range(8)]
                    ktl_ = [sb(ph, "ktil%d" % i, [128, 128], BF16) for i in range(8)]; B_kt = [Buf() for _ in range(8)]
                    kht_ = [sb(ph, "khat%d" % i, [128, 128], BF16) for i in range(8)]; B_kh = [Buf() for _ in range(8)]
                    atm = [sb(ph, "atm%d" % i, [128, 128], BF16) for i in range(8)]; B_atm = [Buf() for _ in range(8)]
                    hs = [0]
                    scs = [(L, 2)] + [(i * 512, 4) for i in range(8)]
                    for dr in range(2):
                        for h in range(4):
                            I("pool", "memset", ap=S[h][:, :], constant=0.0, writes=[B_S[h]])
                            I("pool", "memset", ap=Sb[h][:, :], constant=0.0, writes=[B_Sb[h]])
                        order = scs if dr == 0 else [scs[0]] + scs[:0:-1]
                        Um, Usm, Mm = (U_f, Us_f, M_f) if dr == 0 else (U_b, Us_b, M_b)
                        endcol = 127 if dr == 0 else 0
                        for sci, (t0, nch) in enumerate(order):
                            T = nch * 128
                            b_ = sci % 2
                            I("sp", "dma_start", out=qTl[b_][:, :, :T], in_=qT_d[:, :, t0:t0 + T].rearrange("h p t -> p h t"),
                              reads=[B_scr], writes=[B_ld[b_]], dma=True)
                            I("sp", "dma_start", out=kTl[b_][:, :, :T], in_=kT_d[:, :, t0:t0 + T].rearrange("h p t -> p h t"),
                              reads=[B_scr], writes=[B_ld[b_]], dma=True)
                            I("sp", "dma_start", out=ktl[b_][:, :nch, :], in_=ktok_d[t0:t0 + T, :].rearrange("(c p) n -> p c n", p=128),
                              reads=[B_scr], writes=[B_ld[b_]], dma=True)
                            I("sp", "dma_start", out=vtl[b_][:, :nch, :], in_=vtok_d[t0:t0 + T, :].rearrange("(c p) n -> p c n", p=128),
                              reads=[B_scr], writes=[B_ld[b_]], dma=True)
                            if dr == 1:
                                I("sp", "dma_start", out=srl[:, :, :T], in_=srT_d[:, :, t0:t0 + T].rearrange("k p t -> p k t"),
                                  reads=[B_scr], writes=[B_srl], dma=True)
                                I("sp", "dma_start", out=ofl[:, :, :T], in_=oT_d[:, :, t0:t0 + T].rearrange("k p t -> p k t"),
                                  reads=[B_oT], writes=[B_ofl], dma=True)
                            chunks = list(range(nch)) if dr == 0 else list(range(nch - 1, -1, -1))
                            def stA(cc, gen):
                                tc0 = cc * 128
                                pg, pgb = psum()
                                I("pe", "matmul", out=pg[:, :], lhsT=lrT[dr][0:17, t0 + tc0:t0 + tc0 + 128], rhs=gwa[0:17, dr, :],
                                  start=True, stop=True, reads=[B_lr[dr], B_gwa], writes=[pgb])
                                I("act", "activation", out=Gt[:, :], in_=pg[:, :], func=AF.Exp, scale=-1.0, reads=[pgb], writes=[B_Gt])
                                I("act", "activation", out=Gt[:, :], in_=Gt[:, :], func=AF.Ln, bias=1.0, reads=[B_Gt], writes=[B_Gt])
                                for h in range(4):
                                    w_ = gen * 4 + h
                                    Gh = Gt[:, h * 128:(h + 1) * 128]
                                    p1, p1b = psum()
                                    I("pe", "matmul", out=p1[:, 0:128], lhsT=Gh, rhs=Um[:, :], start=True, stop=True,
                                      reads=[B_Gt, B_U], writes=[p1b], inc=False)
                                    I("pe", "matmul", out=p1[:, 128:256], lhsT=Usm[:, :], rhs=Gh, start=True, stop=True,
                                      reads=[B_Gt, B_U], writes=[p1b])
                                    I("act", "activation", out=e2[w_][:, :], in_=p1[:, 0:256], func=AF.Exp, reads=[p1b], writes=[B_e2[w_]])
                                    I("act", "activation", out=en[w_][:, :], in_=p1[:, 0:128], func=AF.Exp, scale=-1.0, reads=[p1b], writes=[B_en[w_]])
                                    I("dve", "tensor_tensor", out=qtl_[w_][:, :], in0=qTl[b_][:, h, tc0:tc0 + 128], in1=e2[w_][:, 0:128], op=ALU.mult,
                                      reads=[B_ld[b_], B_e2[w_]], writes=[B_qt[w_]])
                                    I("pool", "tensor_tensor", out=ktl_[w_][:, :], in0=kTl[b_][:, h, tc0:tc0 + 128], in1=en[w_][:, :], op=ALU.mult,
                                      reads=[B_ld[b_], B_en[w_]], writes=[B_kt[w_]])
                                    I("pool", "tensor_tensor", out=kht_[w_][:, :], in0=ktl[b_][:, cc, h * 128:(h + 1) * 128], in1=e2[w_][:, 128:256], op=ALU.mult,
                                      reads=[B_ld[b_], B_e2[w_]], writes=[B_kh[w_]])

                            def stB(cc, gen):
                                tc0 = cc * 128
                                for h in range(4):
                                    w_ = gen * 4 + h
                                    p2, p2b = psum()
                                    I("pe", "matmul", out=p2[:, 0:128], lhsT=ktl_[w_][:, :], rhs=qtl_[w_][:, :], start=True, stop=True,
                                      reads=[B_kt[w_], B_qt[w_]], writes=[p2b])
                                    I("dve", "tensor_tensor", out=atm[w_][:, :], in0=p2[:, 0:128], in1=Mm[:, :], op=ALU.mult,
                                      reads=[p2b, B_U], writes=[B_atm[w_]])

                            def stC(cc, gen):
                                tc0 = cc * 128
                                for h in range(4):
                                    w_ = gen * 4 + h
                                    p3, p3b = psum()
                                    for eb in range(2):
                                        I("pe", "matmul", out=p3[:, eb * 128:(eb + 1) * 128], lhsT=vtl[b_][:, cc, h * 256 + eb * 128:h * 256 + (eb + 1) * 128],
                                          rhs=atm[w_][:, :], start=True, stop=False, reads=[B_ld[b_], B_atm[w_]], writes=[p3b], inc=False)
                                        I("pe", "matmul", out=p3[:, eb * 128:(eb + 1) * 128], lhsT=Sb[h][:, eb * 128:(eb + 1) * 128],
                                          rhs=qtl_[w_][:, :], start=False, stop=True, reads=[B_Sb[h], B_qt[w_]], writes=[p3b], inc=(eb == 1))
                                    o3 = p3[:, 0:256].rearrange("p (j t) -> p j t", j=2)
                                    if dr == 0:
                                        I("act", "copy", out=oTs[:, 2 * h:2 * h + 2, tc0:tc0 + 128], in_=o3, reads=[p3b], writes=[B_oTs])
                                    else:
                                        I("dve", "tensor_tensor", out=oTs[:, 2 * h:2 * h + 2, tc0:tc0 + 128], in0=o3, in1=ofl[:, 2 * h:2 * h + 2, tc0:tc0 + 128],
                                          op=ALU.add, reads=[p3b, B_ofl], writes=[B_oTs])
                                    p4, p4b = psum()
                                    I("pe", "matmul", out=p4[:, 0:256], lhsT=kht_[w_][:, :], rhs=vtl[b_][:, cc, h * 256:(h + 1) * 256], start=True, stop=True,
                                      reads=[B_kh[w_], B_ld[b_]], writes=[p4b])
                                    I("dve", "scalar_tensor_tensor", out=S[h][:, :], in0=S[h][:, :], scalar=e2[w_][:, endcol:endcol + 1], in1=p4[:, 0:256],
                                      op0=ALU.mult, op1=ALU.add, reads=[B_S[h], B_e2[w_], p4b], writes=[B_S[h]])
                                    I("act", "copy", out=Sb[h][:, :], in_=S[h][:, :], reads=[B_S[h]], writes=[B_Sb[h]])

                            gens = []
                            for cc in chunks:
                                gens.append(hs[0]); hs[0] = 1 - hs[0]
                            for i_, cc in enumerate(chunks):
                                stA(cc, gens[i_])
                                if i_ > 0:
                                    stC(chunks[i_ - 1], gens[i_ - 1])
                                stB(cc, gens[i_])
                            stC(chunks[-1], gens[-1])
                            if dr == 0:
                                I("sp", "dma_start", out=oT_d[:, :, t0:t0 + T].rearrange("k p t -> p k t"), in_=oTs[:, :, :T],
                                  reads=[B_oTs], writes=[B_oT], dma=True)
                            else:
                                I("act", "activation", out=osq[:, :, :T], in_=oTs[:, :, :T], func=AF.Square, reads=[B_oTs], writes=[B_osq])
                                for h in range(4):
                                    pn, pnb = psum()
                                    for eb in range(2):
                                        I("pe", "matmul", out=pn[:, :T], lhsT=ones[:, :], rhs=osq[:, 2 * h + eb, :T], start=(eb == 0), stop=(eb == 1),
                                          reads=[B_ones, B_osq], writes=[pnb], inc=(eb == 1))
                                    I("dve", "tensor_scalar", out=rsn[:, :T], in0=pn[:, :T], scalar1=1.0 / 256, scalar2=EPS, op0=ALU.mult, op1=ALU.add,
                                      reads=[pnb], writes=[B_rsn])
                                    I("act", "activation", out=rsn[:, :T], in_=rsn[:, :T], func=AF.Sqrt, reads=[B_rsn], writes=[B_rsn])
                                    I("dve", "reciprocal", out=rsn[:, :T], in_=rsn[:, :T], reads=[B_rsn], writes=[B_rsn])
                                    for eb in range(2):
                                        I("dve", "tensor_tensor", out=osq[:, 2 * h + eb, :T], in0=oTs[:, 2 * h + eb, :T], in1=rsn[:, :T], op=ALU.mult,
                                          reads=[B_oTs, B_rsn, B_osq], writes=[B_osq])
                                        I("dve", "scalar_tensor_tensor", out=ygs[:, 2 * h + eb, :T], in0=osq[:, 2 * h + eb, :T], scalar=gng[:, eb:eb + 1],
                                          in1=srl[:, 2 * h + eb, :T], op0=ALU.mult, op1=ALU.mult, reads=[B_osq, B_gng, B_srl], writes=[B_ygs])
                                I("sp", "dma_start", out=ygT_d[:, :, t0:t0 + T].rearrange("k p t -> p k t"), in_=ygs[:, :, :T],
                                  reads=[B_ygs], writes=[B_scr2], dma=True)
                kb.barrier()
                mixst.close()

                if do_hy:
                    with ExitStack() as ph:
                        cfgs = [dict(R=64, P1=128, PK=128, boff=64, Lc=4096, tb=0, ci=0)]
                        if not last:
                            cfgs.append(dict(R=4, P1=8, PK=36, boff=32, Lc=256, tb=L, ci=1))
                        fw1 = sb(ph, "fw1", [33, 64]); fw2 = sb(ph, "fw2", [64, 64]); fpar = sb(ph, "fpar", [64, 5]); B_fp = Buf()
                        I("sp", "dma_start", out=fw1[:, :], in_=hy_filt_w1[l, :, :], writes=[B_fp], dma=True)
                        I("sp", "dma_start", out=fw2[:, :], in_=hy_filt_w2[l, :, :], writes=[B_fp], dma=True)
                        for i_, src in enumerate((hy_filt_freq, hy_filt_b1, hy_filt_b2)):
                            I("sp", "dma_start", out=fpar[:, i_:i_ + 1], in_=src[l, :].rearrange("(p o) -> p o", o=1), writes=[B_fp], dma=True)
                        I("dve", "tensor_tensor", out=fpar[:, 3:4], in0=fpar[:, 0:1], in1=fpar[:, 1:2], op=ALU.mult, reads=[B_fp], writes=[B_fp])
                        I("dve", "tensor_tensor", out=fpar[:, 4:5], in0=fpar[:, 0:1], in1=fpar[:, 2:3], op=ALU.mult, reads=[B_fp], writes=[B_fp])
                        w3s = sb(ph, "w3s", [64, 2048], BF16); B_w3 = Buf()
                        I("pool", "dma_start", out=w3s[:, :], in_=hy_filt_w3[l, :, :], writes=[B_w3], dma=True)
                        nad = sb(ph, "nad", [1, 1024]); B_nad = Buf()
                        I("sp", "dma_start", out=nad[:, :], in_=hy_decay[l:l + 1, :], writes=[B_nad], dma=True)
                        nad2 = sb(ph, "nad2", [1, 1024])
                        I("dve", "tensor_scalar", out=nad2[:, :], in0=nad[:, :], scalar1=-1.0, scalar2=None, op0=ALU.mult,
                          reads=[B_nad], writes=[B_nad])
                        I("dve", "tensor_tensor", out=nad[:, :], in0=nad[:, :], in1=nad2[:, :], op=ALU.min,
                          reads=[B_nad], writes=[B_nad])
                        a2tab = [sb(ph, "a2t0", [64, 8192], BF16), sb(ph, "a2t1", [64, 512], BF16)]; B_a2 = [Buf(), Buf()]
                        tqs = [sb(ph, "tq0", [4, 8192], BF16), sb(ph, "tq1", [4, 512], BF16)]; B_tq = Buf()
                        I("sp", "dma_start", out=tqs[0][:, :], in_=tq_d[0:4, 0:8192], writes=[B_tq], dma=True)
                        I("sp", "dma_start", out=tqs[1][:, :], in_=tq_d[0:4, 8192:8704], writes=[B_tq], dma=True)
                        nhi = sb(ph, "nhi", [1, 1024], BF16); nlo = sb(ph, "nlo", [1, 1024], BF16); nad4 = sb(ph, "nad4", [4, 1024], BF16)
                        B_nad4 = Buf()
                        I("dve", "tensor_copy", out=nhi[:, :], in_=nad[:, :], reads=[B_nad], writes=[B_nad4])
                        I("dve", "tensor_tensor", out=nad2[:, :], in0=nad[:, :], in1=nhi[:, :], op=ALU.subtract, reads=[B_nad, B_nad4], writes=[B_nad4])
                        I("dve", "tensor_copy", out=nlo[:, :], in_=nad2[:, :], reads=[B_nad4], writes=[B_nad4])
                        for r_, src_ in ((0, nhi), (1, nlo), (2, nhi), (3, nlo)):
                            I("sp", "dma_start", out=nad4[r_:r_ + 1, :], in_=src_[:, :], reads=[B_nad4], writes=[B_nad4], dma=True)
                        cst = {}
                        B_cst = Buf()
                        for ci_, (R_, P1_, PK_) in enumerate(((64, 128, 128), (4, 8, 36))):
                            for nm, shp, src in (("F1", [R_, 2 * P1_], dF1[ci_]), ("F1h", [PK_, 2 * P1_], dF1h[ci_]),
                                                 ("Ea", [P1_, 4 * R_], dEa[ci_]), ("Eb", [P1_, 4 * R_], dEb[ci_])):
                                t_ = sb(ph, "%s%d" % (nm, ci_), shp, BF16)
                                I("sp", "dma_start", out=t_[:, :], in_=src[:, :], writes=[B_cst], dma=True)
                                cst[(nm, ci_)] = t_
                            t_ = sb(ph, "R4_%d" % ci_, [65, 4, 64], BF16)
                            I("sp", "dma_start", out=t_[:, :, :], in_=dR4[ci_][:, :, :], writes=[B_cst], dma=True)
                            cst[("R4", ci_)] = t_
                        Eca = sb(ph, "Eca", [128, 130], BF16); Ecb = sb(ph, "Ecb", [128, 130], BF16)
                        I("sp", "dma_start", out=Eca[:, :], in_=dEc[0][:, :], writes=[B_cst], dma=True)
                        I("sp", "dma_start", out=Ecb[:, :], in_=dEc[1][:, :], writes=[B_cst], dma=True)
                        G4 = sb(ph, "G4", [64, 4, 260], BF16)
                        I("sp", "dma_start", out=G4[:, :, :], in_=dG4[:, :, :], writes=[B_cst], dma=True)

                        tgst = ExitStack()
                        zfc = [sb(tgst, "zfc%d" % i, [33, 512]) for i in range(2)]; B_zfc = [Buf(), Buf()]
                        arg = sb(tgst, "arg", [64, 512]); B_arg = Buf()
                        tw = sb(tgst, "tw", [64, 512]); B_tw = Buf()
                        a1 = sb(tgst, "a1", [64, 512]); B_a1 = Buf()

                        def sin_layer(pt, pb, n, fcol, dst, B_dst):
                            I("dve", "tensor_scalar", out=arg[:, :n], in0=pt[0:64, :n], scalar1=fpar[:, 0:1], scalar2=fpar[:, fcol:fcol + 1],
                              op0=ALU.mult, op1=ALU.add, reads=[pb, B_fp], writes=[B_arg])
                            I("dve", "tensor_scalar", out=tw[:, :n], in0=arg[:, :n], scalar1=PI, scalar2=-2 * PI, op0=ALU.is_gt, op1=ALU.mult,
                              reads=[B_arg], writes=[B_tw])
                            I("dve", "tensor_tensor", out=arg[:, :n], in0=arg[:, :n], in1=tw[:, :n], op=ALU.add, reads=[B_arg, B_tw], writes=[B_arg])
                            I("dve", "tensor_scalar", out=tw[:, :n], in0=arg[:, :n], scalar1=-PI, scalar2=2 * PI, op0=ALU.is_lt, op1=ALU.mult,
                              reads=[B_arg], writes=[B_tw])
                            I("dve", "tensor_tensor", out=arg[:, :n], in0=arg[:, :n], in1=tw[:, :n], op=ALU.add, reads=[B_arg, B_tw], writes=[B_arg])
                            I("dve", "tensor_scalar", out=arg[:, :n], in0=arg[:, :n], scalar1=-PI, scalar2=PI, op0=ALU.max, op1=ALU.min,
                              reads=[B_arg], writes=[B_arg])
                            I("act", "activation", out=dst, in_=arg[:, :n], func=AF.Sin, reads=[B_arg], writes=[B_dst])

                        for cf in cfgs:
                            ci_, Lc = cf["ci"], cf["Lc"]
                            zoff = 0 if ci_ == 0 else 8192
                            for c0 in range(0, 2 * Lc, 512):
                                n = min(512, 2 * Lc - c0)
                                s = (c0 // 512) % 2
                                I("sp", "dma_start", out=zfc[s][:, :n], in_=zfeat_d[:, zoff + c0:zoff + c0 + n], writes=[B_zfc[s]], dma=True)
                                pt, pb = psum()
                                I("pe", "matmul", out=pt[0:64, :n], lhsT=fw1[:, :], rhs=zfc[s][:, :n], start=True, stop=True,
                                  reads=[B_fp, B_zfc[s]], writes=[pb])
                                sin_layer(pt, pb, n, 3, a1[:, :n], B_a1)
                                pt, pb = psum()
                                I("pe", "matmul", out=pt[0:64, :n], lhsT=fw2[:, :], rhs=a1[:, :n], start=True, stop=True,
                                  reads=[B_fp, B_a1], writes=[pb])
                                sin_layer(pt, pb, n, 4, a2tab[ci_][:, c0:c0 + n], B_a2[ci_])
                            I("dve", "memset", ap=a2tab[ci_][:, Lc:Lc + 1], constant=0.0, writes=[B_a2[ci_]])
                        kb.barrier()
                        tgst.close()

                        if not last:
                            with ExitStack() as cst_:
                                Fc = sb(cst_, "Fc", [128, 4, 512], BF16); Finv = sb(cst_, "Finv", [128, 4, 256], BF16); B_Fc = Buf()
                                I("sp", "dma_start", out=Fc[:, :, :], in_=dFc[:, :, :], writes=[B_Fc], dma=True)
                                I("sp", "dma_start", out=Finv[:, :, :], in_=dFinv[:, :, :], writes=[B_Fc], dma=True)
                                hct = sb(cst_, "hct", [128, 4, 1024], BF16); B_hct = Buf()
                                Hs = sb(cst_, "Hs", [128, 4, 1024]); B_Hs = Buf()
                                wexc = [sb(cst_, "wexc%d" % i, [128, 512]) for i in range(2)]; B_wexc = [Buf(), Buf()]
                                zcf = sb(cst_, "zcf", [128, 8, 256], BF16); B_zcf = Buf()
                                zcf32 = sb(cst_, "zcf32", [128, 8, 256]); B_zcf32 = Buf()
                                zct = sb(cst_, "zct", [128, 2, 1024], BF16); B_zct = Buf()
                                Yc = sb(cst_, "Yc", [128, 4, 1024], BF16); B_Yc = Buf()
                                yct = sb(cst_, "yct", [128, 2, 1024]); B_yct = Buf()
                                ycf = sb(cst_, "ycf", [128, 8, 256], BF16); B_ycf = Buf()
                                tA = [sb(cst_, "tA%d" % i, [128, 512]) for i in range(2)]; B_tA = [Buf(), Buf()]
                                tB = [sb(cst_, "tB%d" % i, [128, 512]) for i in range(2)]; B_tB = [Buf(), Buf()]
                                I("sp", "dma_start", out=zcf[:, :, :], in_=zT_d[:, :, L:NT].rearrange("k p t -> p k t"), reads=[B_scr], writes=[B_zcf], dma=True)
                                I("act", "copy", out=zcf32[:, :, :], in_=zcf[:, :, :], reads=[B_zcf], writes=[B_zcf32])
                                wi = 0
                                for j in range(4):
                                    half = 0 if j < 2 else 1
                                    for cN in range(2):
                                        csl = slice(cN * 512, (cN + 1) * 512)
                                        psh, pshb = psum()
                                        I("pe", "matmul", out=psh[:, :], lhsT=a2tab[1][:, j * 128:(j + 1) * 128],
                                          rhs=w3s[:, half * 1024 + cN * 512:half * 1024 + (cN + 1) * 512], start=True, stop=True,
                                          reads=[B_a2[1], B_w3], writes=[pshb])
                                        psw, pswb = psum()
                                        I("pe", "matmul", out=psw[:, :], lhsT=tqs[1][0:4, j * 128:(j + 1) * 128], rhs=nad4[0:4, csl], start=True, stop=True,
                                          reads=[B_tq, B_nad4], writes=[pswb])
                                        w_ = wi % 2; wi += 1
                                        I("act", "activation", out=wexc[w_][:, :], in_=psw[:, :], func=AF.Exp, reads=[pswb], writes=[B_wexc[w_]])
                                        I("dve", "tensor_tensor", out=hct[:, j, csl], in0=psh[:, :], in1=wexc[w_][:, :], op=ALU.mult,
                                          reads=[pshb, B_wexc[w_]], writes=[B_hct])
                                for m in range(4):
                                    for cN in range(2):
                                        csl = slice(cN * 512, (cN + 1) * 512)
                                        pt, pb = psum()
                                        for j in range(4):
                                            I("pe", "matmul", out=pt[:, :], lhsT=Fc[:, j, m * 128:(m + 1) * 128], rhs=hct[:, j, csl], start=(j == 0), stop=(j == 3),
                                              reads=[B_Fc, B_hct], writes=[pb], inc=(j == 3))
                                        if (m + cN) % 2 == 0:
                                            I("act", "copy", out=Hs[:, m, csl], in_=pt[:, :], reads=[pb], writes=[B_Hs])
                                        else:
                                            I("dve", "tensor_copy", out=Hs[:, m, csl], in_=pt[:, :], reads=[pb], writes=[B_Hs])
                                for sc in range(2):
                                    for cbh in range(2):
                                        pt, pb = psum()
                                        for jj in range(4):
                                            cb = cbh * 4 + jj
                                            I("pe", "transpose", out=pt[:, jj * 128:(jj + 1) * 128], in_=zcf32[:, cb, sc * 128:(sc + 1) * 128], identity=ident[:, :],
                                              reads=[B_zcf32, B_ident], writes=[pb], inc=(jj == 3))
                                        I("act", "copy", out=zct[:, sc, cbh * 512:(cbh + 1) * 512], in_=pt[:, :], reads=[pb], writes=[B_zct])
                                for cN in range(2):
                                    csl = slice(cN * 512, (cN + 1) * 512)
                                    pz = []
                                    for m in range(4):
                                        pt, pb = psum()
                                        for j in range(2):
                                            I("pe", "matmul", out=pt[:, :], lhsT=Fc[:, j, m * 128:(m + 1) * 128], rhs=zct[:, j, csl], start=(j == 0), stop=(j == 1),
                                              reads=[B_Fc, B_zct], writes=[pb], inc=(j == 1))
                                        pz.append((pt, pb))
                                    for b_ in range(2):
                                        (rp, rb), (ip, ib) = pz[b_], pz[2 + b_]
                                        I("dve", "tensor_tensor", out=tA[0][:, :], in0=rp[:, :], in1=Hs[:, b_, csl], op=ALU.mult, reads=[rb, B_Hs], writes=[B_tA[0]])
                                        I("dve", "tensor_tensor", out=tB[0][:, :], in0=ip[:, :], in1=Hs[:, 2 + b_, csl], op=ALU.mult, reads=[ib, B_Hs], writes=[B_tB[0]])
                                        I("pool", "tensor_tensor", out=Yc[:, b_, csl], in0=tA[0][:, :], in1=tB[0][:, :], op=ALU.subtract,
                                          reads=[B_tA[0], B_tB[0]], writes=[B_Yc])
                                        I("dve", "tensor_tensor", out=tA[1][:, :], in0=rp[:, :], in1=Hs[:, 2 + b_, csl], op=ALU.mult, reads=[rb, B_Hs], writes=[B_tA[1]])
                                        I("dve", "tensor_tensor", out=tB[1][:, :], in0=ip[:, :], in1=Hs[:, b_, csl], op=ALU.mult, reads=[ib, B_Hs], writes=[B_tB[1]])
                                        I("pool", "tensor_tensor", out=Yc[:, 2 + b_, csl], in0=tA[1][:, :], in1=tB[1][:, :], op=ALU.add,
                                          reads=[B_tA[1], B_tB[1]], writes=[B_Yc])
                                    for blk in (0, 2):
                                        pp, ppb = pz[blk]
                                        I("dve", "tensor_tensor", out=Yc[0:1, blk, csl], in0=pp[0:1, :], in1=Hs[0:1, blk, csl], op=ALU.mult,
                                          reads=[ppb, B_Hs, B_Yc], writes=[B_Yc])
                                for tbk in range(2):
                                    for cN in range(2):
                                        csl = slice(cN * 512, (cN + 1) * 512)
                                        pt, pb = psum()
                                        for j in range(4):
                                            I("pe", "matmul", out=pt[:, :], lhsT=Finv[:, j, tbk * 128:(tbk + 1) * 128], rhs=Yc[:, j, csl], start=(j == 0), stop=(j == 3),
                                              reads=[B_Fc, B_Yc], writes=[pb], inc=(j == 3))
                                        I("act", "copy", out=yct[:, tbk, csl], in_=pt[:, :], reads=[pb], writes=[B_yct])
                                for cbh in range(2):
                                    for tbk in range(2):
                                        pt, pb = psum()
                                        for jj in range(4):
                                            cb = cbh * 4 + jj
                                            I("pe", "transpose", out=pt[:, jj * 128:(jj + 1) * 128], in_=yct[:, tbk, cb * 128:(cb + 1) * 128], identity=ident[:, :],
                                              reads=[B_yct, B_ident], writes=[pb], inc=(jj == 3))
                                        I("dve", "tensor_copy", out=ycf[:, cbh * 4:(cbh + 1) * 4, tbk * 128:(tbk + 1) * 128],
                                          in_=pt[:, :].rearrange("p (j t) -> p j t", j=4), reads=[pb], writes=[B_ycf])
                                I("sp", "dma_start", out=ycT_d[:, :, L:NT].rearrange("k p t -> p k t"), in_=ycf[:, :, :], reads=[B_ycf], writes=[B_yc], dma=True)
                            kb.barrier()

                        hq2 = sb(ph, "hq2", [128, 256, 64], BF16); B_hq = Buf()
                        wexp = [sb(ph, "wexp%d" % i, [128, 256]) for i in range(2)]; B_we = [Buf(), Buf()]
                        HAB = sb(ph, "HAB", [128, 32, 260], BF16); B_HAB = [Buf() for _ in range(32)]
                        zin = [sb(ph, "zin%d" % i, [64, 32, 64], BF16) for i in range(2)]; B_zin = [Buf(), Buf()]
                        yout = [sb(ph, "yout%d" % i, [64, 32, 64], BF16) for i in range(2)]; B_yo = [Buf(), Buf()]
                        AhG = sb(ph, "AhG", [64, 32, 256], BF16); B_AhG = [Buf() for _ in range(16)]
                        AsG = sb(ph, "AsG", [64, 32, 256], BF16); B_AsG = [Buf() for _ in range(16)]
                        YsG = sb(ph, "YsG", [128, 32, 130], BF16); B_YsG = [Buf() for _ in range(32)]
                        BsG = sb(ph, "BsG", [65, 32, 130], BF16); B_BsG = [Buf() for _ in range(16)]
                        tt = [sb(ph, "tt%d" % i, [128, 260]) for i in range(4)]; B_tt = [Buf() for _ in range(4)]
                        rot = {"tt": 0, "we": 0, "g": 0, "ev": 0}

                        def nxt(key, n=2):
                            v = rot[key]; rot[key] = (v + 1) % n
                            return v

                        def evac(out, in_, pb, B_out):
                            if nxt("ev") == 0:
                                I("act", "copy", out=out, in_=in_, reads=[pb], writes=[B_out])
                            else:
                                I("dve", "tensor_copy", out=out, in_=in_, reads=[pb], writes=[B_out])

                        cf = cfgs[0]
                        R_, P1, PK, boff, Lc, tb, ci_ = cf["R"], cf["P1"], cf["PK"], cf["boff"], cf["Lc"], cf["tb"], cf["ci"]
                        F1, F1h, Ea, Eb, R4 = [cst[(nm, ci_)] for nm in ("F1", "F1h", "Ea", "Eb", "R4")]
                        pf = None
                        if l + 1 < nlayers:
                            pf = P0_steps(l + 1, ph, (l + 1) % 2, W=256)
                            pf[0]()
                            prefetched.add(l + 1)
                        gidx = 0
                        for cq in range(4):
                            for l2 in range(64):
                                psh, pshb = psum()
                                psw, pswb = psum()
                                for half in range(2):
                                    r0 = 0 if half == 0 else boff
                                    base = half * Lc + l2
                                    cols = slice(base, base + 64 * (R_ - 1) + 1, 64)
                                    I("pe", "matmul", out=psh[r0:r0 + R_, 0:256], lhsT=a2tab[ci_][:, cols],
                                      rhs=w3s[:, half * 1024 + cq * 256:half * 1024 + (cq + 1) * 256], start=True, stop=True,
                                      reads=[B_a2[ci_], B_w3], writes=[pshb])
                                    I("pe", "matmul", out=psw[r0:r0 + R_, 0:256], lhsT=tqs[ci_][0:4, cols], rhs=nad4[0:4, cq * 256:(cq + 1) * 256],
                                      start=True, stop=True, reads=[B_tq, B_nad4], writes=[pswb])
                                w_ = nxt("we")
                                I("act", "activation", out=wexp[w_][:, :], in_=psw[:, 0:256], func=AF.Exp, reads=[pswb], writes=[B_we[w_]])
                                I("dve", "tensor_tensor", out=hq2[:, :, l2], in0=psh[:, 0:256], in1=wexp[w_][:, :],
                                  op=ALU.mult, reads=[pshb, B_we[w_]], writes=[B_hq])
                            for g in range(8):
                                if pf is not None:
                                    if 1 <= gidx <= 24:
                                        pf[2](gidx - 1)
                                    if gidx < 24:
                                        pf[1](gidx)
                                gidx += 1
                                c_lo = cq * 256 + g * 32
                                cb, p0 = c_lo // 128, c_lo % 128
                                gs = nxt("g")
                                for hh in range(2):
                                    I("sp", "dma_start", out=zin[gs][0:R_, hh * 16:(hh + 1) * 16, :],
                                      in_=zT_d[cb, p0 + hh * 16:p0 + (hh + 1) * 16, tb:tb + Lc].rearrange("c (a b) -> a c b", b=64),
                                      reads=[B_scr], writes=[B_zin[gs]], dma=True)
                                for pr in range(16):
                                    pt, pb = psum()
                                    for j in range(2):
                                        I("pe", "matmul", out=pt[0:64, j * 256:(j + 1) * 256], lhsT=hq2[:, g * 32 + pr * 2 + j, :], rhs=F1h[:, :],
                                          start=True, stop=True, reads=[B_hq, B_cst], writes=[pb], inc=(j == 1))
                                    evac(AhG[:, pr * 2:pr * 2 + 2, :], pt[0:64, :].rearrange("p (j w) -> p j w", j=2), pb, B_AhG[pr])
                                for pr in range(16):
                                    pt, pb = psum()
                                    for j in range(2):
                                        I("pe", "matmul", out=pt[0:64, j * 256:(j + 1) * 256], lhsT=zin[gs][:, pr * 2 + j, :], rhs=F1[:, :],
                                          start=True, stop=True, reads=[B_zin[gs], B_cst], writes=[pb], inc=(j == 1))
                                    evac(AsG[:, pr * 2:pr * 2 + 2, :], pt[0:64, :].rearrange("p (j w) -> p j w", j=2), pb, B_AsG[pr])
                                for ch in range(32):
                                    pt2, pb2 = psum()
                                    I("pe", "matmul", out=pt2[:, 0:260], lhsT=AhG[:, ch, 0:128], rhs=G4[:, 2, :], start=True, stop=False,
                                      reads=[B_AhG[ch // 2], B_cst], writes=[pb2], inc=False)
                                    I("pe", "matmul", out=pt2[:, 0:260], lhsT=AhG[:, ch, 128:256], rhs=G4[:, 3, :], start=False, stop=True,
                                      reads=[B_AhG[ch // 2], B_cst], writes=[pb2])
                                    evac(HAB[:, ch, :], pt2[:, 0:260], pb2, B_HAB[ch])
                                for ch in range(32):
                                    pt2, pb2 = psum()
                                    I("pe", "matmul", out=pt2[:, 0:260], lhsT=AsG[:, ch, 0:128], rhs=G4[:, 0, :], start=True, stop=False,
                                      reads=[B_AsG[ch // 2], B_cst], writes=[pb2], inc=False)
                                    I("pe", "matmul", out=pt2[:, 0:260], lhsT=AsG[:, ch, 128:256], rhs=G4[:, 1, :], start=False, stop=True,
                                      reads=[B_AsG[ch // 2], B_cst], writes=[pb2])
                                    t_ = nxt("tt", 4)
                                    I("dve", "tensor_tensor", out=tt[t_][:, :], in0=pt2[:, 0:260], in1=HAB[:, ch, :], op=ALU.mult,
                                      reads=[pb2, B_HAB[ch]], writes=[B_tt[t_]])
                                    I("pool", "tensor_tensor", out=YsG[:, ch, :], in0=tt[t_][:, 0:130], in1=tt[t_][:, 130:260], op=ALU.add,
                                      reads=[B_tt[t_]], writes=[B_YsG[ch]])
                                for pr in range(16):
                                    pt3, pb3 = psum()
                                    for j in range(2):
                                        ch = pr * 2 + j
                                        I("pe", "matmul", out=pt3[0:65, j * 256:j * 256 + 130], lhsT=YsG[:, ch, 0:65], rhs=Eca[:, :], start=True, stop=False,
                                          reads=[B_YsG[ch], B_cst], writes=[pb3], inc=False)
                                        I("pe", "matmul", out=pt3[0:65, j * 256:j * 256 + 130], lhsT=YsG[:, ch, 65:130], rhs=Ecb[:, :], start=False, stop=True,
                                          reads=[B_YsG[ch], B_cst], writes=[pb3], inc=(j == 1))
                                    evac(BsG[:, pr * 2:pr * 2 + 2, :], pt3[0:65, :].rearrange("p (j w) -> p j w", j=2)[:, :, 0:130], pb3, B_BsG[pr])
                                for oc in range(4):
                                    pt4, pb4 = psum()
                                    for jj in range(8):
                                        ch = oc * 8 + jj
                                        for i4, o_ in enumerate((1, 0, 66, 65)):
                                            I("pe", "matmul", out=pt4[0:R_, jj * 64:(jj + 1) * 64], lhsT=BsG[:, ch, o_:o_ + 64], rhs=R4[:, i4, :],
                                              start=(i4 == 0), stop=(i4 == 3), reads=[B_BsG[ch // 2], B_cst], writes=[pb4], inc=(i4 == 3 and jj == 7))
                                    evac(yout[gs][0:R_, oc * 8:(oc + 1) * 8, :], pt4[0:R_, :].rearrange("p (c n) -> p c n", c=8), pb4, B_yo[gs])
                                for hh in range(2):
                                    I("sp", "dma_start", out=ycT_d[cb, p0 + hh * 16:p0 + (hh + 1) * 16, tb:tb + Lc].rearrange("c (a b) -> a c b", b=64),
                                      in_=yout[gs][0:R_, hh * 16:(hh + 1) * 16, :], reads=[B_yo[gs]], writes=[B_yc], dma=True)
                        if pf is not None:
                            pf[3]()
                    kb.barrier()

                if dbg and l == 0:
                    for n, src in (("z", zT_d), ("yc", ycT_d), ("x0", x0T_d), ("yg", ygT_d), ("sgh", sghT_d), ("sgg", sggT_d), ("sr", srT_d)):
                        for k in range(8):
                            I("sp", "dma_start", out=dbg_t[n][k, :, :], in_=src[k, :, :], reads=[B_scr, B_scr2, B_yc], writes=[Buf()], dma=True)
                    kb.barrier()
                with ExitStack() as ph:
                    TT = 512
                    wb = [sb(ph, "wb%d" % i, [128, 8, D], BF16) for i in range(3)]; B_wb = [Buf(), Buf(), Buf()]
                    for i, wsrc in enumerate((w_branch_hy, w_branch_gla, w_out)):
                        I("pool", "dma_start", out=wb[i][:, :, :], in_=wsrc[l, :, :].rearrange("(k p) n -> p k n", p=128), writes=[B_wb[i]], dma=True)
                    skp = sb(ph, "skp", [128, 8]); B_skp = Buf()
                    load_T(skp[:, :], hy_skip[l, :].rearrange("(k p) -> k p", p=128), 8, B_skp)
                    names = ("x0", "yc", "z", "sgh", "sgg", "yg")
                    srcs = (x0T_d, ycT_d, zT_d, sghT_d, sggT_d, ygT_d)
                    lt = [sb(ph, "l%s" % n, [128, 8, TT], BF16) for n in names]
                    B_ltn = [Buf() for _ in names]
                    xt = [sb(ph, "oxt%d" % i, [128, 8, TT]) for i in range(2)]; B_xt = [Buf(), Buf()]
                    yh = sb(ph, "yh", [128, 8, TT], BF16); B_yh = Buf()
                    tmpf = sb(ph, "tmpf", [128, 8, TT]); B_tmpf = Buf()
                    mg = sb(ph, "mg", [128, 8, TT], BF16); B_mg = Buf()
                    mgf = [sb(ph, "mgf%d" % i, [128, TT]) for i in range(2)]; B_mgf = [Buf(), Buf()]
                    mgg = [sb(ph, "mgg%d" % i, [128, TT]) for i in range(2)]; B_mgg = [Buf(), Buf()]
                    tiles5 = [(i * 512, 512, 0) for i in range(8)] + ([] if last else [(L, 256, 1)])
                    for ti, (t0, T, col) in enumerate(tiles5):
                        s = ti % 2
                        for n_, src in enumerate(srcs):
                            if n_ == 1 and not do_hy:
                                continue
                            I("sp", "dma_start", out=lt[n_][:, :, :T], in_=src[:, :, t0:t0 + T].rearrange("k p t -> p k t"),
                              reads=[B_scr, B_scr2, B_yc], writes=[B_ltn[n_]], dma=True)
                        I("sp", "dma_start", out=xt[s][:, :, :T], in_=xT_d[:, :, t0:t0 + T].rearrange("k p t -> p k t"),
                          reads=[B_xT], writes=[B_xt[s]], dma=True)
                        x0l, ycl, zl, sghl, sggl, ygl = lt
                        if do_hy:
                            for k in range(8):
                                I("dve", "scalar_tensor_tensor", out=tmpf[:, k, :T], in0=zl[:, k, :T], scalar=skp[:, k:k + 1], in1=ycl[:, k, :T],
                                  op0=ALU.mult, op1=ALU.add, reads=[B_ltn[2], B_ltn[1], B_skp], writes=[B_tmpf])
                            I("dve", "tensor_tensor", out=yh[:, :, :T], in0=tmpf[:, :, :T], in1=x0l[:, :, :T], op=ALU.mult,
                              reads=[B_tmpf, B_ltn[0]], writes=[B_yh])
                        for db in range(8):
                            w_ = db % 2
                            pg_, pgb_ = psum()
                            for k in range(8):
                                I("pe", "matmul", out=pg_[:, :T], lhsT=wb[1][:, k, db * 128:(db + 1) * 128], rhs=ygl[:, k, :T],
                                  start=(k == 0), stop=(k == 7), reads=[B_wb[1], B_ltn[5]], writes=[pgb_], inc=(k == 7))
                            if do_hy:
                                ph_, phb_ = psum()
                                for k in range(8):
                                    I("pe", "matmul", out=ph_[:, :T], lhsT=wb[0][:, k, db * 128:(db + 1) * 128], rhs=yh[:, k, :T],
                                      start=(k == 0), stop=(k == 7), reads=[B_wb[0], B_yh], writes=[phb_], inc=(k == 7))
                                I("dve", "tensor_tensor", out=mgf[w_][:, :T], in0=ph_[:, :T], in1=sghl[:, db, :T], op=ALU.mult,
                                  reads=[phb_, B_ltn[3]], writes=[B_mgf[w_]])
                                I("dve", "tensor_tensor", out=mgg[w_][:, :T], in0=pg_[:, :T], in1=sggl[:, db, :T], op=ALU.mult,
                                  reads=[pgb_, B_ltn[4]], writes=[B_mgg[w_]])
                                I("dve", "tensor_tensor", out=mg[:, db, :T], in0=mgf[w_][:, :T], in1=mgg[w_][:, :T], op=ALU.add,
                                  reads=[B_mgf[w_], B_mgg[w_]], writes=[B_mg])
                            else:
                                I("dve", "tensor_tensor", out=mg[:, db, :T], in0=pg_[:, :T], in1=sggl[:, db, :T], op=ALU.mult,
                                  reads=[pgb_, B_ltn[4]], writes=[B_mg])
                        for db in range(8):
                            pt, pb = psum()
                            for k in range(8):
                                I("pe", "matmul", out=pt[:, :T], lhsT=wb[2][:, k, db * 128:(db + 1) * 128], rhs=mg[:, k, :T],
                                  start=(k == 0), stop=(k == 7), reads=[B_wb[2], B_mg], writes=[pb], inc=(k == 7))
                            I("dve", "scalar_tensor_tensor", out=xt[s][:, db, :T], in0=pt[:, :T], scalar=modT[:, 16 + db, col:col + 1],
                              in1=xt[s][:, db, :T], op0=ALU.mult, op1=ALU.add, reads=[pb, B_mod, B_xt[s]], writes=[B_xt[s]])
                        I("sp", "dma_start", out=xT_d[:, :, t0:t0 + T].rearrange("k p t -> p k t"), in_=xt[s][:, :, :T],
                          reads=[B_xt[s]], writes=[B_xT], dma=True)
                kb.barrier()

            if dbg and l == 0 and do_mix:
                for k in range(8):
                    I("sp", "dma_start", out=dbg_x[k, :, :], in_=xT_d[k, :, :], reads=[B_xT], writes=[Buf()], dma=True)
                kb.barrier()
            if do_mlp:
                with ExitStack() as ph:
                    TT = 512
                    w1s = sb(ph, "w1s", [128, 8, DFF], BF16); B_w1 = [Buf() for _ in range(4)]
                    w2s = sb(ph, "w2s", [128, 32, D], BF16); B_w2 = [Buf() for _ in range(4)]
                    for q in range(4):
                        I("pool", "dma_start", out=w1s[:, :, q * 1024:(q + 1) * 1024],
                          in_=mlp_w1[l, :, q * 1024:(q + 1) * 1024].rearrange("(k p) f -> p k f", p=128), writes=[B_w1[q]], dma=True)
                    for q in range(4):
                        I("pool", "dma_start", out=w2s[:, q * 8:(q + 1) * 8, :],
                          in_=mlp_w2[l, q * 1024:(q + 1) * 1024, :].rearrange("(k p) d -> p k d", p=128), writes=[B_w2[q]], dma=True)
                    xt1 = sb(ph, "mxt", [128, 8, TT]); xt = [xt1, xt1]; B_x1 = Buf(); B_xt = [B_x1, B_x1]
                    h2 = sb(ph, "mh2", [128, 8, TT], BF16); B_h2 = Buf()
                    fT = sb(ph, "mfT", [128, 32, TT], BF16); B_fT = [Buf() for _ in range(32)]
                    rl1 = sb(ph, "mrl", [128, TT]); B_r1 = Buf()
                    scr = {"sq": sb(ph, "msq", [128, 8, TT]), "B_sq": Buf(), "rs": sb(ph, "mrs", [128, TT]), "B_rs": Buf()}
                    rl = [rl1, scr["rs"]]; B_rl = [B_r1, scr["B_rs"]]
                    tiles6 = [(i * 512, 512, 0) for i in range(8)] + ([] if last else [(L, 256, 1)])
                    for ti, (t0, T, col) in enumerate(tiles6):
                        s = ti % 2
                        I("sp", "dma_start", out=xt[s][:, :, :T], in_=xT_d[:, :, t0:t0 + T].rearrange("k p t -> p k t"),
                          reads=[B_xT], writes=[B_xt[s]], dma=True)
                        norm_tile(ph, "m", xt[s], B_xt[s], T, lambda k, col=col: sc2[:, k, col:col + 1],
                                  lambda k, col=col: modT[:, 24 + k, col:col + 1], h2, B_h2, scr)
                        for fb in range(32):
                            pt, pb = psum()
                            for k in range(8):
                                I("pe", "matmul", out=pt[:, :T], lhsT=w1s[:, k, fb * 128:(fb + 1) * 128], rhs=h2[:, k, :T],
                                  start=(k == 0), stop=(k == 7), reads=[B_w1[fb // 8], B_h2], writes=[pb], inc=(k == 7))
                            r = fb % 2
                            I("act", "activation", out=rl[r][:, :T], in_=pt[:, :T], func=AF.Relu, reads=[pb], writes=[B_rl[r]])
                            I("dve", "tensor_tensor", out=fT[:, fb, :T], in0=rl[r][:, :T], in1=rl[r][:, :T], op=ALU.mult,
                              reads=[B_rl[r]], writes=[B_fT[fb]])
                        for db in range(8):
                            pt, pb = psum()
                            for fk in range(32):
                                I("pe", "matmul", out=pt[:, :T], lhsT=w2s[:, fk, db * 128:(db + 1) * 128], rhs=fT[:, fk, :T],
                                  start=(fk == 0), stop=(fk == 31), reads=[B_w2[fk // 8], B_fT[fk]], writes=[pb], inc=(fk == 31))
                            I("dve", "scalar_tensor_tensor", out=xt[s][:, db, :T], in0=pt[:, :T], scalar=modT[:, 40 + db, col:col + 1],
                              in1=xt[s][:, db, :T], op0=ALU.mult, op1=ALU.add, reads=[pb, B_mod, B_xt[s]], writes=[B_xt[s]])
                        I("sp", "dma_start", out=xT_d[:, :, t0:t0 + T].rearrange("k p t -> p k t"), in_=xt[s][:, :, :T],
                          reads=[B_xt[s]], writes=[B_xT], dma=True)
                kb.barrier()

        with ExitStack() as ph:
            TT = 128
            xt = [sb(ph, "fxt%d" % i, [128, 8, TT]) for i in range(2)]; B_xt = [Buf(), Buf()]
            yn = sb(ph, "fyn", [128, 8, TT]); B_yn = Buf()
            yo = [sb(ph, "fyo%d" % i, [128, D]) for i in range(2)]; B_yo = [Buf(), Buf()]
            scr = {"sq": sb(ph, "fsq", [128, 8, TT]), "B_sq": Buf(), "rs": sb(ph, "frs", [128, TT]), "B_rs": Buf()}
            for ti in range(L // TT):
                s = ti % 2
                t0 = ti * TT
                E("sp", ("dma_start", dict(out=xt[s][:, :, :], in_=xT_d[:, :, t0:t0 + TT].rearrange("k p t -> p k t"))),
                  reads=[B_xT], writes=[B_xt[s]], dma=True)
                B_gvec = B_gfin
                norm_tile(ph, "f", xt[s], B_xt[s], TT, lambda k: gfin[:, k:k + 1], None, yn, B_yn, scr)
                for half in range(2):
                    pt, pb = psum()
                    for j in range(4):
                        k = half * 4 + j
                        E("pe", ("transpose", dict(out=pt[:, j * 128:(j + 1) * 128], in_=yn[:, k, :], identity=ident[:, :])),
                          reads=[B_yn, B_ident], writes=[pb], inc=(j == 3))
                    if half == 0:
                        E("act", ("copy", dict(out=yo[s][:, 0:512], in_=pt[:, :])), reads=[pb], writes=[B_yo[s]])
                    else:
                        E("dve", ("tensor_copy", dict(out=yo[s][:, 512:1024], in_=pt[:, :])), reads=[pb], writes=[B_yo[s]])
                E("sp", ("dma_start", dict(out=y_out[t0:t0 + TT, :], in_=yo[s][:, :])), reads=[B_yo[s]], dma=True)
        kb.barrier()
        kb.replay()
    return nc


def kernel(**inputs):
    consts = host_consts()
    f32 = lambda a: np.ascontiguousarray(np.asarray(a, np.float32))
    shared = {k: f32(inputs[k]) for k in WKEYS}
    nc = build()
    in_maps = []
    for core in range(8):
        b = core % 4
        m = dict(shared)
        m["x"] = f32(inputs["x"][b])
        m["ctx"] = f32(inputs["ctx"][b])
        m["cvec"] = f32(np.stack([np.asarray(inputs["c"])[b], np.asarray(inputs["c_ctx"])]))
        m.update(consts)
        in_maps.append(m)
    res = run_bass_kernel_spmd(nc, in_maps, core_ids=list(range(8)))
    out = np.stack([np.asarray(res.results[b]["y"], np.float32) for b in range(4)])
    return out
```

```python
import math
from contextlib import ExitStack
import numpy as np
import ml_dtypes
import concourse.bass as bass
import concourse.mybir as mybir
from concourse.bass_utils import run_bass_kernel_spmd

F32 = mybir.dt.float32
BF16 = mybir.dt.bfloat16
AF = mybir.ActivationFunctionType
ALU = mybir.AluOpType

D = 1024
L = 4096
LC = 256
NT = L + LC
DEPTH = 4
DFF = 4096
INW = 8224
EPS = 1e-6
PI = math.pi


SAME_ENGINE_ORDERED = ("pe", "act", "dve")


class Buf:
    __slots__ = ("w", "r")

    def __init__(self):
        self.w = None
        self.r = {}


class KB:
    def __init__(self, nc, stack, ndma=8):
        self.nc = nc
        self.engs = {"pe": nc.tensor, "act": nc.scalar, "dve": nc.vector, "pool": nc.gpsimd, "sp": nc.sync}
        self.ops = {e: [] for e in self.engs}
        self.cnt = {e: 0 for e in self.engs}
        self.seen = {e: {} for e in self.engs}
        self.sem = {}
        for e in self.engs:
            self.sem[e] = stack.enter_context(nc.semaphore("s_" + e))
        self.ndma = ndma
        self.dma_i = 0
        self.dma_val = [0] * ndma
        for i in range(ndma):
            self.sem["d%d" % i] = stack.enter_context(nc.semaphore("sd%d" % i))

    def emit(self, eng, fn, reads=(), writes=(), dma=False, inc=True):
        need = {}

        def add(ev):
            if ev is None:
                return
            k, v = ev
            if need.get(k, 0) < v:
                need[k] = v

        for b in reads:
            add(b.w)
        for b in writes:
            add(b.w)
            for k, v in b.r.items():
                add((k, v))
        if dma:
            i = self.dma_i
            self.dma_i = (i + 1) % self.ndma
            key = "d%d" % i
            add((key, self.dma_val[i]))
            self.dma_val[i] += 16
            ev = (key, self.dma_val[i])
            incv = 16
        else:
            if inc:
                self.cnt[eng] += 1
                ev = (eng, self.cnt[eng])
            else:
                ev = (eng, self.cnt[eng] + 1)
            incv = 1
        waits = []
        seen = self.seen[eng]
        for k, v in need.items():
            if v <= 0:
                continue
            if k == eng and eng in SAME_ENGINE_ORDERED:
                continue
            if seen.get(k, 0) < v:
                seen[k] = v
                waits.append((k, v))
        self.ops[eng].append((waits, fn, ev[0] if (dma or inc) else None, incv))
        for b in reads:
            if b.r.get(ev[0], 0) < ev[1]:
                b.r[ev[0]] = ev[1]
        for b in writes:
            b.w = ev
            b.r = {}

    def barrier(self):
        cur = dict(self.cnt)
        for i in range(self.ndma):
            cur["d%d" % i] = self.dma_val[i]
        for e in self.engs:
            waits = []
            seen = self.seen[e]
            for k, v in cur.items():
                if k == e or v <= 0:
                    continue
                if seen.get(k, 0) < v:
                    seen[k] = v
                    waits.append((k, v))
            if waits:
                self.ops[e].append((waits, None, None, 0))

    def replay(self):
        nc = self.nc
        with nc.Block() as block:
            decs = {"sp": block.sync, "act": block.scalar, "dve": block.vector, "pool": block.gpsimd, "pe": block.tensor}
            for name, dec in decs.items():
                def mk(name):
                    def body(e):
                        for waits, fn, semk, incv in self.ops[name]:
                            for k, v in waits:
                                e.wait_ge(self.sem[k], v)
                            if fn is None:
                                continue
                            ins = getattr(e, fn[0])(**fn[1])
                            if semk is not None:
                                ins.then_inc(self.sem[semk], incv)
                    return body
                dec(mk(name))


WKEYS = ("hy_filt_w1", "hy_filt_b1", "hy_filt_w2", "hy_filt_b2", "hy_filt_w3", "hy_filt_freq", "hy_decay", "ada_w", "ada_b", "norm1_g", "norm2_g", "mlp_w1", "mlp_w2", "final_g", "w_in", "hy_conv_w", "hy_conv_b", "hy_skip",
         "gla_gate_w", "gla_gate_b", "gla_norm_g", "w_branch_hy", "w_branch_gla", "w_out")


def bf(a):
    return np.asarray(a, np.float32).astype(ml_dtypes.bfloat16)


def host_consts():
    c = {}
    c["ident"] = np.eye(128, dtype=np.float32)
    c["ones"] = np.ones((128, 128), np.float32)
    s_ = np.arange(128)[:, None]; t_ = np.arange(128)[None, :]
    g = -1.0 / 16.0
    n2 = np.arange(64)[:, None]; k2 = np.arange(65)[None, :]
    th = 2 * np.pi * n2 * k2 / 128.0
    C2, S2 = np.cos(th), np.sin(th)
    c["G4"] = bf(np.stack([np.concatenate([C2, -S2, -S2, C2], 1), np.concatenate([S2, C2, C2, S2], 1),
                           np.concatenate([C2, C2, S2, -S2], 1), np.concatenate([S2, S2, -C2, C2], 1)], 1))
    bands = np.linspace(1e-4, 15, 16, dtype=np.float32)
    zf, tq = [], []
    for ci_, (R_, P1, PK, boff) in enumerate(((64, 128, 128, 64), (4, 8, 36, 32))):
        Lc = 64 * R_
        n1 = np.arange(R_)[:, None]; k1 = np.arange(P1)[None, :]
        a = 2 * np.pi * n1 * k1 / P1
        c["F1_%d" % ci_] = bf(np.concatenate([np.cos(a), -np.sin(a)], 1))
        l1 = np.zeros(PK); valid = np.zeros(PK)
        l1[0:R_] = np.arange(R_); valid[0:R_] = 1
        l1[boff:boff + R_] = np.arange(R_) - R_; valid[boff:boff + R_] = 1
        a = 2 * np.pi * l1[:, None] * k1 / P1
        c["F1h_%d" % ci_] = bf(valid[:, None] * np.concatenate([np.cos(a), -np.sin(a)], 1))
        kk = np.arange(P1)[:, None]
        na = np.arange(R_)[None, :]; nb = (np.arange(R_)[None, :] - 1) % P1
        Ca, Cb = np.cos(2 * np.pi * kk * na / P1), np.cos(2 * np.pi * kk * nb / P1)
        Sa, Sb = np.sin(2 * np.pi * kk * na / P1), np.sin(2 * np.pi * kk * nb / P1)
        c["Ea_%d" % ci_] = bf(np.concatenate([Ca, Cb, Sa, Sb], 1))
        c["Eb_%d" % ci_] = bf(np.concatenate([-Sa, -Sb, Ca, Cb], 1))
        if ci_ == 0:
            ne = (np.arange(R_ + 1)[None, :] - 1) % P1
            Ce, Se = np.cos(2 * np.pi * kk * ne / P1), np.sin(2 * np.pi * kk * ne / P1)
            c["Ec_a"] = bf(np.concatenate([Ce, Se], 1))
            c["Ec_b"] = bf(np.concatenate([-Se, Ce], 1))
        w = np.full((65, 1), 2.0); w[0] = 1.0; w[64] = 1.0
        w = w / (P1 * 128.0)
        kq = np.arange(65)[:, None]; nn = np.arange(64)[None, :]
        tlo = 2 * np.pi * kq * nn / 128.0; thi = 2 * np.pi * kq * (nn + 64) / 128.0
        c["R4_%d" % ci_] = bf(np.stack([w * np.cos(tlo), w * np.cos(thi), -w * np.sin(tlo), -w * np.sin(thi)], 1))
        q = np.arange(Lc)
        if ci_ == 0:
            pos_b = 64 * (R_ - q // 64) - (q % 64)
            pos_b = np.where(pos_b >= Lc, 0, pos_b)
        else:
            pos_b = np.where(q == 0, 0, Lc - q)
        pos = np.concatenate([q, pos_b])
        tl = np.linspace(0.0, 1.0, Lc, dtype=np.float32)
        ang = (np.float32(2.0 * math.pi / Lc) * np.arange(Lc, dtype=np.float32))[:, None] * bands[None, :]
        feat = np.concatenate([tl[:, None], np.cos(ang), -np.sin(ang)], 1).astype(np.float32)
        zf.append(feat[pos].T)
        tq.append(tl[pos][None, :])
    n_ = np.arange(512)[:, None].astype(np.float64); k_ = np.arange(256)[None, :].astype(np.float64)
    fre = np.cos(2 * np.pi * n_ * k_ / 512.0)
    fim = -np.sin(2 * np.pi * n_ * k_ / 512.0)
    fim[:, 0] = (-1.0) ** np.arange(512)
    Fc = np.concatenate([fre, fim], 1)
    c["Fc"] = bf(Fc.reshape(4, 128, 512).transpose(1, 0, 2))
    tt_ = np.arange(256)[None, :].astype(np.float64); kk_ = np.arange(256)[:, None].astype(np.float64)
    ire = 2.0 * np.cos(2 * np.pi * kk_ * tt_ / 512.0) / 512.0
    ire[0, :] = 1.0 / 512.0
    iim = -2.0 * np.sin(2 * np.pi * kk_ * tt_ / 512.0) / 512.0
    iim[0, :] = ((-1.0) ** np.arange(256)) / 512.0
    Fi = np.concatenate([ire, iim], 0)
    c["Finv"] = bf(Fi.reshape(4, 128, 256).transpose(1, 0, 2))
    c["zfeat"] = np.ascontiguousarray(np.concatenate(zf, 1), np.float32)
    tqf = np.concatenate(tq, 1).astype(np.float32)[0]
    thi = tqf.astype(ml_dtypes.bfloat16)
    tlo = (tqf - thi.astype(np.float32)).astype(ml_dtypes.bfloat16)
    c["tq"] = np.ascontiguousarray(np.stack([thi, thi, tlo, tlo]))
    c["umats"] = np.stack([g * (s_ <= t_), g * (s_ >= t_), g * (s_ > t_), g * (s_ < t_),
                           1.0 * (s_ <= t_), 1.0 * (s_ >= t_)]).astype(np.float32)
    return c


def build(nlayers=DEPTH, do_mix=True, do_mlp=True, do_hy=True, dbg=False):
    nc = bass.Bass("TRN2", target_bir_lowering=False)

    def din(name, shape, dt=F32):
        return nc.dram_tensor(name, list(shape), dt, kind="ExternalInput").ap()

    def dscr(name, shape, dt=F32):
        return nc.dram_tensor(name, list(shape), dt, kind="Internal").ap()

    x_in = din("x", [L, D])
    ctx_in = din("ctx", [LC, D])
    cvec = din("cvec", [2, D])
    ada_w = din("ada_w", [DEPTH, D, 6 * D])
    ada_b = din("ada_b", [DEPTH, 6 * D])
    norm1_g = din("norm1_g", [DEPTH, D])
    norm2_g = din("norm2_g", [DEPTH, D])
    mlp_w1 = din("mlp_w1", [DEPTH, D, DFF])
    mlp_w2 = din("mlp_w2", [DEPTH, DFF, D])
    final_g = din("final_g", [D])
    w_in = din("w_in", [DEPTH, D, INW])
    hy_conv_w = din("hy_conv_w", [DEPTH, 3, 3072])
    hy_conv_b = din("hy_conv_b", [DEPTH, 3072])
    hy_skip = din("hy_skip", [DEPTH, D])
    gla_gate_w = din("gla_gate_w", [DEPTH, 2, 16, 512])
    gla_gate_b = din("gla_gate_b", [DEPTH, 2, 512])
    gla_norm_g = din("gla_norm_g", [DEPTH, 256])
    w_branch_hy = din("w_branch_hy", [DEPTH, D, D])
    w_branch_gla = din("w_branch_gla", [DEPTH, D, D])
    w_out = din("w_out", [DEPTH, D, D])
    umats_d = din("umats", [6, 128, 128])
    hy_filt_w1 = din("hy_filt_w1", [DEPTH, 33, 64])
    hy_filt_b1 = din("hy_filt_b1", [DEPTH, 64])
    hy_filt_w2 = din("hy_filt_w2", [DEPTH, 64, 64])
    hy_filt_b2 = din("hy_filt_b2", [DEPTH, 64])
    hy_filt_w3 = din("hy_filt_w3", [DEPTH, 64, 2048])
    hy_filt_freq = din("hy_filt_freq", [DEPTH, 64])
    hy_decay = din("hy_decay", [DEPTH, D])
    zfeat_d = din("zfeat", [33, 8704])
    tq_d = din("tq", [4, 8704], BF16)
    dF1 = [din("F1_0", [64, 256], BF16), din("F1_1", [4, 16], BF16)]
    dF1h = [din("F1h_0", [128, 256], BF16), din("F1h_1", [36, 16], BF16)]
    dEa = [din("Ea_0", [128, 256], BF16), din("Ea_1", [8, 16], BF16)]
    dEb = [din("Eb_0", [128, 256], BF16), din("Eb_1", [8, 16], BF16)]
    dR4 = [din("R4_0", [65, 4, 64], BF16), din("R4_1", [65, 4, 64], BF16)]
    dG4 = din("G4", [64, 4, 260], BF16)
    dEc = [din("Ec_a", [128, 130], BF16), din("Ec_b", [128, 130], BF16)]
    dFc = din("Fc", [128, 4, 512], BF16)
    dFinv = din("Finv", [128, 4, 256], BF16)
    x0T_d = dscr("x0T_d", [8, 128, NT], BF16)
    zT_d = dscr("zT_d", [8, 128, NT], BF16)
    ycT_d = dscr("ycT_d", [8, 128, NT], BF16)
    qT_d = dscr("qT_d", [4, 128, NT], BF16)
    kT_d = dscr("kT_d", [4, 128, NT], BF16)
    ktok_d = dscr("ktok_d", [NT, 512], BF16)
    vtok_d = dscr("vtok_d", [NT, 1024], BF16)
    srT_d = dscr("srT_d", [8, 128, NT], BF16)
    sghT_d = dscr("sghT_d", [8, 128, NT], BF16)
    sggT_d = dscr("sggT_d", [8, 128, NT], BF16)
    ygT_d = dscr("ygT_d", [8, 128, NT], BF16)
    oT_d = dscr("oT_d", [8, 128, NT])
    B_scr = Buf(); B_scr2 = Buf(); B_oT = Buf(); B_yc = Buf()
    if dbg:
        dbg_t = {n: nc.dram_tensor("dbg_" + n, [8, 128, NT], BF16, kind="ExternalOutput").ap() for n in ("z", "yc", "x0", "yg", "sgh", "sgg", "sr")}
        dbg_x = nc.dram_tensor("dbg_x", [8, 128, NT], F32, kind="ExternalOutput").ap()
    ident_d = din("ident", [128, 128])
    ones_d = din("ones", [128, 128])
    y_out = nc.dram_tensor("y", [L, D], F32, kind="ExternalOutput").ap()

    xT_d = dscr("xT_d", [8, 128, NT])
    B_xT = Buf()

    with ExitStack() as top:
        kb = KB(nc, top)
        E = kb.emit

        def I(eng, _op, reads=(), writes=(), dma=False, inc=True, **kw):
            kb.emit(eng, (_op, kw), reads=reads, writes=writes, dma=dma, inc=inc)

        uid = [0]

        def sb(stack, name, shape, dt=F32):
            uid[0] += 1
            return stack.enter_context(nc.sbuf_tensor("sb%d_%s" % (uid[0], name), list(shape), dt))

        ps_t = [top.enter_context(nc.psum_tensor("ps%d" % i, [128, 512], F32)) for i in range(8)]
        ps_b = [Buf() for _ in range(8)]
        ps_i = [0]

        def psum():
            i = ps_i[0]
            ps_i[0] = (i + 1) % 7
            return ps_t[i], ps_b[i]

        ident = sb(top, "ident", [128, 128]); B_ident = Buf()
        ones = sb(top, "ones", [128, 128]); B_ones = Buf()
        E("sp", ("dma_start", dict(out=ident[:, :], in_=ident_d[:, :])), writes=[B_ident], dma=True)
        E("sp", ("dma_start", dict(out=ones[:, :], in_=ones_d[:, :])), writes=[B_ones], dma=True)
        B_U = Buf()
        B_lr = [Buf(), Buf()]
        vstg = [sb(top, "vstg%d" % i, [128, 128]) for i in range(2)]; B_vst = [Buf(), Buf()]
        vsi = [0]

        def load_T(dst, src2d, J, B_dst, view=None):
            i = vsi[0]; vsi[0] = 1 - i
            I("sp", "dma_start", out=vstg[i][0:J, :], in_=src2d, writes=[B_vst[i]], dma=True)
            pt, pb = psum()
            I("pe", "transpose", out=pt[:, 0:J], in_=vstg[i][0:J, :], identity=ident[0:J, 0:J], reads=[B_vst[i], B_ident], writes=[pb])
            src = pt[:, 0:J] if view is None else view(pt[:, 0:J])
            I("dve", "tensor_copy", out=dst, in_=src, reads=[pb], writes=[B_dst])

        scv = sb(top, "scv", [128, 8, 2]); B_scv = Buf()
        craw = sb(top, "craw", [128, 2, 8]); B_craw = Buf()
        load_T(craw[:, :, :], cvec.rearrange("j (k p) -> (j k) p", p=128), 16, B_craw, view=lambda a: a.rearrange("p (j k) -> p j k", j=2))
        for j in range(2):
            E("act", ("activation", dict(out=scv[:, :, j], in_=craw[:, j, :], func=AF.Silu)),
              reads=[B_craw], writes=[B_scv])
        modTs = [sb(top, "modT%d" % i, [128, 48, 2]) for i in range(2)]; Bm = [Buf(), Buf()]
        sc1s = [sb(top, "sc1_%d" % i, [128, 8, 2]) for i in range(2)]; sc2s = [sb(top, "sc2_%d" % i, [128, 8, 2]) for i in range(2)]
        Bs = [Buf(), Buf()]
        gvl = [sb(top, "gvl%d" % i, [128, 2, 8]) for i in range(2)]; Bg = [Buf(), Buf()]
        gfin = sb(top, "gfin", [128, 8]); B_gfin = Buf()
        load_T(gfin[:, :], final_g.rearrange("(k p) -> k p", p=128), 8, B_gfin)
        modT, sc1, sc2, B_mod, B_sc, B_gvec = modTs[0], sc1s[0], sc2s[0], Bm[0], Bs[0], Bg[0]
        prefetched = set()

        def P0_steps(l_, stack, tgt, W=512):
            wa = [sb(stack, "wa%d" % i, [128, 8, W]) for i in range(2)]; B_wa = [Buf(), Buf()]
            abv = sb(stack, "abv", [128, 48]); B_abv = Buf()
            pt, pb = ps_t[7], ps_b[7]

            def pre():
                load_T(abv[:, :], ada_b[l_, :].rearrange("(j p) -> j p", p=128), 48, B_abv)
                load_T(gvl[tgt][:, 0, :], norm1_g[l_, :].rearrange("(k p) -> k p", p=128), 8, Bg[tgt])
                load_T(gvl[tgt][:, 1, :], norm2_g[l_, :].rearrange("(k p) -> k p", p=128), 8, Bg[tgt])

            def dma(g):
                I("sp", "dma_start", out=wa[g % 2][:, :, :], in_=ada_w[l_, :, g * W:(g + 1) * W].rearrange("(k p) n -> p k n", p=128),
                  writes=[B_wa[g % 2]], dma=True)

            def mm(g):
                for jj in range(W // 128):
                    j = g * (W // 128) + jj
                    for k in range(8):
                        I("pe", "matmul", out=pt[:, 2 * j:2 * j + 2], lhsT=wa[g % 2][:, k, jj * 128:(jj + 1) * 128], rhs=scv[:, k, :],
                          start=(k == 0), stop=(k == 7), reads=[B_wa[g % 2], B_scv], writes=[pb], inc=(k == 7))

            def fin():
                I("dve", "tensor_tensor", out=modTs[tgt][:, :, :], in0=pt[:, 0:96].rearrange("p (j c) -> p j c", c=2),
                  in1=abv[:, :].unsqueeze(2).broadcast_to([128, 48, 2]), op=ALU.add, reads=[pb, B_abv], writes=[Bm[tgt]])
                for (dst, gi, mb) in ((sc1s[tgt], 0, 8), (sc2s[tgt], 1, 32)):
                    I("dve", "scalar_tensor_tensor", out=dst[:, :, :], in0=modTs[tgt][:, mb:mb + 8, :], scalar=1.0,
                      in1=gvl[tgt][:, gi, :].unsqueeze(2).broadcast_to([128, 8, 2]), op0=ALU.add, op1=ALU.mult,
                      reads=[Bm[tgt], Bg[tgt]], writes=[Bs[tgt]])
            return pre, dma, mm, fin

        def norm_tile(ph, tag, xt, B_x, T, scale_fn, shift_fn, out, B_out, scr):
            sq, B_sq, rs, B_rs = scr["sq"], scr["B_sq"], scr["rs"], scr["B_rs"]
            E("act", ("activation", dict(out=sq[:, :, :T], in_=xt[:, :, :T], func=AF.Square)),
              reads=[B_x], writes=[B_sq])
            pt, pb = psum()
            for k in range(8):
                E("pe", ("matmul", dict(out=pt[:, :T], lhsT=ones[:, :], rhs=sq[:, k, :T], start=(k == 0), stop=(k == 7))),
                  reads=[B_sq, B_ones], writes=[pb], inc=(k == 7))
            E("dve", ("tensor_scalar", dict(out=rs[:, :T], in0=pt[:, :T], scalar1=1.0 / D, scalar2=EPS,
                                               op0=ALU.mult, op1=ALU.add)), reads=[pb], writes=[B_rs])
            E("act", ("activation", dict(out=rs[:, :T], in_=rs[:, :T], func=AF.Sqrt)), reads=[B_rs], writes=[B_rs])
            E("dve", ("reciprocal", dict(out=rs[:, :T], in_=rs[:, :T])), reads=[B_rs], writes=[B_rs])
            E("dve", ("tensor_tensor", dict(out=sq[:, :, :T], in0=xt[:, :, :T],
                                               in1=rs[:, :T].unsqueeze(1).broadcast_to([128, 8, T]), op=ALU.mult)),
              reads=[B_x, B_rs], writes=[B_sq])
            for k in range(8):
                eng = "act" if k % 2 == 0 else "dve"
                sc = scale_fn(k)
                sh = shift_fn(k) if shift_fn is not None else None
                if eng == "act":
                    if sh is None:
                        E("act", ("activation", dict(out=out[:, k, :T], in_=sq[:, k, :T], func=AF.Identity,
                                                                      scale=sc)), reads=[B_sq, B_sc, B_gvec], writes=[B_out])
                    else:
                        E("act", ("activation", dict(out=out[:, k, :T], in_=sq[:, k, :T], func=AF.Identity,
                                                                             scale=sc, bias=sh)), reads=[B_sq, B_sc, B_gvec, B_mod], writes=[B_out])
                else:
                    if sh is None:
                        E("dve", ("tensor_scalar", dict(out=out[:, k, :T], in0=sq[:, k, :T], scalar1=sc, scalar2=None,
                                                                          op0=ALU.mult)), reads=[B_sq, B_sc, B_gvec], writes=[B_out])
                    else:
                        E("dve", ("tensor_scalar", dict(out=out[:, k, :T], in0=sq[:, k, :T], scalar1=sc, scalar2=sh,
                                                                                 op0=ALU.mult, op1=ALU.add)),
                          reads=[B_sq, B_sc, B_gvec, B_mod], writes=[B_out])

        with ExitStack() as ph:
            xin = [sb(ph, "xin%d" % i, [128, D]) for i in range(2)]; B_xin = [Buf(), Buf()]
            xo = [sb(ph, "xo%d" % i, [128, 8, 128]) for i in range(2)]; B_xo = [Buf(), Buf()]
            for ti in range(NT // 128):
                s = ti % 2
                src = x_in[ti * 128:(ti + 1) * 128, :] if ti < L // 128 else ctx_in[(ti - L // 128) * 128:(ti - L // 128 + 1) * 128, :]
                E("sp", ("dma_start", dict(out=xin[s][:, :], in_=src)), writes=[B_xin[s]], dma=True)
                for half in range(2):
                    pt, pb = psum()
                    for j in range(4):
                        k = half * 4 + j
                        E("pe", ("transpose", dict(out=pt[:, j * 128:(j + 1) * 128], in_=xin[s][:, k * 128:(k + 1) * 128],
                                                                            identity=ident[:, :])),
                          reads=[B_xin[s], B_ident], writes=[pb], inc=(j == 3))
                    eng = "act" if half == 0 else "dve"
                    if eng == "act":
                        E("act", ("copy", dict(out=xo[s][:, half * 4:(half + 1) * 4, :],
                                                                         in_=pt[:, :].rearrange("p (j t) -> p j t", j=4))),
                          reads=[pb], writes=[B_xo[s]])
                    else:
                        E("dve", ("tensor_copy", dict(out=xo[s][:, half * 4:(half + 1) * 4, :],
                                                                                in_=pt[:, :].rearrange("p (j t) -> p j t", j=4))),
                          reads=[pb], writes=[B_xo[s]])
                E("sp", ("dma_start", dict(out=xT_d[:, :, ti * 128:(ti + 1) * 128].rearrange("k p t -> p k t"),
                                                         in_=xo[s][:, :, :])), reads=[B_xo[s]], writes=[B_xT], dma=True)
        kb.barrier()

        for l in range(nlayers):
            last = (l == DEPTH - 1)
            ntok = L if last else NT
            cur = l % 2
            modT, sc1, sc2, B_mod, B_sc, B_gvec = modTs[cur], sc1s[cur], sc2s[cur], Bm[cur], Bs[cur], Bg[cur]
            if l not in prefetched:
                with ExitStack() as ph:
                    pre_, dma_, mm_, fin_ = P0_steps(l, ph, cur)
                    pre_()
                    for g in range(12):
                        dma_(g)
                        mm_(g)
                    fin_()
                kb.barrier()

            if do_mix:
                mixst = ExitStack()
                lrT = [sb(mixst, "lrT%d" % i, [32, NT]) for i in range(2)]
                with ExitStack() as ph:
                    hT = sb(ph, "hT", [128, 8, NT], BF16); B_hT = Buf()
                    xt = [sb(ph, "pxt%d" % i, [128, 8, 512]) for i in range(2)]; B_xt = [Buf(), Buf()]
                    scr = {"sq": sb(ph, "psq", [128, 8, 512]), "B_sq": Buf(), "rs": sb(ph, "prs", [128, 512]), "B_rs": Buf()}
                    tiles = [(i * 512, 512, 0) for i in range(8)] + [(L, 256, 1)]
                    for ti, (t0, T, col) in enumerate(tiles):
                        s = ti % 2
                        I("sp", "dma_start", out=xt[s][:, :, :T], in_=xT_d[:, :, t0:t0 + T].rearrange("k p t -> p k t"),
                          reads=[B_xT], writes=[B_xt[s]], dma=True)
                        norm_tile(ph, "p", xt[s], B_xt[s], T, lambda k, col=col: sc1[:, k, col:col + 1],
                                  lambda k, col=col: modT[:, k, col:col + 1], hT[:, :, t0:t0 + T], B_hT, scr)
                    cw = sb(ph, "cw", [128, 3, 24]); cbias = sb(ph, "cbias", [128, 24]); B_cw = Buf()
                    load_T(cw[:, :, :], hy_conv_w[l, :, :].rearrange("t (j p) -> (t j) p", p=128), 72, B_cw,
                           view=lambda a: a.rearrange("p (t j) -> p t j", t=3))
                    load_T(cbias[:, :], hy_conv_b[l, :].rearrange("(j p) -> j p", p=128), 24, B_cw)
                    I("dve", "memset", ap=lrT[0][:, :], constant=1.0, writes=[B_lr[0]])
                    I("dve", "memset", ap=lrT[1][:, :], constant=1.0, writes=[B_lr[1]])
                    wg = [sb(ph, "wg%d" % i, [128, 8, 512], BF16) for i in range(2)]; B_wg = [Buf(), Buf()]
                    wgi = [0]
                    st = [sb(ph, "st%d" % i, [128, 512], BF16) for i in range(4)]; B_st = [Buf() for _ in range(4)]
                    sti = [0]
                    uu = [sb(ph, "uu%d" % i, [128, 512]) for i in range(3)]; B_uu = [Buf() for _ in range(3)]

                    def load_w(cols):
                        s = wgi[0]; wgi[0] = 1 - s
                        o = 0
                        for (c0, n) in cols:
                            I("pool", "dma_start", out=wg[s][:, :, o:o + n], in_=w_in[l, :, c0:c0 + n].rearrange("(k p) n -> p k n", p=128),
                              writes=[B_wg[s]], dma=True)
                            o += n
                        return s

                    def fm_mm(s, j, t0, T, M=128):
                        pt, pb = psum()
                        for k in range(8):
                            I("pe", "matmul", out=pt[0:M, :T], lhsT=wg[s][:, k, j * 128:j * 128 + M], rhs=hT[:, k, t0:t0 + T],
                              start=(k == 0), stop=(k == 7), reads=[B_wg[s], B_hT], writes=[pb], inc=(k == 7))
                        return pt, pb

                    def stage_out(dst_ap, T):
                        i = sti[0]; sti[0] = (i + 1) % 4
                        return i

                    def fm_family(c0, nblk, dst, act, scale=1.0):
                        for g0 in range(0, nblk, 4):
                            nb = min(4, nblk - g0)
                            s = load_w([(c0 + g0 * 128, nb * 128)])
                            for (t0, T, col) in tiles:
                                for j in range(nb):
                                    pt, pb = fm_mm(s, j, t0, T)
                                    i = sti[0]; sti[0] = (i + 1) % 4
                                    I("act", "activation", out=st[i][:, :T], in_=pt[:, :T], func=act, scale=scale, reads=[pb], writes=[B_st[i]])
                                    I("sp", "dma_start", out=dst[g0 + j, :, t0:t0 + T], in_=st[i][:, :T], reads=[B_st[i]], writes=[B_scr], dma=True)

                    for cb in range(8):
                        s = load_w([(cb * 128, 128), (1024 + cb * 128, 128), (2048 + cb * 128, 128)])
                        for (t0, T, col) in tiles:
                            Wd = 64 if col == 0 else 256
                            R_ = T // Wd
                            pts = [fm_mm(s, j, t0, T) for j in range(3)]
                            for j in range(3):
                                pt, pb = pts[j]
                                blk = j * 8 + cb
                                u3 = uu[j][:, :T].rearrange("p (r w) -> p r w", w=Wd)
                                p3 = pt[:, :T].rearrange("p (r w) -> p r w", w=Wd)
                                I("act", "activation", out=uu[j][:, :T], in_=pt[:, :T], func=AF.Identity, scale=cw[:, 1, blk:blk + 1],
                                  bias=cbias[:, blk:blk + 1], reads=[pb, B_cw], writes=[B_uu[j]])
                                I("dve", "scalar_tensor_tensor", out=u3[:, :, 1:Wd], in0=p3[:, :, 0:Wd - 1], scalar=cw[:, 0, blk:blk + 1],
                                  in1=u3[:, :, 1:Wd], op0=ALU.mult, op1=ALU.add, reads=[pb, B_cw, B_uu[j]], writes=[B_uu[j]])
                                I("dve", "scalar_tensor_tensor", out=u3[:, :, 0:Wd - 1], in0=p3[:, :, 1:Wd], scalar=cw[:, 2, blk:blk + 1],
                                  in1=u3[:, :, 0:Wd - 1], op0=ALU.mult, op1=ALU.add, reads=[pb, B_cw, B_uu[j]], writes=[B_uu[j]])
                            i = sti[0]; sti[0] = (i + 1) % 4
                            I("act", "copy", out=st[i][:, :T], in_=uu[0][:, :T], reads=[B_uu[0]], writes=[B_st[i]])
                            I("sp", "dma_start", out=x0T_d[cb, :, t0:t0 + T], in_=st[i][:, :T], reads=[B_st[i]], writes=[B_scr], dma=True)
                            i = sti[0]; sti[0] = (i + 1) % 4
                            I("dve", "tensor_tensor", out=st[i][:, :T], in0=uu[1][:, :T], in1=uu[2][:, :T], op=ALU.mult,
                              reads=[B_uu[1], B_uu[2]], writes=[B_st[i]])
                            I("sp", "dma_start", out=zT_d[cb, :, t0:t0 + T], in_=st[i][:, :T], reads=[B_st[i]], writes=[B_scr], dma=True)
                    fm_family(3072, 4, qT_d, AF.Copy, scale=128.0 ** -0.5)
                    fm_family(3584, 4, kT_d, AF.Copy)
                    fm_family(5120, 8, srT_d, AF.Silu)
                    fm_family(6176, 8, sghT_d, AF.Sigmoid)
                    fm_family(7200, 8, sggT_d, AF.Sigmoid)
                    s = load_w([(6144, 32)])
                    for (t0, T, col) in tiles:
                        for d_ in range(2):
                            pt, pb = psum()
                            for k in range(8):
                                I("pe", "matmul", out=pt[0:16, :T], lhsT=wg[s][:, k, d_ * 16:d_ * 16 + 16], rhs=hT[:, k, t0:t0 + T],
                                  start=(k == 0), stop=(k == 7), reads=[B_wg[s], B_hT], writes=[pb], inc=(k == 7))
                            I("act", "copy", out=lrT[d_][0:16, t0:t0 + T], in_=pt[0:16, :T], reads=[pb], writes=[B_lr[d_]])
                    for (c0, ncol, dst) in ((3584, 512, ktok_d), (4096, 512, vtok_d), (4608, 512, vtok_d)):
                        s = load_w([(c0, 512)])
                        o0 = 512 if c0 == 4608 else 0
                        for tb in range(NT // 128):
                            pt, pb = psum()
                            for k in range(8):
                                I("pe", "matmul", out=pt[:, :], lhsT=hT[:, k, tb * 128:(tb + 1) * 128], rhs=wg[s][:, k, :],
                                  start=(k == 0), stop=(k == 7), reads=[B_wg[s], B_hT], writes=[pb], inc=(k == 7))
                            i = sti[0]; sti[0] = (i + 1) % 4
                            I("act" if tb % 2 == 0 else "dve", "copy" if tb % 2 == 0 else "tensor_copy", out=st[i][:, :], in_=pt[:, :], reads=[pb], writes=[B_st[i]])
                            I("sp", "dma_start", out=dst[tb * 128:(tb + 1) * 128, o0:o0 + 512], in_=st[i][:, :], reads=[B_st[i]], writes=[B_scr], dma=True)
                kb.barrier()

                with ExitStack() as ph:
                    gwa = sb(ph, "gwa", [32, 2, 512]); B_gwa = Buf()
                    I("sp", "dma_start", out=gwa[0:16, :, :], in_=gla_gate_w[l, :, :, :].rearrange("d r n -> r d n"), writes=[B_gwa], dma=True)
                    I("sp", "dma_start", out=gwa[16:17, :, :], in_=gla_gate_b[l:l + 1, :, :], writes=[B_gwa], dma=True)
                    um = sb(ph, "um", [128, 6, 128])
                    I("sp", "dma_start", out=um[:, :, :], in_=umats_d.rearrange("m p t -> p m t"), writes=[B_U], dma=True)
                    U_f, U_b, Us_f, Us_b, M_f, M_b = [um[:, i, :] for i in range(6)]
                    gng = sb(ph, "gng", [128, 2]); B_gng = Buf()
                    load_T(gng[:, :], gla_norm_g[l, :].rearrange("(j p) -> j p", p=128), 2, B_gng)
                    S = [sb(ph, "S%d" % h, [128, 256]) for h in range(4)]; B_S = [Buf() for _ in range(4)]
                    Sb = [sb(ph, "Sb%d" % h, [128, 256], BF16) for h in range(4)]; B_Sb = [Buf() for _ in range(4)]
                    qTl = [sb(ph, "qTl%d" % i, [128, 4, 512], BF16) for i in range(2)]
                    kTl = [sb(ph, "kTl%d" % i, [128, 4, 512], BF16) for i in range(2)]
                    ktl = [sb(ph, "ktl%d" % i, [128, 4, 512], BF16) for i in range(2)]
                    vtl = [sb(ph, "vtl%d" % i, [128, 4, 1024], BF16) for i in range(2)]
                    B_ld = [Buf(), Buf()]
                    srl = sb(ph, "srl", [128, 8, 512], BF16); B_srl = Buf()
                    ofl = sb(ph, "ofl", [128, 8, 512]); B_ofl = Buf()
                    oTs = sb(ph, "oTs", [128, 8, 512]); B_oTs = Buf()
                    osq = sb(ph, "osq", [128, 8, 512]); B_osq = Buf()
                    ygs = sb(ph, "ygs", [128, 8, 512], BF16); B_ygs = Buf()
                    rsn = sb(ph, "rsn", [128, 512]); B_rsn = Buf()
                    Gt = sb(ph, "Gt", [128, 512]); B_Gt = Buf()
                    e2 = [sb(ph, "e2_%d" % i, [128, 256]) for i in range(8)]; B_e2 = [Buf() for _ in range(8)]
                    en = [sb(ph, "en_%d" % i, [128, 128]) for i in range(8)]; B_en = [Buf() for _ in range(8)]
                    qtl_ = [sb(ph, "qtil%d" % i, [128, 128], BF16) for i in range(8)]; B_qt = [Buf() for _ in range(8)]
                    ktl_ = [sb(ph, "ktil%d" % i, [128, 128], BF16) for i in range(8)]; B_kt = [Buf() for _ in range(8)]
                    kht_ = [sb(ph, "khat%d" % i, [128, 128], BF16) for i in range(8)]; B_kh = [Buf() for _ in range(8)]
                    atm = [sb(ph, "atm%d" % i, [128, 128], BF16) for i in range(8)]; B_atm = [Buf() for _ in range(8)]
                    hs = [0]
                    scs = [(L, 2)] + [(i * 512, 4) for i in range(8)]
                    for dr in range(2):
                        for h in range(4):
                            I("pool", "memset", ap=S[h][:, :], constant=0.0, writes=[B_S[h]])
                            I("pool", "memset", ap=Sb[h][:, :], constant=0.0, writes=[B_Sb[h]])
                        order = scs if dr == 0 else [scs[0]] + scs[:0:-1]
                        Um, Usm, Mm = (U_f, Us_f, M_f) if dr == 0 else (U_b, Us_b, M_b)
                        endcol = 127 if dr == 0 else 0
                        for sci, (t0, nch) in enumerate(order):
                            T = nch * 128
                            b_ = sci % 2
                            I("sp", "dma_start", out=qTl[b_][:, :, :T], in_=qT_d[:, :, t0:t0 + T].rearrange("h p t -> p h t"),
                              reads=[B_scr], writes=[B_ld[b_]], dma=True)
                            I("sp", "dma_start", out=kTl[b_][:, :, :T], in_=kT_d[:, :, t0:t0 + T].rearrange("h p t -> p h t"),
                              reads=[B_scr], writes=[B_ld[b_]], dma=True)
                            I("sp", "dma_start", out=ktl[b_][:, :nch, :], in_=ktok_d[t0:t0 + T, :].rearrange("(c p) n -> p c n", p=128),
                              reads=[B_scr], writes=[B_ld[b_]], dma=True)
                            I("sp", "dma_start", out=vtl[b_][:, :nch, :], in_=vtok_d[t0:t0 + T, :].rearrange("(c p) n -> p c n", p=128),
                              reads=[B_scr], writes=[B_ld[b_]], dma=True)
                            if dr == 1:
                                I("sp", "dma_start", out=srl[:, :, :T], in_=srT_d[:, :, t0:t0 + T].rearrange("k p t -> p k t"),
                                  reads=[B_scr], writes=[B_srl], dma=True)
                                I("sp", "dma_start", out=ofl[:, :, :T], in_=oT_d[:, :, t0:t0 + T].rearrange("k p t -> p k t"),
                                  reads=[B_oT], writes=[B_ofl], dma=True)
                            chunks = list(range(nch)) if dr == 0 else list(range(nch - 1, -1, -1))
                            def stA(cc, gen):
                                tc0 = cc * 128
                                pg, pgb = psum()
                                I("pe", "matmul", out=pg[:, :], lhsT=lrT[dr][0:17, t0 + tc0:t0 + tc0 + 128], rhs=gwa[0:17, dr, :],
                                  start=True, stop=True, reads=[B_lr[dr], B_gwa], writes=[pgb])
                                I("act", "activation", out=Gt[:, :], in_=pg[:, :], func=AF.Exp, scale=-1.0, reads=[pgb], writes=[B_Gt])
                                I("act", "activation", out=Gt[:, :], in_=Gt[:, :], func=AF.Ln, bias=1.0, reads=[B_Gt], writes=[B_Gt])
                                for h in range(4):
                                    w_ = gen * 4 + h
                                    Gh = Gt[:, h * 128:(h + 1) * 128]
                                    p1, p1b = psum()
                                    I("pe", "matmul", out=p1[:, 0:128], lhsT=Gh, rhs=Um[:, :], start=True, stop=True,
                                      reads=[B_Gt, B_U], writes=[p1b], inc=False)
                                    I("pe", "matmul", out=p1[:, 128:256], lhsT=Usm[:, :], rhs=Gh, start=True, stop=True,
                                      reads=[B_Gt, B_U], writes=[p1b])
                                    I("act", "activation", out=e2[w_][:, :], in_=p1[:, 0:256], func=AF.Exp, reads=[p1b], writes=[B_e2[w_]])
                                    I("act", "activation", out=en[w_][:, :], in_=p1[:, 0:128], func=AF.Exp, scale=-1.0, reads=[p1b], writes=[B_en[w_]])
                                    I("dve", "tensor_tensor", out=qtl_[w_][:, :], in0=qTl[b_][:, h, tc0:tc0 + 128], in1=e2[w_][:, 0:128], op=ALU.mult,
                                      reads=[B_ld[b_], B_e2[w_]], writes=[B_qt[w_]])
                                    I("pool", "tensor_tensor", out=ktl_[w_][:, :], in0=kTl[b_][:, h, tc0:tc0 + 128], in1=en[w_][:, :], op=ALU.mult,
                                      reads=[B_ld[b_], B_en[w_]], writes=[B_kt[w_]])
                                    I("pool", "tensor_tensor", out=kht_[w_][:, :], in0=ktl[b_][:, cc, h * 128:(h + 1) * 128], in1=e2[w_][:, 128:256], op=ALU.mult,
                                      reads=[B_ld[b_], B_e2[w_]], writes=[B_kh[w_]])

                            def stB(cc, gen):
                                tc0 = cc * 128
                                for h in range(4):
                                    w_ = gen * 4 + h
                                    p2, p2b = psum()
                                    I("pe", "matmul", out=p2[:, 0:128], lhsT=ktl_[w_][:, :], rhs=qtl_[w_][:, :], start=True, stop=True,
                                      reads=[B_kt[w_], B_qt[w_]], writes=[p2b])
                                    I("dve", "tensor_tensor", out=atm[w_][:, :], in0=p2[:, 0:128], in1=Mm[:, :], op=ALU.mult,
                                      reads=[p2b, B_U], writes=[B_atm[w_]])

                            def stC(cc, gen):
                                tc0 = cc * 128
                                for h in range(4):
                                    w_ = gen * 4 + h
                                    p3, p3b = psum()
                                    for eb in range(2):
                                        I("pe", "matmul", out=p3[:, eb * 128:(eb + 1) * 128], lhsT=vtl[b_][:, cc, h * 256 + eb * 128:h * 256 + (eb + 1) * 128],
                                          rhs=atm[w_][:, :], start=True, stop=False, reads=[B_ld[b_], B_atm[w_]], writes=[p3b], inc=False)
                                        I("pe", "matmul", out=p3[:, eb * 128:(eb + 1) * 128], lhsT=Sb[h][:, eb * 128:(eb + 1) * 128],
                                          rhs=qtl_[w_][:, :], start=False, stop=True, reads=[B_Sb[h], B_qt[w_]], writes=[p3b], inc=(eb == 1))
                                    o3 = p3[:, 0:256].rearrange("p (j t) -> p j t", j=2)
                                    if dr == 0:
                                        I("act", "copy", out=oTs[:, 2 * h:2 * h + 2, tc0:tc0 + 128], in_=o3, reads=[p3b], writes=[B_oTs])
                                    else:
                                        I("dve", "tensor_tensor", out=oTs[:, 2 * h:2 * h + 2, tc0:tc0 + 128], in0=o3, in1=ofl[:, 2 * h:2 * h + 2, tc0:tc0 + 128],
                                          op=ALU.add, reads=[p3b, B_ofl], writes=[B_oTs])
                                    p4, p4b = psum()
                                    I("pe", "matmul", out=p4[:, 0:256], lhsT=kht_[w_][:, :], rhs=vtl[b_][:, cc, h * 256:(h + 1) * 256], start=True, stop=True,
                                      reads=[B_kh[w_], B_ld[b_]], writes=[p4b])
                                    I("dve", "scalar_tensor_tensor", out=S[h][:, :], in0=S[h][:, :], scalar=e2[w_][:, endcol:endcol + 1], in1=p4[:, 0:256],
                                      op0=ALU.mult, op1=ALU.add, reads=[B_S[h], B_e2[w_], p4b], writes=[B_S[h]])
                                    I("act", "copy", out=Sb[h][:, :], in_=S[h][:, :], reads=[B_S[h]], writes=[B_Sb[h]])

                            gens = []
                            for cc in chunks:
                                gens.append(hs[0]); hs[0] = 1 - hs[0]
                            for i_, cc in enumerate(chunks):
                                stA(cc, gens[i_])
                                if i_ > 0:
                                    stC(chunks[i_ - 1], gens[i_ - 1])
                                stB(cc, gens[i_])
                            stC(chunks[-1], gens[-1])
                            if dr == 0:
                                I("sp", "dma_start", out=oT_d[:, :, t0:t0 + T].rearrange("k p t -> p k t"), in_=oTs[:, :, :T],
                                  reads=[B_oTs], writes=[B_oT], dma=True)
                            else:
                                I("act", "activation", out=osq[:, :, :T], in_=oTs[:, :, :T], func=AF.Square, reads=[B_oTs], writes=[B_osq])
                                for h in range(4):
                                    pn, pnb = psum()
                                    for eb in range(2):
                                        I("pe", "matmul", out=pn[:, :T], lhsT=ones[:, :], rhs=osq[:, 2 * h + eb, :T], start=(eb == 0), stop=(eb == 1),
                                          reads=[B_ones, B_osq], writes=[pnb], inc=(eb == 1))
                                    I("dve", "tensor_scalar", out=rsn[:, :T], in0=pn[:, :T], scalar1=1.0 / 256, scalar2=EPS, op0=ALU.mult, op1=ALU.add,
                                      reads=[pnb], writes=[B_rsn])
                                    I("act", "activation", out=rsn[:, :T], in_=rsn[:, :T], func=AF.Sqrt, reads=[B_rsn], writes=[B_rsn])
                                    I("dve", "reciprocal", out=rsn[:, :T], in_=rsn[:, :T], reads=[B_rsn], writes=[B_rsn])
                                    for eb in range(2):
                                        I("dve", "tensor_tensor", out=osq[:, 2 * h + eb, :T], in0=oTs[:, 2 * h + eb, :T], in1=rsn[:, :T], op=ALU.mult,
                                          reads=[B_oTs, B_rsn, B_osq], writes=[B_osq])
                                        I("dve", "scalar_tensor_tensor", out=ygs[:, 2 * h + eb, :T], in0=osq[:, 2 * h + eb, :T], scalar=gng[:, eb:eb + 1],
                                          in1=srl[:, 2 * h + eb, :T], op0=ALU.mult, op1=ALU.mult, reads=[B_osq, B_gng, B_srl], writes=[B_ygs])
                                I("sp", "dma_start", out=ygT_d[:, :, t0:t0 + T].rearrange("k p t -> p k t"), in_=ygs[:, :, :T],
                                  reads=[B_ygs], writes=[B_scr2], dma=True)
                kb.barrier()
                mixst.close()

                if do_hy:
                    with ExitStack() as ph:
                        cfgs = [dict(R=64, P1=128, PK=128, boff=64, Lc=4096, tb=0, ci=0)]
                        if not last:
                            cfgs.append(dict(R=4, P1=8, PK=36, boff=32, Lc=256, tb=L, ci=1))
                        fw1 = sb(ph, "fw1", [33, 64]); fw2 = sb(ph, "fw2", [64, 64]); fpar = sb(ph, "fpar", [64, 5]); B_fp = Buf()
                        I("sp", "dma_start", out=fw1[:, :], in_=hy_filt_w1[l, :, :], writes=[B_fp], dma=True)
                        I("sp", "dma_start", out=fw2[:, :], in_=hy_filt_w2[l, :, :], writes=[B_fp], dma=True)
                        for i_, src in enumerate((hy_filt_freq, hy_filt_b1, hy_filt_b2)):
                            I("sp", "dma_start", out=fpar[:, i_:i_ + 1], in_=src[l, :].rearrange("(p o) -> p o", o=1), writes=[B_fp], dma=True)
                        I("dve", "tensor_tensor", out=fpar[:, 3:4], in0=fpar[:, 0:1], in1=fpar[:, 1:2], op=ALU.mult, reads=[B_fp], writes=[B_fp])
                        I("dve", "tensor_tensor", out=fpar[:, 4:5], in0=fpar[:, 0:1], in1=fpar[:, 2:3], op=ALU.mult, reads=[B_fp], writes=[B_fp])
                        w3s = sb(ph, "w3s", [64, 2048], BF16); B_w3 = Buf()
                        I("pool", "dma_start", out=w3s[:, :], in_=hy_filt_w3[l, :, :], writes=[B_w3], dma=True)
                        nad = sb(ph, "nad", [1, 1024]); B_nad = Buf()
                        I("sp", "dma_start", out=nad[:, :], in_=hy_decay[l:l + 1, :], writes=[B_nad], dma=True)
                        nad2 = sb(ph, "nad2", [1, 1024])
                        I("dve", "tensor_scalar", out=nad2[:, :], in0=nad[:, :], scalar1=-1.0, scalar2=None, op0=ALU.mult,
                          reads=[B_nad], writes=[B_nad])
                        I("dve", "tensor_tensor", out=nad[:, :], in0=nad[:, :], in1=nad2[:, :], op=ALU.min,
                          reads=[B_nad], writes=[B_nad])
                        a2tab = [sb(ph, "a2t0", [64, 8192], BF16), sb(ph, "a2t1", [64, 512], BF16)]; B_a2 = [Buf(), Buf()]
                        tqs = [sb(ph, "tq0", [4, 8192], BF16), sb(ph, "tq1", [4, 512], BF16)]; B_tq = Buf()
                        I("sp", "dma_start", out=tqs[0][:, :], in_=tq_d[0:4, 0:8192], writes=[B_tq], dma=True)
                        I("sp", "dma_start", out=tqs[1][:, :], in_=tq_d[0:4, 8192:8704], writes=[B_tq], dma=True)
                        nhi = sb(ph, "nhi", [1, 1024], BF16); nlo = sb(ph, "nlo", [1, 1024], BF16); nad4 = sb(ph, "nad4", [4, 1024], BF16)
                        B_nad4 = Buf()
                        I("dve", "tensor_copy", out=nhi[:, :], in_=nad[:, :], reads=[B_nad], writes=[B_nad4])
                        I("dve", "tensor_tensor", out=nad2[:, :], in0=nad[:, :], in1=nhi[:, :], op=ALU.subtract, reads=[B_nad, B_nad4], writes=[B_nad4])
                        I("dve", "tensor_copy", out=nlo[:, :], in_=nad2[:, :], reads=[B_nad4], writes=[B_nad4])
                        for r_, src_ in ((0, nhi), (1, nlo), (2, nhi), (3, nlo)):
                            I("sp", "dma_start", out=nad4[r_:r_ + 1, :], in_=src_[:, :], reads=[B_nad4], writes=[B_nad4], dma=True)
                        cst = {}
                        B_cst = Buf()
                        for ci_, (R_, P1_, PK_) in enumerate(((64, 128, 128), (4, 8, 36))):
                            for nm, shp, src in (("F1", [R_, 2 * P1_], dF1[ci_]), ("F1h", [PK_, 2 * P1_], dF1h[ci_]),
                                                 ("Ea", [P1_, 4 * R_], dEa[ci_]), ("Eb", [P1_, 4 * R_], dEb[ci_])):
                                t_ = sb(ph, "%s%d" % (nm, ci_), shp, BF16)
                                I("sp", "dma_start", out=t_[:, :], in_=src[:, :], writes=[B_cst], dma=True)
                                cst[(nm, ci_)] = t_
                            t_ = sb(ph, "R4_%d" % ci_, [65, 4, 64], BF16)
                            I("sp", "dma_start", out=t_[:, :, :], in_=dR4[ci_][:, :, :], writes=[B_cst], dma=True)
                            cst[("R4", ci_)] = t_
                        Eca = sb(ph, "Eca", [128, 130], BF16); Ecb = sb(ph, "Ecb", [128, 130], BF16)
                        I("sp", "dma_start", out=Eca[:, :], in_=dEc[0][:, :], writes=[B_cst], dma=True)
                        I("sp", "dma_start", out=Ecb[:, :], in_=dEc[1][:, :], writes=[B_cst], dma=True)
                        G4 = sb(ph, "G4", [64, 4, 260], BF16)
                        I("sp", "dma_start", out=G4[:, :, :], in_=dG4[:, :, :], writes=[B_cst], dma=True)

                        tgst = ExitStack()
                        zfc = [sb(tgst, "zfc%d" % i, [33, 512]) for i in range(2)]; B_zfc = [Buf(), Buf()]
                        arg = sb(tgst, "arg", [64, 512]); B_arg = Buf()
                        tw = sb(tgst, "tw", [64, 512]); B_tw = Buf()
                        a1 = sb(tgst, "a1", [64, 512]); B_a1 = Buf()

                        def sin_layer(pt, pb, n, fcol, dst, B_dst):
                            I("dve", "tensor_scalar", out=arg[:, :n], in0=pt[0:64, :n], scalar1=fpar[:, 0:1], scalar2=fpar[:, fcol:fcol + 1],
                              op0=ALU.mult, op1=ALU.add, reads=[pb, B_fp], writes=[B_arg])
                            I("dve", "tensor_scalar", out=tw[:, :n], in0=arg[:, :n], scalar1=PI, scalar2=-2 * PI, op0=ALU.is_gt, op1=ALU.mult,
                              reads=[B_arg], writes=[B_tw])
                            I("dve", "tensor_tensor", out=arg[:, :n], in0=arg[:, :n], in1=tw[:, :n], op=ALU.add, reads=[B_arg, B_tw], writes=[B_arg])
                            I("dve", "tensor_scalar", out=tw[:, :n], in0=arg[:, :n], scalar1=-PI, scalar2=2 * PI, op0=ALU.is_lt, op1=ALU.mult,
                              reads=[B_arg], writes=[B_tw])
                            I("dve", "tensor_tensor", out=arg[:, :n], in0=arg[:, :n], in1=tw[:, :n], op=ALU.add, reads=[B_arg, B_tw], writes=[B_arg])
                            I("dve", "tensor_scalar", out=arg[:, :n], in0=arg[:, :n], scalar1=-PI, scalar2=PI, op0=ALU.max, op1=ALU.min,
                              reads=[B_arg], writes=[B_arg])
                            I("act", "activation", out=dst, in_=arg[:, :n], func=AF.Sin, reads=[B_arg], writes=[B_dst])

                        for cf in cfgs:
                            ci_, Lc = cf["ci"], cf["Lc"]
                            zoff = 0 if ci_ == 0 else 8192
                            for c0 in range(0, 2 * Lc, 512):
                                n = min(512, 2 * Lc - c0)
                                s = (c0 // 512) % 2
                                I("sp", "dma_start", out=zfc[s][:, :n], in_=zfeat_d[:, zoff + c0:zoff + c0 + n], writes=[B_zfc[s]], dma=True)
                                pt, pb = psum()
                                I("pe", "matmul", out=pt[0:64, :n], lhsT=fw1[:, :], rhs=zfc[s][:, :n], start=True, stop=True,
                                  reads=[B_fp, B_zfc[s]], writes=[pb])
                                sin_layer(pt, pb, n, 3, a1[:, :n], B_a1)
                                pt, pb = psum()
                                I("pe", "matmul", out=pt[0:64, :n], lhsT=fw2[:, :], rhs=a1[:, :n], start=True, stop=True,
                                  reads=[B_fp, B_a1], writes=[pb])
                                sin_layer(pt, pb, n, 4, a2tab[ci_][:, c0:c0 + n], B_a2[ci_])
                            I("dve", "memset", ap=a2tab[ci_][:, Lc:Lc + 1], constant=0.0, writes=[B_a2[ci_]])
                        kb.barrier()
                        tgst.close()

                        if not last:
                            with ExitStack() as cst_:
                                Fc = sb(cst_, "Fc", [128, 4, 512], BF16); Finv = sb(cst_, "Finv", [128, 4, 256], BF16); B_Fc = Buf()
                                I("sp", "dma_start", out=Fc[:, :, :], in_=dFc[:, :, :], writes=[B_Fc], dma=True)
                                I("sp", "dma_start", out=Finv[:, :, :], in_=dFinv[:, :, :], writes=[B_Fc], dma=True)
                                hct = sb(cst_, "hct", [128, 4, 1024], BF16); B_hct = Buf()
                                Hs = sb(cst_, "Hs", [128, 4, 1024]); B_Hs = Buf()
                                wexc = [sb(cst_, "wexc%d" % i, [128, 512]) for i in range(2)]; B_wexc = [Buf(), Buf()]
                                zcf = sb(cst_, "zcf", [128, 8, 256], BF16); B_zcf = Buf()
                                zcf32 = sb(cst_, "zcf32", [128, 8, 256]); B_zcf32 = Buf()
                                zct = sb(cst_, "zct", [128, 2, 1024], BF16); B_zct = Buf()
                                Yc = sb(cst_, "Yc", [128, 4, 1024], BF16); B_Yc = Buf()
                                yct = sb(cst_, "yct", [128, 2, 1024]); B_yct = Buf()
                                ycf = sb(cst_, "ycf", [128, 8, 256], BF16); B_ycf = Buf()
                                tA = [sb(cst_, "tA%d" % i, [128, 512]) for i in range(2)]; B_tA = [Buf(), Buf()]
                                tB = [sb(cst_, "tB%d" % i, [128, 512]) for i in range(2)]; B_tB = [Buf(), Buf()]
                                I("sp", "dma_start", out=zcf[:, :, :], in_=zT_d[:, :, L:NT].rearrange("k p t -> p k t"), reads=[B_scr], writes=[B_zcf], dma=True)
                                I("act", "copy", out=zcf32[:, :, :], in_=zcf[:, :, :], reads=[B_zcf], writes=[B_zcf32])
                                wi = 0
                                for j in range(4):
                                    half = 0 if j < 2 else 1
                                    for cN in range(2):
                                        csl = slice(cN * 512, (cN + 1) * 512)
                                        psh, pshb = psum()
                                        I("pe", "matmul", out=psh[:, :], lhsT=a2tab[1][:, j * 128:(j + 1) * 128],
                                          rhs=w3s[:, half * 1024 + cN * 512:half * 1024 + (cN + 1) * 512], start=True, stop=True,
                                          reads=[B_a2[1], B_w3], writes=[pshb])
                                        psw, pswb = psum()
                                        I("pe", "matmul", out=psw[:, :], lhsT=tqs[1][0:4, j * 128:(j + 1) * 128], rhs=nad4[0:4, csl], start=True, stop=True,
                                          reads=[B_tq, B_nad4], writes=[pswb])
                                        w_ = wi % 2; wi += 1
                                        I("act", "activation", out=wexc[w_][:, :], in_=psw[:, :], func=AF.Exp, reads=[pswb], writes=[B_wexc[w_]])
                                        I("dve", "tensor_tensor", out=hct[:, j, csl], in0=psh[:, :], in1=wexc[w_][:, :], op=ALU.mult,
                                          reads=[pshb, B_wexc[w_]], writes=[B_hct])
                                for m in range(4):
                                    for cN in range(2):
                                        csl = slice(cN * 512, (cN + 1) * 512)
                                        pt, pb = psum()
                                        for j in range(4):
                                            I("pe", "matmul", out=pt[:, :], lhsT=Fc[:, j, m * 128:(m + 1) * 128], rhs=hct[:, j, csl], start=(j == 0), stop=(j == 3),
                                              reads=[B_Fc, B_hct], writes=[pb], inc=(j == 3))
                                        if (m + cN) % 2 == 0:
                                            I("act", "copy", out=Hs[:, m, csl], in_=pt[:, :], reads=[pb], writes=[B_Hs])
                                        else:
                                            I("dve", "tensor_copy", out=Hs[:, m, csl], in_=pt[:, :], reads=[pb], writes=[B_Hs])
                                for sc in range(2):
                                    for cbh in range(2):
                                        pt, pb = psum()
                                        for jj in range(4):
                                            cb = cbh * 4 + jj
                                            I("pe", "transpose", out=pt[:, jj * 128:(jj + 1) * 128], in_=zcf32[:, cb, sc * 128:(sc + 1) * 128], identity=ident[:, :],
                                              reads=[B_zcf32, B_ident], writes=[pb], inc=(jj == 3))
                                        I("act", "copy", out=zct[:, sc, cbh * 512:(cbh + 1) * 512], in_=pt[:, :], reads=[pb], writes=[B_zct])
                                for cN in range(2):
                                    csl = slice(cN * 512, (cN + 1) * 512)
                                    pz = []
                                    for m in range(4):
                                        pt, pb = psum()
                                        for j in range(2):
                                            I("pe", "matmul", out=pt[:, :], lhsT=Fc[:, j, m * 128:(m + 1) * 128], rhs=zct[:, j, csl], start=(j == 0), stop=(j == 1),
                                              reads=[B_Fc, B_zct], writes=[pb], inc=(j == 1))
                                        pz.append((pt, pb))
                                    for b_ in range(2):
                                        (rp, rb), (ip, ib) = pz[b_], pz[2 + b_]
                                        I("dve", "tensor_tensor", out=tA[0][:, :], in0=rp[:, :], in1=Hs[:, b_, csl], op=ALU.mult, reads=[rb, B_Hs], writes=[B_tA[0]])
                                        I("dve", "tensor_tensor", out=tB[0][:, :], in0=ip[:, :], in1=Hs[:, 2 + b_, csl], op=ALU.mult, reads=[ib, B_Hs], writes=[B_tB[0]])
                                        I("pool", "tensor_tensor", out=Yc[:, b_, csl], in0=tA[0][:, :], in1=tB[0][:, :], op=ALU.subtract,
                                          reads=[B_tA[0], B_tB[0]], writes=[B_Yc])
                                        I("dve", "tensor_tensor", out=tA[1][:, :], in0=rp[:, :], in1=Hs[:, 2 + b_, csl], op=ALU.mult, reads=[rb, B_Hs], writes=[B_tA[1]])
                                        I("dve", "tensor_tensor", out=tB[1][:, :], in0=ip[:, :], in1=Hs[:, b_, csl], op=ALU.mult, reads=[ib, B_Hs], writes=[B_tB[1]])
                                        I("pool", "tensor_tensor", out=Yc[:, 2 + b_, csl], in0=tA[1][:, :], in1=tB[1][:, :], op=ALU.add,
                                          reads=[B_tA[1], B_tB[1]], writes=[B_Yc])
                                    for blk in (0, 2):
                                        pp, ppb = pz[blk]
                                        I("dve", "tensor_tensor", out=Yc[0:1, blk, csl], in0=pp[0:1, :], in1=Hs[0:1, blk, csl], op=ALU.mult,
                                          reads=[ppb, B_Hs, B_Yc], writes=[B_Yc])
                                for tbk in range(2):
                                    for cN in range(2):
                                        csl = slice(cN * 512, (cN + 1) * 512)
                                        pt, pb = psum()
                                        for j in range(4):
                                            I("pe", "matmul", out=pt[:, :], lhsT=Finv[:, j, tbk * 128:(tbk + 1) * 128], rhs=Yc[:, j, csl], start=(j == 0), stop=(j == 3),
                                              reads=[B_Fc, B_Yc], writes=[pb], inc=(j == 3))
                                        I("act", "copy", out=yct[:, tbk, csl], in_=pt[:, :], reads=[pb], writes=[B_yct])
                                for cbh in range(2):
                                    for tbk in range(2):
                                        pt, pb = psum()
                                        for jj in range(4):
                                            cb = cbh * 4 + jj
                                            I("pe", "transpose", out=pt[:, jj * 128:(jj + 1) * 128], in_=yct[:, tbk, cb * 128:(cb + 1) * 128], identity=ident[:, :],
                                              reads=[B_yct, B_ident], writes=[pb], inc=(jj == 3))
                                        I("dve", "tensor_copy", out=ycf[:, cbh * 4:(cbh + 1) * 4, tbk * 128:(tbk + 1) * 128],
                                          in_=pt[:, :].rearrange("p (j t) -> p j t", j=4), reads=[pb], writes=[B_ycf])
                                I("sp", "dma_start", out=ycT_d[:, :, L:NT].rearrange("k p t -> p k t"), in_=ycf[:, :, :], reads=[B_ycf], writes=[B_yc], dma=True)
                            kb.barrier()

                        hq2 = sb(ph, "hq2", [128, 256, 64], BF16); B_hq = Buf()
                        wexp = [sb(ph, "wexp%d" % i, [128, 256]) for i in range(2)]; B_we = [Buf(), Buf()]
                        HAB = sb(ph, "HAB", [128, 32, 260], BF16); B_HAB = [Buf() for _ in range(32)]
                        zin = [sb(ph, "zin%d" % i, [64, 32, 64], BF16) for i in range(2)]; B_zin = [Buf(), Buf()]
                        yout = [sb(ph, "yout%d" % i, [64, 32, 64], BF16) for i in range(2)]; B_yo = [Buf(), Buf()]
                        AhG = sb(ph, "AhG", [64, 32, 256], BF16); B_AhG = [Buf() for _ in range(16)]
                        AsG = sb(ph, "AsG", [64, 32, 256], BF16); B_AsG = [Buf() for _ in range(16)]
                        YsG = sb(ph, "YsG", [128, 32, 130], BF16); B_YsG = [Buf() for _ in range(32)]
                        BsG = sb(ph, "BsG", [65, 32, 130], BF16); B_BsG = [Buf() for _ in range(16)]
                        tt = [sb(ph, "tt%d" % i, [128, 260]) for i in range(4)]; B_tt = [Buf() for _ in range(4)]
                        rot = {"tt": 0, "we": 0, "g": 0, "ev": 0}

                        def nxt(key, n=2):
                            v = rot[key]; rot[key] = (v + 1) % n
                            return v

                        def evac(out, in_, pb, B_out):
                            if nxt("ev") == 0:
                                I("act", "copy", out=out, in_=in_, reads=[pb], writes=[B_out])
                            else:
                                I("dve", "tensor_copy", out=out, in_=in_, reads=[pb], writes=[B_out])

                        cf = cfgs[0]
                        R_, P1, PK, boff, Lc, tb, ci_ = cf["R"], cf["P1"], cf["PK"], cf["boff"], cf["Lc"], cf["tb"], cf["ci"]
                        F1, F1h, Ea, Eb, R4 = [cst[(nm, ci_)] for nm in ("F1", "F1h", "Ea", "Eb", "R4")]
                        pf = None
                        if l + 1 < nlayers:
                            pf = P0_steps(l + 1, ph, (l + 1) % 2, W=256)
                            pf[0]()
                            prefetched.add(l + 1)
                        gidx = 0
                        for cq in range(4):
                            for l2 in range(64):
                                psh, pshb = psum()
                                psw, pswb = psum()
                                for half in range(2):
                                    r0 = 0 if half == 0 else boff
                                    base = half * Lc + l2
                                    cols = slice(base, base + 64 * (R_ - 1) + 1, 64)
                                    I("pe", "matmul", out=psh[r0:r0 + R_, 0:256], lhsT=a2tab[ci_][:, cols],
                                      rhs=w3s[:, half * 1024 + cq * 256:half * 1024 + (cq + 1) * 256], start=True, stop=True,
                                      reads=[B_a2[ci_], B_w3], writes=[pshb])
                                    I("pe", "matmul", out=psw[r0:r0 + R_, 0:256], lhsT=tqs[ci_][0:4, cols], rhs=nad4[0:4, cq * 256:(cq + 1) * 256],
                                      start=True, stop=True, reads=[B_tq, B_nad4], writes=[pswb])
                                w_ = nxt("we")
                                I("act", "activation", out=wexp[w_][:, :], in_=psw[:, 0:256], func=AF.Exp, reads=[pswb], writes=[B_we[w_]])
                                I("dve", "tensor_tensor", out=hq2[:, :, l2], in0=psh[:, 0:256], in1=wexp[w_][:, :],
                                  op=ALU.mult, reads=[pshb, B_we[w_]], writes=[B_hq])
                            for g in range(8):
                                if pf is not None:
                                    if 1 <= gidx <= 24:
                                        pf[2](gidx - 1)
                                    if gidx < 24:
                                        pf[1](gidx)
                                gidx += 1
                                c_lo = cq * 256 + g * 32
                                cb, p0 = c_lo // 128, c_lo % 128
                                gs = nxt("g")
                                for hh in range(2):
                                    I("sp", "dma_start", out=zin[gs][0:R_, hh * 16:(hh + 1) * 16, :],
                                      in_=zT_d[cb, p0 + hh * 16:p0 + (hh + 1) * 16, tb:tb + Lc].rearrange("c (a b) -> a c b", b=64),
                                      reads=[B_scr], writes=[B_zin[gs]], dma=True)
                                for pr in range(16):
                                    pt, pb = psum()
                                    for j in range(2):
                                        I("pe", "matmul", out=pt[0:64, j * 256:(j + 1) * 256], lhsT=hq2[:, g * 32 + pr * 2 + j, :], rhs=F1h[:, :],
                                          start=True, stop=True, reads=[B_hq, B_cst], writes=[pb], inc=(j == 1))
                                    evac(AhG[:, pr * 2:pr * 2 + 2, :], pt[0:64, :].rearrange("p (j w) -> p j w", j=2), pb, B_AhG[pr])
                                for pr in range(16):
                                    pt, pb = psum()
                                    for j in range(2):
                                        I("pe", "matmul", out=pt[0:64, j * 256:(j + 1) * 256], lhsT=zin[gs][:, pr * 2 + j, :], rhs=F1[:, :],
                                          start=True, stop=True, reads=[B_zin[gs], B_cst], writes=[pb], inc=(j == 1))
                                    evac(AsG[:, pr * 2:pr * 2 + 2, :], pt[0:64, :].rearrange("p (j w) -> p j w", j=2), pb, B_AsG[pr])
                                for ch in range(32):
                                    pt2, pb2 = psum()
                                    I("pe", "matmul", out=pt2[:, 0:260], lhsT=AhG[:, ch, 0:128], rhs=G4[:, 2, :], start=True, stop=False,
                                      reads=[B_AhG[ch // 2], B_cst], writes=[pb2], inc=False)
                                    I("pe", "matmul", out=pt2[:, 0:260], lhsT=AhG[:, ch, 128:256], rhs=G4[:, 3, :], start=False, stop=True,
                                      reads=[B_AhG[ch // 2], B_cst], writes=[pb2])
                                    evac(HAB[:, ch, :], pt2[:, 0:260], pb2, B_HAB[ch])
                                for ch in range(32):
                                    pt2, pb2 = psum()
                                    I("pe", "matmul", out=pt2[:, 0:260], lhsT=AsG[:, ch, 0:128], rhs=G4[:, 0, :], start=True, stop=False,
                                      reads=[B_AsG[ch // 2], B_cst], writes=[pb2], inc=False)
                                    I("pe", "matmul", out=pt2[:, 0:260], lhsT=AsG[:, ch, 128:256], rhs=G4[:, 1, :], start=False, stop=True,
                                      reads=[B_AsG[ch // 2], B_cst], writes=[pb2])
                                    t_ = nxt("tt", 4)
                                    I("dve", "tensor_tensor", out=tt[t_][:, :], in0=pt2[:, 0:260], in1=HAB[:, ch, :], op=ALU.mult,
                                      reads=[pb2, B_HAB[ch]], writes=[B_tt[t_]])
                                    I("pool", "tensor_tensor", out=YsG[:, ch, :], in0=tt[t_][:, 0:130], in1=tt[t_][:, 130:260], op=ALU.add,
                                      reads=[B_tt[t_]], writes=[B_YsG[ch]])
                                for pr in range(16):
                                    pt3, pb3 = psum()
                                    for j in range(2):
                                        ch = pr * 2 + j
                                        I("pe", "matmul", out=pt3[0:65, j * 256:j * 256 + 130], lhsT=YsG[:, ch, 0:65], rhs=Eca[:, :], start=True, stop=False,
                                          reads=[B_YsG[ch], B_cst], writes=[pb3], inc=False)
                                        I("pe", "matmul", out=pt3[0:65, j * 256:j * 256 + 130], lhsT=YsG[:, ch, 65:130], rhs=Ecb[:, :], start=False, stop=True,
                                          reads=[B_YsG[ch], B_cst], writes=[pb3], inc=(j == 1))
                                    evac(BsG[:, pr * 2:pr * 2 + 2, :], pt3[0:65, :].rearrange("p (j w) -> p j w", j=2)[:, :, 0:130], pb3, B_BsG[pr])
                                for oc in range(4):
                                    pt4, pb4 = psum()
                                    for jj in range(8):
                                        ch = oc * 8 + jj
                                        for i4, o_ in enumerate((1, 0, 66, 65)):
                                            I("pe", "matmul", out=pt4[0:R_, jj * 64:(jj + 1) * 64], lhsT=BsG[:, ch, o_:o_ + 64], rhs=R4[:, i4, :],
                                              start=(i4 == 0), stop=(i4 == 3), reads=[B_BsG[ch // 2], B_cst], writes=[pb4], inc=(i4 == 3 and jj == 7))
                                    evac(yout[gs][0:R_, oc * 8:(oc + 1) * 8, :], pt4[0:R_, :].rearrange("p (c n) -> p c n", c=8), pb4, B_yo[gs])
                                for hh in range(2):
                                    I("sp", "dma_start", out=ycT_d[cb, p0 + hh * 16:p0 + (hh + 1) * 16, tb:tb + Lc].rearrange("c (a b) -> a c b", b=64),
                                      in_=yout[gs][0:R_, hh * 16:(hh + 1) * 16, :], reads=[B_yo[gs]], writes=[B_yc], dma=True)
                        if pf is not None:
                            pf[3]()
                    kb.barrier()

                if dbg and l == 0:
                    for n, src in (("z", zT_d), ("yc", ycT_d), ("x0", x0T_d), ("yg", ygT_d), ("sgh", sghT_d), ("sgg", sggT_d), ("sr", srT_d)):
                        for k in range(8):
                            I("sp", "dma_start", out=dbg_t[n][k, :, :], in_=src[k, :, :], reads=[B_scr, B_scr2, B_yc], writes=[Buf()], dma=True)
                    kb.barrier()
                with ExitStack() as ph:
                    TT = 512
                    wb = [sb(ph, "wb%d" % i, [128, 8, D], BF16) for i in range(3)]; B_wb = [Buf(), Buf(), Buf()]
                    for i, wsrc in enumerate((w_branch_hy, w_branch_gla, w_out)):
                        I("pool", "dma_start", out=wb[i][:, :, :], in_=wsrc[l, :, :].rearrange("(k p) n -> p k n", p=128), writes=[B_wb[i]], dma=True)
                    skp = sb(ph, "skp", [128, 8]); B_skp = Buf()
                    load_T(skp[:, :], hy_skip[l, :].rearrange("(k p) -> k p", p=128), 8, B_skp)
                    names = ("x0", "yc", "z", "sgh", "sgg", "yg")
                    srcs = (x0T_d, ycT_d, zT_d, sghT_d, sggT_d, ygT_d)
                    lt = [sb(ph, "l%s" % n, [128, 8, TT], BF16) for n in names]
                    B_ltn = [Buf() for _ in names]
                    xt = [sb(ph, "oxt%d" % i, [128, 8, TT]) for i in range(2)]; B_xt = [Buf(), Buf()]
                    yh = sb(ph, "yh", [128, 8, TT], BF16); B_yh = Buf()
                    tmpf = sb(ph, "tmpf", [128, 8, TT]); B_tmpf = Buf()
                    mg = sb(ph, "mg", [128, 8, TT], BF16); B_mg = Buf()
                    mgf = [sb(ph, "mgf%d" % i, [128, TT]) for i in range(2)]; B_mgf = [Buf(), Buf()]
                    mgg = [sb(ph, "mgg%d" % i, [128, TT]) for i in range(2)]; B_mgg = [Buf(), Buf()]
                    tiles5 = [(i * 512, 512, 0) for i in range(8)] + ([] if last else [(L, 256, 1)])
                    for ti, (t0, T, col) in enumerate(tiles5):
                        s = ti % 2
                        for n_, src in enumerate(srcs):
                            if n_ == 1 and not do_hy:
                                continue
                            I("sp", "dma_start", out=lt[n_][:, :, :T], in_=src[:, :, t0:t0 + T].rearrange("k p t -> p k t"),
                              reads=[B_scr, B_scr2, B_yc], writes=[B_ltn[n_]], dma=True)
                        I("sp", "dma_start", out=xt[s][:, :, :T], in_=xT_d[:, :, t0:t0 + T].rearrange("k p t -> p k t"),
                          reads=[B_xT], writes=[B_xt[s]], dma=True)
                        x0l, ycl, zl, sghl, sggl, ygl = lt
                        if do_hy:
                            for k in range(8):
                                I("dve", "scalar_tensor_tensor", out=tmpf[:, k, :T], in0=zl[:, k, :T], scalar=skp[:, k:k + 1], in1=ycl[:, k, :T],
                                  op0=ALU.mult, op1=ALU.add, reads=[B_ltn[2], B_ltn[1], B_skp], writes=[B_tmpf])
                            I("dve", "tensor_tensor", out=yh[:, :, :T], in0=tmpf[:, :, :T], in1=x0l[:, :, :T], op=ALU.mult,
                              reads=[B_tmpf, B_ltn[0]], writes=[B_yh])
                        for db in range(8):
                            w_ = db % 2
                            pg_, pgb_ = psum()
                            for k in range(8):
                                I("pe", "matmul", out=pg_[:, :T], lhsT=wb[1][:, k, db * 128:(db + 1) * 128], rhs=ygl[:, k, :T],
                                  start=(k == 0), stop=(k == 7), reads=[B_wb[1], B_ltn[5]], writes=[pgb_], inc=(k == 7))
                            if do_hy:
                                ph_, phb_ = psum()
                                for k in range(8):
                                    I("pe", "matmul", out=ph_[:, :T], lhsT=wb[0][:, k, db * 128:(db + 1) * 128], rhs=yh[:, k, :T],
                                      start=(k == 0), stop=(k == 7), reads=[B_wb[0], B_yh], writes=[phb_], inc=(k == 7))
                                I("dve", "tensor_tensor", out=mgf[w_][:, :T], in0=ph_[:, :T], in1=sghl[:, db, :T], op=ALU.mult,
                                  reads=[phb_, B_ltn[3]], writes=[B_mgf[w_]])
                                I("dve", "tensor_tensor", out=mgg[w_][:, :T], in0=pg_[:, :T], in1=sggl[:, db, :T], op=ALU.mult,
                                  reads=[pgb_, B_ltn[4]], writes=[B_mgg[w_]])
                                I("dve", "tensor_tensor", out=mg[:, db, :T], in0=mgf[w_][:, :T], in1=mgg[w_][:, :T], op=ALU.add,
                                  reads=[B_mgf[w_], B_mgg[w_]], writes=[B_mg])
                            else:
                                I("dve", "tensor_tensor", out=mg[:, db, :T], in0=pg_[:, :T], in1=sggl[:, db, :T], op=ALU.mult,
                                  reads=[pgb_, B_ltn[4]], writes=[B_mg])
                        for db in range(8):
                            pt, pb = psum()
                            for k in range(8):
                                I("pe", "matmul", out=pt[:, :T], lhsT=wb[2][:, k, db * 128:(db + 1) * 128], rhs=mg[:, k, :T],
                                  start=(k == 0), stop=(k == 7), reads=[B_wb[2], B_mg], writes=[pb], inc=(k == 7))
                            I("dve", "scalar_tensor_tensor", out=xt[s][:, db, :T], in0=pt[:, :T], scalar=modT[:, 16 + db, col:col + 1],
                              in1=xt[s][:, db, :T], op0=ALU.mult, op1=ALU.add, reads=[pb, B_mod, B_xt[s]], writes=[B_xt[s]])
                        I("sp", "dma_start", out=xT_d[:, :, t0:t0 + T].rearrange("k p t -> p k t"), in_=xt[s][:, :, :T],
                          reads=[B_xt[s]], writes=[B_xT], dma=True)
                kb.barrier()

            if dbg and l == 0 and do_mix:
                for k in range(8):
                    I("sp", "dma_start", out=dbg_x[k, :, :], in_=xT_d[k, :, :], reads=[B_xT], writes=[Buf()], dma=True)
                kb.barrier()
            if do_mlp:
                with ExitStack() as ph:
                    TT = 512
                    w1s = sb(ph, "w1s", [128, 8, DFF], BF16); B_w1 = [Buf() for _ in range(4)]
                    w2s = sb(ph, "w2s", [128, 32, D], BF16); B_w2 = [Buf() for _ in range(4)]
                    for q in range(4):
                        I("pool", "dma_start", out=w1s[:, :, q * 1024:(q + 1) * 1024],
                          in_=mlp_w1[l, :, q * 1024:(q + 1) * 1024].rearrange("(k p) f -> p k f", p=128), writes=[B_w1[q]], dma=True)
                    for q in range(4):
                        I("pool", "dma_start", out=w2s[:, q * 8:(q + 1) * 8, :],
                          in_=mlp_w2[l, q * 1024:(q + 1) * 1024, :].rearrange("(k p) d -> p k d", p=128), writes=[B_w2[q]], dma=True)
                    xt1 = sb(ph, "mxt", [128, 8, TT]); xt = [xt1, xt1]; B_x1 = Buf(); B_xt = [B_x1, B_x1]
                    h2 = sb(ph, "mh2", [128, 8, TT], BF16); B_h2 = Buf()
                    fT = sb(ph, "mfT", [128, 32, TT], BF16); B_fT = [Buf() for _ in range(32)]
                    rl1 = sb(ph, "mrl", [128, TT]); B_r1 = Buf()
                    scr = {"sq": sb(ph, "msq", [128, 8, TT]), "B_sq": Buf(), "rs": sb(ph, "mrs", [128, TT]), "B_rs": Buf()}
                    rl = [rl1, scr["rs"]]; B_rl = [B_r1, scr["B_rs"]]
                    tiles6 = [(i * 512, 512, 0) for i in range(8)] + ([] if last else [(L, 256, 1)])
                    for ti, (t0, T, col) in enumerate(tiles6):
                        s = ti % 2
                        I("sp", "dma_start", out=xt[s][:, :, :T], in_=xT_d[:, :, t0:t0 + T].rearrange("k p t -> p k t"),
                          reads=[B_xT], writes=[B_xt[s]], dma=True)
                        norm_tile(ph, "m", xt[s], B_xt[s], T, lambda k, col=col: sc2[:, k, col:col + 1],
                                  lambda k, col=col: modT[:, 24 + k, col:col + 1], h2, B_h2, scr)
                        for fb in range(32):
                            pt, pb = psum()
                            for k in range(8):
                                I("pe", "matmul", out=pt[:, :T], lhsT=w1s[:, k, fb * 128:(fb + 1) * 128], rhs=h2[:, k, :T],
                                  start=(k == 0), stop=(k == 7), reads=[B_w1[fb // 8], B_h2], writes=[pb], inc=(k == 7))
                            r = fb % 2
                            I("act", "activation", out=rl[r][:, :T], in_=pt[:, :T], func=AF.Relu, reads=[pb], writes=[B_rl[r]])
                            I("dve", "tensor_tensor", out=fT[:, fb, :T], in0=rl[r][:, :T], in1=rl[r][:, :T], op=ALU.mult,
                              reads=[B_rl[r]], writes=[B_fT[fb]])
                        for db in range(8):
                            pt, pb = psum()
                            for fk in range(32):
                                I("pe", "matmul", out=pt[:, :T], lhsT=w2s[:, fk, db * 128:(db + 1) * 128], rhs=fT[:, fk, :T],
                                  start=(fk == 0), stop=(fk == 31), reads=[B_w2[fk // 8], B_fT[fk]], writes=[pb], inc=(fk == 31))
                            I("dve", "scalar_tensor_tensor", out=xt[s][:, db, :T], in0=pt[:, :T], scalar=modT[:, 40 + db, col:col + 1],
                              in1=xt[s][:, db, :T], op0=ALU.mult, op1=ALU.add, reads=[pb, B_mod, B_xt[s]], writes=[B_xt[s]])
                        I("sp", "dma_start", out=xT_d[:, :, t0:t0 + T].rearrange("k p t -> p k t"), in_=xt[s][:, :, :T],
                          reads=[B_xt[s]], writes=[B_xT], dma=True)
                kb.barrier()

        with ExitStack() as ph:
            TT = 128
            xt = [sb(ph, "fxt%d" % i, [128, 8, TT]) for i in range(2)]; B_xt = [Buf(), Buf()]
            yn = sb(ph, "fyn", [128, 8, TT]); B_yn = Buf()
            yo = [sb(ph, "fyo%d" % i, [128, D]) for i in range(2)]; B_yo = [Buf(), Buf()]
            scr = {"sq": sb(ph, "fsq", [128, 8, TT]), "B_sq": Buf(), "rs": sb(ph, "frs", [128, TT]), "B_rs": Buf()}
            for ti in range(L // TT):
                s = ti % 2
                t0 = ti * TT
                E("sp", ("dma_start", dict(out=xt[s][:, :, :], in_=xT_d[:, :, t0:t0 + TT].rearrange("k p t -> p k t"))),
                  reads=[B_xT], writes=[B_xt[s]], dma=True)
                B_gvec = B_gfin
                norm_tile(ph, "f", xt[s], B_xt[s], TT, lambda k: gfin[:, k:k + 1], None, yn, B_yn, scr)
                for half in range(2):
                    pt, pb = psum()
                    for j in range(4):
                        k = half * 4 + j
                        E("pe", ("transpose", dict(out=pt[:, j * 128:(j + 1) * 128], in_=yn[:, k, :], identity=ident[:, :])),
                          reads=[B_yn, B_ident], writes=[pb], inc=(j == 3))
                    if half == 0:
                        E("act", ("copy", dict(out=yo[s][:, 0:512], in_=pt[:, :])), reads=[pb], writes=[B_yo[s]])
                    else:
                        E("dve", ("tensor_copy", dict(out=yo[s][:, 512:1024], in_=pt[:, :])), reads=[pb], writes=[B_yo[s]])
                E("sp", ("dma_start", dict(out=y_out[t0:t0 + TT, :], in_=yo[s][:, :])), reads=[B_yo[s]], dma=True)
        kb.barrier()
        kb.replay()
    return nc


def kernel(**inputs):
    consts = host_consts()
    f32 = lambda a: np.ascontiguousarray(np.asarray(a, np.float32))
    shared = {k: f32(inputs[k]) for k in WKEYS}
    nc = build()
    in_maps = []
    for core in range(8):
        b = core % 4
        m = dict(shared)
        m["x"] = f32(inputs["x"][b])
        m["ctx"] = f32(inputs["ctx"][b])
        m["cvec"] = f32(np.stack([np.asarray(inputs["c"])[b], np.asarray(inputs["c_ctx"])]))
        m.update(consts)
        in_maps.append(m)
    res = run_bass_kernel_spmd(nc, in_maps, core_ids=list(range(8)))
    out = np.stack([np.asarray(res.results[b]["y"], np.float32) for b in range(4)])
    return out
```
